# Optimizing a Trainium2 kernel written in Bass

```python
import jax
import jax.numpy as jnp
from jax import lax
import numpy as np

D_MODEL = 2048
BATCH = 2
SEQ = 4096
DEPTH = 4

GRID_W = 64
CTX_LEN = 256
MIX_W = 512
N_BRANCH = 4
CONV_K = 31
ML_HEADS = 4
ML_DH = 128
GD_HEADS = 4
GD_DH = 128
GD_CONV_K = 5
RW_HEADS = 8
RW_DH = 64
RW_DECAY_LORA = 96
RW_ICLR_LORA = 96
RW_GATE_LORA = 256
RW_MIX = 3 * MIX_W + RW_DECAY_LORA + RW_ICLR_LORA + RW_GATE_LORA
CHUNK = 64
D_FF = 5632
FFN_K = 3
NORM_EPS = 1e-6
GN_EPS = 64e-5
IN_SIZES = (2 * MIX_W, 3 * MIX_W, MIX_W, 4 * ML_HEADS, 3 * MIX_W, MIX_W, 4 * GD_HEADS, RW_MIX, N_BRANCH * D_MODEL)
N_IN = 2 * MIX_W + 3 * MIX_W + MIX_W + 4 * ML_HEADS + 3 * MIX_W + MIX_W + 4 * GD_HEADS + RW_MIX + N_BRANCH * D_MODEL

kernel_name = "hybrid_parallel_gated_mixer_dit"

F32 = jnp.float32


def rms_norm(x, g):
    xf = x.astype(F32)
    y = xf * lax.rsqrt(jnp.mean(xf * xf, axis=-1, keepdims=True) + NORM_EPS)
    return (y * g.astype(F32)).astype(x.dtype)


def layer_norm(x, g, b, eps):
    xf = x.astype(F32)
    mu = jnp.mean(xf, axis=-1, keepdims=True)
    var = jnp.mean(jnp.square(xf - mu), axis=-1, keepdims=True)
    y = (xf - mu) * lax.rsqrt(var + eps)
    return (y * g.astype(F32) + b.astype(F32)).astype(x.dtype)


def l2_normalize(x):
    return x * lax.rsqrt(jnp.sum(x * x, axis=-1, keepdims=True) + 1e-6)


def modulate(h, shift, scale):
    return h * (1 + scale) + shift


def dwconv1d(x, w):
    k = w.shape[0]
    return lax.conv_general_dilated(x, w.astype(x.dtype)[:, None, :], (1,), [(k // 2, k // 2)],
                                    dimension_numbers=('NWC', 'WIO', 'NWC'), feature_group_count=x.shape[-1])


def dwconv2d_grid(x, w):
    b, t, ch = x.shape
    k = w.shape[0]
    xg = x.reshape(b, t // GRID_W, GRID_W, ch)
    y = lax.conv_general_dilated(xg, w.astype(x.dtype)[:, :, None, :], (1, 1), [(k // 2, k // 2)] * 2,
                                 dimension_numbers=('NHWC', 'HWIO', 'NHWC'), feature_group_count=ch)
    return y.reshape(b, t, ch)


def token_shift_mix(x, mu):
    prev = jnp.pad(x[:, :-1], ((0, 0), (1, 0), (0, 0)))
    nxt = jnp.pad(x[:, 1:], ((0, 0), (0, 1), (0, 0)))
    return x + mu[0] * (prev - x) + mu[1] * (nxt - x)


def dir_stack(fwd, bwd):
    return jnp.concatenate([fwd, jnp.flip(bwd, 1)], axis=0)


def dir_merge(y):
    n = y.shape[0] // 2
    return y[:n] + jnp.flip(y[n:], 1)


def to_chunks(a):
    g, t = a.shape[:2]
    a = a.reshape(g, t // CHUNK, CHUNK, *a.shape[2:])
    return jnp.moveaxis(jnp.moveaxis(a, 1, 0), 3, 2)


def from_chunks(a):
    a = jnp.moveaxis(jnp.moveaxis(a, 0, 1), 3, 2)
    return a.reshape(a.shape[0], a.shape[1] * a.shape[2], *a.shape[3:])


def split_cols(z):
    idx = [int(i) for i in np.cumsum(IN_SIZES)[:-1]]
    return jnp.split(z, idx, axis=-1)


def conv_module(u, p):
    a, gte = jnp.split(u, 2, axis=-1)
    y = a * jax.nn.sigmoid(gte)
    y = dwconv1d(y, p['conv_dw']) + p['conv_b']
    y = layer_norm(y, p['conv_ln_g'], p['conv_ln_b'], NORM_EPS)
    return jax.nn.silu(y)


def mlstm_inputs(qkv, gates, gate_b):
    b, t, _ = qkv.shape
    q, k, v = jnp.split(qkv.astype(F32), 3, axis=-1)
    heads = lambda a: a.reshape(b, t, ML_HEADS, ML_DH)
    q, k, v = heads(q) * (ML_DH ** -0.5), heads(k), heads(v)
    g = gates.astype(F32).reshape(b, t, 2, 2, ML_HEADS) + gate_b.astype(F32)
    li = g[:, :, :, 0]
    lf = jax.nn.log_sigmoid(g[:, :, :, 1])
    return (dir_stack(q, q), dir_stack(k, k), dir_stack(v, v),
            dir_stack(li[:, :, 0], li[:, :, 1]), dir_stack(lf[:, :, 0], lf[:, :, 1]))


def mlstm_scan(q, k, v, li, lf, state):
    qc, kc, vc = to_chunks(q), to_chunks(k), to_chunks(v)
    lic = to_chunks(li)
    bc = jnp.cumsum(to_chunks(lf), axis=-1)
    tri = jnp.tril(jnp.ones((CHUNK, CHUNK), dtype=bool))

    def step(carry, inp):
        C, n, m = carry
        qt, kt, vt, it, bt = inp
        dlog = jnp.where(tri, bt[..., :, None] - bt[..., None, :] + it[..., None, :], -jnp.inf)
        inter = bt + m[..., None]
        mt = jnp.maximum(inter, jnp.max(dlog, axis=-1))
        dw = jnp.exp(dlog - mt[..., None])
        iw = jnp.exp(inter - mt)
        s = jnp.einsum('ghtd,ghsd->ghts', qt, kt) * dw
        num = iw[..., None] * jnp.einsum('ghtd,ghde->ghte', qt, C) + jnp.einsum('ghts,ghse->ghte', s, vt)
        den = iw * jnp.einsum('ghtd,ghd->ght', qt, n) + jnp.sum(s, axis=-1)
        h = num / jnp.maximum(jnp.abs(den), jnp.exp(-mt))[..., None]
        bl = bt[..., -1]
        wlog = bl[..., None] - bt + it
        m_new = jnp.maximum(bl + m, jnp.max(wlog, axis=-1))
        sw = jnp.exp(wlog - m_new[..., None])
        dec = jnp.exp(bl + m - m_new)
        C_new = dec[..., None, None] * C + jnp.einsum('ghs,ghsd,ghse->ghde', sw, kt, vt)
        n_new = dec[..., None] * n + jnp.einsum('ghs,ghsd->ghd', sw, kt)
        return (C_new, n_new, m_new), h

    state, h = lax.scan(step, state, (qc, kc, vc, lic, bc))
    return from_chunks(h), state


def mlstm_out(h, o_cols, norm_g):
    h = dir_merge(h)
    h = rms_norm(h, norm_g.reshape(ML_HEADS, ML_DH))
    return h.reshape(*h.shape[:2], -1) * jax.nn.sigmoid(o_cols.astype(F32))


def gdn_inputs(qkv, ab, conv_w, a_log, dt_bias):
    b, t, _ = qkv.shape
    y = jax.nn.silu(dwconv1d(qkv, conv_w)).astype(F32)
    q, k, v = jnp.split(y, 3, axis=-1)
    heads = lambda a: a.reshape(b, t, GD_HEADS, GD_DH)
    q = l2_normalize(heads(q)) * (GD_DH ** -0.5)
    k = l2_normalize(heads(k))
    v = heads(v)
    ab = ab.astype(F32).reshape(b, t, 2, 2, GD_HEADS)
    lg = -jnp.exp(a_log.astype(F32)) * jax.nn.softplus(ab[:, :, :, 0] + dt_bias.astype(F32))
    beta = jax.nn.sigmoid(ab[:, :, :, 1])
    return (dir_stack(q, q), dir_stack(k, k), dir_stack(v, v),
            dir_stack(lg[:, :, 0], lg[:, :, 1]), dir_stack(beta[:, :, 0], beta[:, :, 1]))


def gdn_scan(q, k, v, lg, beta, S):
    qc, kc, vc = to_chunks(q), to_chunks(k), to_chunks(v)
    gc = jnp.cumsum(to_chunks(lg), axis=-1)
    bc = to_chunks(beta)
    tri = jnp.tril(jnp.ones((CHUNK, CHUNK), dtype=bool))
    strict = jnp.tril(jnp.ones((CHUNK, CHUNK), dtype=bool), k=-1)
    dec = jnp.exp(jnp.where(tri, gc[..., :, None] - gc[..., None, :], -jnp.inf))
    kb = kc * bc[..., None]
    kkt = jnp.where(strict, jnp.einsum('nghtd,nghsd->nghts', kb, kc) * dec, 0.0)
    amat = kkt + jnp.eye(CHUNK, dtype=kkt.dtype)
    dv = vc.shape[-1]
    rhs = jnp.concatenate([vc * bc[..., None], kb * jnp.exp(gc)[..., None]], axis=-1)
    sol = lax.linalg.triangular_solve(amat, rhs, left_side=True, lower=True, unit_diagonal=True)
    u, w = sol[..., :dv], sol[..., dv:]
    attn = jnp.einsum('nghtd,nghsd->nghts', qc, kc) * dec
    qd = qc * jnp.exp(gc)[..., None]
    gl = gc[..., -1]
    kd = kc * jnp.exp(gl[..., None] - gc)[..., None]

    def step(S, inp):
        ut, wt, at, qt, kt, glt = inp
        vnew = ut - jnp.einsum('ghld,ghde->ghle', wt, S)
        o = jnp.einsum('ghld,ghde->ghle', qt, S) + jnp.einsum('ghts,ghse->ghte', at, vnew)
        S = S * jnp.exp(glt)[..., None, None] + jnp.einsum('ghld,ghle->ghde', kt, vnew)
        return S, o

    S, o = lax.scan(step, S, (u, w, attn, qd, kd, gl))
    return from_chunks(o), S


def gdn_out(o, z, norm_g):
    o = dir_merge(o)
    o = rms_norm(o, norm_g.reshape(GD_HEADS, GD_DH))
    return o.reshape(*o.shape[:2], -1) * jax.nn.silu(z.astype(F32))


def rwkv_inputs(feat, p):
    f = token_shift_mix(feat, p['rw_mu']).astype(F32)
    idx = [int(i) for i in np.cumsum([MIX_W, MIX_W, MIX_W, RW_DECAY_LORA, RW_ICLR_LORA])]
    r, k, v, xw, xa, xg = jnp.split(f, idx, axis=-1)
    heads = lambda a: a.reshape(*a.shape[:-1], RW_HEADS, RW_DH)
    wl = p['rw_w0'] + jnp.einsum('btr,nrw->btnw', jnp.tanh(xw), p['rw_w2'])
    decay = jnp.exp(-jnp.exp(-jax.nn.softplus(-wl) - 0.5))
    iclr = jax.nn.sigmoid(p['rw_a0'] + jnp.einsum('btr,nrw->btnw', xa, p['rw_a2']))
    g = jax.nn.sigmoid(xg) @ p['rw_g2']
    kk = l2_normalize(heads(k * p['rw_kk']))
    kmod = heads(k[:, :, None] * (1 + (iclr - 1) * p['rw_ka']))
    r_h, v_h = heads(r), heads(v)
    b_vec = kk[:, :, None] * heads(iclr)
    dec_h = heads(decay)
    bonus = jnp.sum(r_h[:, :, None] * kmod * p['rw_rk'], axis=(2, 4))
    scan_in = (dir_stack(r_h, r_h), dir_stack(dec_h[:, :, 0], dec_h[:, :, 1]),
               dir_stack(kmod[:, :, 0], kmod[:, :, 1]), dir_stack(v_h, v_h),
               dir_stack(-kk, -kk), dir_stack(b_vec[:, :, 0], b_vec[:, :, 1]))
    return scan_in, (bonus[..., None] * v_h, g)


def rwkv_scan(r, w, k, v, a, b, S):
    def step(S, inp):
        rt, wt, kt, vt, at, bt = inp
        sa = jnp.einsum('ghij,ghj->ghi', S, at)
        S = S * wt[:, :, None, :] + sa[..., None] * bt[:, :, None, :] + vt[..., None] * kt[:, :, None, :]
        return S, jnp.einsum('ghij,ghj->ghi', S, rt)

    xs = tuple(jnp.moveaxis(t, 1, 0) for t in (r, w, k, v, a, b))
    S, y = lax.scan(step, S, xs)
    return jnp.moveaxis(y, 0, 1), S


def rwkv_out(wkv, aux, p):
    y = dir_merge(wkv)
    y = layer_norm(y, p['rw_ln_g'].reshape(RW_HEADS, RW_DH), p['rw_ln_b'].reshape(RW_HEADS, RW_DH), GN_EPS) + aux[0]
    return y.reshape(*y.shape[:2], -1) * aux[1]


def merge_branches(ys, gate_cols, w_br, w_out):
    dt = gate_cols.dtype
    ys = jnp.stack([y.astype(dt) for y in ys], axis=2)
    br = jnp.einsum('btnw,nwd->btnd', ys, w_br)
    gates = jax.nn.sigmoid(gate_cols.reshape(*gate_cols.shape[:2], N_BRANCH, -1))
    return jnp.sum(gates * br, axis=2) @ w_out


def token_mixers(hc, hx, p, need_ctx):
    sc = split_cols(hc @ p['w_in'])
    sx = split_cols(hx @ p['w_in'])
    g = 2 * hx.shape[0]
    st0 = (jnp.zeros((g, ML_HEADS, ML_DH, ML_DH), F32), jnp.zeros((g, ML_HEADS, ML_DH), F32),
           jnp.zeros((g, ML_HEADS), F32))
    ml_c, st = mlstm_scan(*mlstm_inputs(sc[1], sc[3], p['ml_gate_b']), st0)
    ml_x, _ = mlstm_scan(*mlstm_inputs(sx[1], sx[3], p['ml_gate_b']), st)
    gdn_args = (p['gd_conv'], p['gd_a_log'], p['gd_dt_bias'])
    s0 = jnp.zeros((g, GD_HEADS, GD_DH, GD_DH), F32)
    gd_c, s = gdn_scan(*gdn_inputs(sc[4], sc[6], *gdn_args), s0)
    gd_x, _ = gdn_scan(*gdn_inputs(sx[4], sx[6], *gdn_args), s)
    rin_c, raux_c = rwkv_inputs(sc[7], p)
    rin_x, raux_x = rwkv_inputs(sx[7], p)
    S0 = jnp.zeros((g, RW_HEADS, RW_DH, RW_DH), F32)
    wkv_c, S = rwkv_scan(*rin_c, S0)
    wkv_x, _ = rwkv_scan(*rin_x, S)

    def finish(s_cols, ml, gd, wkv, raux):
        ya = conv_module(s_cols[0], p)
        yb = mlstm_out(ml, s_cols[2], p['ml_norm_g'])
        yc = gdn_out(gd, s_cols[5], p['gd_norm_g'])
        yd = rwkv_out(wkv, raux, p)
        return merge_branches([ya, yb, yc, yd], s_cols[8], p['w_br'], p['w_out'])

    out_x = finish(sx, ml_x, gd_x, wkv_x, raux_x)
    out_c = finish(sc, ml_c, gd_c, wkv_c, raux_c) if need_ctx else None
    return out_c, out_x


def conv_ffn(h, up, dw, down, grid):
    u, v = jnp.split(h @ up, 2, axis=-1)
    u = dwconv2d_grid(u, dw) if grid else dwconv1d(u, dw[FFN_K // 2])
    return (jax.nn.gelu(u, approximate=True) * v) @ down


def setup_inputs(seed: int = 0) -> dict:
    key = jax.random.key(seed)
    ks = iter(jax.random.split(key, 48))
    nrm = lambda shape, scale: jax.random.normal(next(ks), shape, F32) * scale
    uni = lambda shape, lo, hi: jax.random.uniform(next(ks), shape, F32, lo, hi)
    D = D_MODEL
    dt = jnp.exp(uni((DEPTH, 2, GD_HEADS), float(np.log(1e-3)), float(np.log(1e-1))))
    return {
        "x": nrm((BATCH, SEQ, D), 1.0),
        "c": nrm((BATCH, D), 1.0),
        "ctx": nrm((BATCH, CTX_LEN, D), 1.0),
        "c_ctx": nrm((D,), 1.0),
        "w_mod": nrm((DEPTH, D, 6 * D), 0.5 * D ** -0.5),
        "b_mod": nrm((DEPTH, 6 * D), 0.02),
        "norm_g": 1.0 + nrm((DEPTH, 4, D), 0.05),
        "w_in": nrm((DEPTH, D, N_IN), D ** -0.5),
        "conv_dw": nrm((DEPTH, CONV_K, MIX_W), CONV_K ** -0.5),
        "conv_b": nrm((DEPTH, MIX_W), 0.02),
        "conv_ln_g": 1.0 + nrm((DEPTH, MIX_W), 0.05),
        "conv_ln_b": nrm((DEPTH, MIX_W), 0.02),
        "ml_gate_b": jnp.stack([nrm((DEPTH, 2, ML_HEADS), 0.1), 3.0 + nrm((DEPTH, 2, ML_HEADS), 0.5)], axis=2),
        "ml_norm_g": 1.0 + nrm((DEPTH, MIX_W), 0.05),
        "gd_conv": nrm((DEPTH, GD_CONV_K, 3 * MIX_W), GD_CONV_K ** -0.5),
        "gd_a_log": jnp.log(uni((DEPTH, 2, GD_HEADS), 1.0, 16.0)),
        "gd_dt_bias": dt + jnp.log(-jnp.expm1(-dt)),
        "gd_norm_g": 1.0 + nrm((DEPTH, MIX_W), 0.05),
        "rw_mu": uni((DEPTH, 2, RW_MIX), 0.0, 0.5),
        "rw_w0": uni((DEPTH, 2, MIX_W), -6.0, 1.0),
        "rw_w2": nrm((DEPTH, 2, RW_DECAY_LORA, MIX_W), 0.5 * RW_DECAY_LORA ** -0.5),
        "rw_a0": nrm((DEPTH, 2, MIX_W), 0.1),
        "rw_a2": nrm((DEPTH, 2, RW_ICLR_LORA, MIX_W), 0.5 * RW_ICLR_LORA ** -0.5),
        "rw_g2": nrm((DEPTH, RW_GATE_LORA, MIX_W), RW_GATE_LORA ** -0.5),
        "rw_kk": 0.85 + nrm((DEPTH, MIX_W), 0.05),
        "rw_ka": 1.0 + nrm((DEPTH, MIX_W), 0.05),
        "rw_rk": nrm((DEPTH, RW_HEADS, RW_DH), 0.1),
        "rw_ln_g": 1.0 + nrm((DEPTH, MIX_W), 0.05),
        "rw_ln_b": nrm((DEPTH, MIX_W), 0.02),
        "w_br": nrm((DEPTH, N_BRANCH, MIX_W, D), MIX_W ** -0.5),
        "w_out": nrm((DEPTH, D, D), D ** -0.5),
        "ffn_up": nrm((DEPTH, D, 2 * D_FF), D ** -0.5),
        "ffn_dw": nrm((DEPTH, FFN_K, FFN_K, D_FF), 1.0 / FFN_K),
        "ffn_down": nrm((DEPTH, D_FF, D), D_FF ** -0.5),
    }


def reference(x, c, ctx, c_ctx, w_mod, b_mod, norm_g, w_in, conv_dw, conv_b, conv_ln_g, conv_ln_b,
              ml_gate_b, ml_norm_g, gd_conv, gd_a_log, gd_dt_bias, gd_norm_g,
              rw_mu, rw_w0, rw_w2, rw_a0, rw_a2, rw_g2, rw_kk, rw_ka, rw_rk, rw_ln_g, rw_ln_b,
              w_br, w_out, ffn_up, ffn_dw, ffn_down):
    cx = ctx
    for l in range(DEPTH):
        last = l == DEPTH - 1
        p = dict(w_in=w_in[l], conv_dw=conv_dw[l], conv_b=conv_b[l], conv_ln_g=conv_ln_g[l],
                 conv_ln_b=conv_ln_b[l], ml_gate_b=ml_gate_b[l], ml_norm_g=ml_norm_g[l],
                 gd_conv=gd_conv[l], gd_a_log=gd_a_log[l], gd_dt_bias=gd_dt_bias[l], gd_norm_g=gd_norm_g[l],
                 rw_mu=rw_mu[l], rw_w0=rw_w0[l], rw_w2=rw_w2[l], rw_a0=rw_a0[l], rw_a2=rw_a2[l],
                 rw_g2=rw_g2[l], rw_kk=rw_kk[l], rw_ka=rw_ka[l], rw_rk=rw_rk[l], rw_ln_g=rw_ln_g[l],
                 rw_ln_b=rw_ln_b[l], w_br=w_br[l], w_out=w_out[l])
        mod_x = (jax.nn.silu(c) @ w_mod[l] + b_mod[l])[:, None, :]
        mod_c = jax.nn.silu(c_ctx) @ w_mod[l] + b_mod[l]
        sh1x, sc1x, gt1x, sh2x, sc2x, gt2x = jnp.split(mod_x, 6, axis=-1)
        sh1c, sc1c, gt1c, sh2c, sc2c, gt2c = jnp.split(mod_c, 6, axis=-1)
        ux = modulate(rms_norm(x, norm_g[l, 0]), sh1x, sc1x)
        uc = modulate(rms_norm(cx, norm_g[l, 0]), sh1c, sc1c)
        mc, mx = token_mixers(uc, ux, p, not last)
        x = x + gt1x * rms_norm(mx, norm_g[l, 1])
        fx = conv_ffn(modulate(rms_norm(x, norm_g[l, 2]), sh2x, sc2x), ffn_up[l], ffn_dw[l], ffn_down[l], True)
        x = x + gt2x * rms_norm(fx, norm_g[l, 3])
        if not last:
            cx = cx + gt1c * rms_norm(mc, norm_g[l, 1])
            fc = conv_ffn(modulate(rms_norm(cx, norm_g[l, 2]), sh2c, sc2c), ffn_up[l], ffn_dw[l], ffn_down[l], False)
            cx = cx + gt2c * rms_norm(fc, norm_g[l, 3])
    return x
```

```python
import contextlib
import numpy as np
import concourse.bass as bass
import concourse.mybir as mybir

F32 = mybir.dt.float32
BF16 = mybir.dt.bfloat16
AF = mybir.ActivationFunctionType
ALU = mybir.AluOpType
AX = mybir.AxisListType

SAFE_SAME_ENGINE = True


class Buf:
    __slots__ = ("name", "w", "r")

    def __init__(self, name):
        self.name = name
        self.w = None
        self.r = []


class Tn:
    def __init__(self, t, name):
        self.t = t
        self.b = Buf(name)
        self.subs = {}

    def __getitem__(self, idx):
        return self.t[idx]

    def sub(self, key):
        if key not in self.subs:
            self.subs[key] = Buf(f"{self.b.name}/{key}")
        return self.subs[key]


class P:
    def __init__(self, n_dma_sems=24):
        self.nc = bass.Bass("TRN2", target_bir_lowering=False)
        nc = self.nc
        self.st = contextlib.ExitStack()
        self.eng = {"pe": nc.tensor, "act": nc.scalar, "dve": nc.vector, "pool": nc.gpsimd, "sp": nc.sync}
        self.sem = {}
        self.cnt = {}
        for e in ("pe", "act", "dve", "pool"):
            self.sem[e] = self.st.enter_context(nc.semaphore(f"sem_{e}"))
            self.cnt[e] = 0
        self.waited = {e: {} for e in self.eng}
        self.dsem = [self.st.enter_context(nc.semaphore(f"dsem{i}")) for i in range(n_dma_sems)]
        self.dtgt = [0] * n_dma_sems
        self.drr = 0
        self.n_inst = 0
        self.out_events = []

    def sb(self, name, shape, dt=F32):
        self._uid = getattr(self, "_uid", 0) + 1
        name = f"{name}_{self._uid}"
        return Tn(self.st.enter_context(self.nc.sbuf_tensor(name, list(shape), dt)), name)

    def ps(self, name, shape, dt=F32):
        return Tn(self.st.enter_context(self.nc.psum_tensor(name, list(shape), dt)), name)

    def dram(self, name, shape, kind, dt=F32):
        return self.nc.dram_tensor(name, list(shape), dt, kind=kind).ap()

    def _wait(self, w, ev):
        if ev is None:
            return
        kind = ev[0]
        if kind == "eng":
            _, e, n = ev
            if e == w and (e == "pe" or not SAFE_SAME_ENGINE):
                return
            key = ("eng", e)
            if self.waited[w].get(key, 0) >= n:
                return
            self.eng[w].wait_ge(self.sem[e], n)
            self.waited[w][key] = n
        else:
            _, i, tgt = ev
            key = ("dma", i)
            if self.waited[w].get(key, 0) >= tgt:
                return
            self.eng[w].wait_ge(self.dsem[i], tgt)
            self.waited[w][key] = tgt

    @staticmethod
    def _b(x):
        return x.b if isinstance(x, Tn) else x

    def _deps(self, w, reads, writes):
        for r in reads:
            self._wait(w, self._b(r).w)
        for x in writes:
            b = self._b(x)
            self._wait(w, b.w)
            for ev in b.r:
                self._wait(w, ev)

    def _record(self, ev, reads, writes):
        for r in reads:
            self._b(r).r.append(ev)
            if len(self._b(r).r) > 64:
                self._b(r).r = self._compact(self._b(r).r)
        for x in writes:
            b = self._b(x)
            b.w = ev
            b.r = []

    @staticmethod
    def _compact(evs):
        best = {}
        for ev in evs:
            k = (ev[0], ev[1])
            if k not in best or best[k][2] < ev[2]:
                best[k] = ev
        return list(best.values())

    def op(self, e, fn, reads=(), writes=()):
        self._deps(e, reads, writes)
        inst = fn(self.eng[e])
        self.cnt[e] += 1
        inst.then_inc(self.sem[e], 1)
        ev = ("eng", e, self.cnt[e])
        self._record(ev, reads, writes)
        self.n_inst += 1
        return ev

    def dma(self, q, out, in_, reads=(), writes=(), is_output=False, **kw):
        self._deps(q, reads, writes)
        i = self.drr
        self.drr = (self.drr + 1) % len(self.dsem)
        if self.dtgt[i] > 0:
            self._wait(q, ("dma", i, self.dtgt[i]))
        inst = self.eng[q].dma_start(out=out, in_=in_, **kw)
        self.dtgt[i] += 16
        inst.then_inc(self.dsem[i], 16)
        ev = ("dma", i, self.dtgt[i])
        self._record(ev, reads, writes)
        if is_output:
            self.out_events.append(ev)
        self.n_inst += 1
        return ev

    def finish(self, w="sp"):
        for ev in self.out_events:
            self._wait(w, ev)
        for i, t in enumerate(self.dtgt):
            if t:
                self._wait(w, ("dma", i, t))
        for e in self.cnt:
            if self.cnt[e]:
                self._wait(w, ("eng", e, self.cnt[e]))

    def close(self):
        self.st.close()


def _p_barrier(self):
    engs = list(self.eng.keys())
    for w in engs:
        for e in self.cnt:
            if self.cnt[e]:
                self._wait(w, ("eng", e, self.cnt[e]))
        for i, t in enumerate(self.dtgt):
            if t:
                self._wait(w, ("dma", i, t))


P.barrier = _p_barrier


@contextlib.contextmanager
def _p_scope(self):
    outer = self.st
    self.st = contextlib.ExitStack()
    try:
        yield
    finally:
        self.barrier()
        self.st.close()
        self.st = outer


P.scope = _p_scope


def _p_init_ps(self, n=8):
    self.pstiles = [self.ps(f"psr{i}", [128, 512]) for i in range(n)]
    self.psi = 0


def _p_next_ps(self):
    t = self.pstiles[self.psi % len(self.pstiles)]
    self.psi += 1
    return t


P.init_ps = _p_init_ps
P.next_ps = _p_next_ps


def bc_last(ap2d, L):
    P_, n = ap2d.shape
    return ap2d.unsqueeze(2).to_broadcast([P_, n, L])


def bc_mid(ap2d, n):
    P_, L = ap2d.shape
    return ap2d.unsqueeze(1).to_broadcast([P_, n, L])


L = 64


def tt(p, out, in0, in1, op, R, W, eng="dve"):
    return p.op(eng, lambda e: e.tensor_tensor(out=out, in0=in0, in1=in1, op=op), reads=R, writes=W)


def ts(p, out, in0, s1, op0, R, W, s2=None, op1=None, eng="dve"):
    if op1 is None:
        return p.op(eng, lambda e: e.tensor_scalar(out=out, in0=in0, scalar1=s1, scalar2=None, op0=op0), reads=R, writes=W)
    return p.op(eng, lambda e: e.tensor_scalar(out=out, in0=in0, scalar1=s1, scalar2=s2, op0=op0, op1=op1), reads=R, writes=W)


def stt(p, out, in0, scalar, in1, op0, op1, R, W):
    return p.op("dve", lambda e: e.scalar_tensor_tensor(out=out, in0=in0, scalar=scalar, in1=in1, op0=op0, op1=op1), reads=R, writes=W)


def act(p, out, in_, func, R, W, bias=None, scale=None):
    kw = {}
    if bias is not None:
        kw["bias"] = bias
    if scale is not None:
        kw["scale"] = scale
    return p.op("act", lambda e: e.activation(out=out, in_=in_, func=func, **kw), reads=R, writes=W)


def mm(p, out, lhsT, rhs, R, W, start=True, stop=True):
    return p.op("pe", lambda e: e.matmul(out, lhsT=lhsT, rhs=rhs, start=start, stop=stop), reads=R, writes=W)


def cp(p, out, in_, R, W, eng):
    if eng == "act":
        return act(p, out, in_, AF.Copy, R, W)
    return p.op(eng, lambda e: e.tensor_copy(out=out, in_=in_), reads=R, writes=W)


def make_consts(p):
    c = {}
    c["ones"] = p.sb("c_ones", [128, 128])
    p.op("dve", lambda e: e.memset(c["ones"][:, :], 1.0), writes=[c["ones"]])
    for name, pat, cm, cmp_ in (("UT", 1, -1, ALU.is_ge), ("LT", -1, 1, ALU.is_ge), ("UTs", 1, -1, ALU.is_gt),
                                ("LTs", -1, 1, ALU.is_gt), ("I", 1, -1, ALU.is_equal)):
        t = p.sb("c_" + name, [128, 128])
        p.op("pool", lambda e: e.affine_select(out=t[:, :], in_=c["ones"][:, :], pattern=[[pat, 128]], compare_op=cmp_,
                                               fill=0.0, base=0, channel_multiplier=cm), reads=[c["ones"]], writes=[t])
        c[name] = t
    c["eps6"] = p.sb("c_eps6", [128, 1])
    p.op("dve", lambda e: e.memset(c["eps6"][:, :], 1e-6), writes=[c["eps6"]])
    return c


def dir_masks(c, d):
    if d == 0:
        return c["UT"], c["UTs"], c["LT"], c["LTs"]
    return c["LT"], c["LTs"], c["UT"], c["UTs"]


def chunk_order(d, nc_ctx, nc_lat):
    n = nc_ctx + nc_lat
    if d == 0:
        return list(range(n))
    return list(range(nc_ctx - 1, -1, -1)) + list(range(n - 1, nc_ctx - 1, -1))


def rowbc(p, c, col, out, pout, nch, scratch):
    tt(p, scratch[:, :, :], bc_last(col[0:L, 0:nch], L), bc_mid(c["I"][0:L, 0:L], nch), ALU.mult, [col, c["I"]], [scratch])
    flat = scratch[:, :, :].rearrange("p n t -> p (n t)")
    tot = nch * L
    for i, c0 in enumerate(range(0, tot, 512)):
        c1 = min(c0 + 512, tot)
        ps = p.next_ps()
        mm(p, ps[0:pout, 0:c1 - c0], c["ones"][0:L, 0:pout], flat[:, c0:c1], [c["ones"], scratch], [ps])
        cp(p, out[0:pout, c0:c1], ps[0:pout, 0:c1 - c0], [ps], [out], "act" if i % 2 else "dve")


def colmm(p, lhsT_ap, rhs_tn, rhs_ap, out_tn, out_ap, pout, ncol, Rextra=()):
    ps = p.next_ps()
    mm(p, ps[0:pout, 0:ncol], lhsT_ap, rhs_ap, [rhs_tn] + list(Rextra), [ps])
    cp(p, out_ap, ps[0:pout, 0:ncol], [ps], [out_tn], "dve")


def batched_chunk_mm(p, nch, lhsT_fn, rhs_fn, R, mrows, ncols, evac):
    G = 512 // ncols
    for n0 in range(0, nch, G):
        g = min(G, nch - n0)
        ps = p.next_ps()
        for j in range(g):
            n = n0 + j
            mm(p, ps[0:mrows, j * ncols:(j + 1) * ncols], lhsT_fn(n), rhs_fn(n), R, [ps])
        evac(ps, n0, g)


def transpose_chunks(p, c, src_fn, Rsrc, kin, dst, nch, eng_alt=True):
    m = None
    cnt = [0]

    def evac(ps, n0, g):
        eng = "act" if (cnt[0] % 2 and eng_alt) else "dve"
        cnt[0] += 1
        cp(p, dst[:, n0:n0 + g, :], ps[0:dst.t.shape[0], 0:g * kin].rearrange("p (g k) -> p g k", k=kin), [ps], [dst], eng)

    G = 512 // kin
    for n0 in range(0, nch, G):
        g = min(G, nch - n0)
        ps = p.next_ps()
        for j in range(g):
            src = src_fn(n0 + j)
            mm_out = ps[0:src.shape[1], j * kin:(j + 1) * kin]
            p.op("pe", lambda e: e.transpose(mm_out, src, c["I"][0:kin, 0:kin]), reads=list(Rsrc) + [c["I"]], writes=[ps])
        evac(ps, n0, g)


def tri_inverse_T(p, c, X, Y, nch, tmp):
    Pm, Qm, X2, Y2 = tmp["P"], tmp["Q"], tmp["X2"], tmp["Y2"]
    I3 = bc_mid(c["I"][0:L, 0:L], nch)
    tt(p, Pm[:, :, :], Y[:, :, :], I3, ALU.add, [Y, c["I"]], [Pm.sub(n0) for n0 in range(0, nch, 8)])
    tt(p, Qm[:, :, :], X[:, :, :], I3, ALU.add, [X, c["I"]], [Qm.sub(n0) for n0 in range(0, nch, 8)])
    curX, curY, nxtX, nxtY = X, Y, X2, Y2
    for lvl in range(5):
        def ev_x(ps, n0, g):
            cp(p, nxtX[:, n0:n0 + g, :], ps[0:L, 0:g * L].rearrange("p (g k) -> p g k", k=L), [ps], [nxtX], "act")

        def ev_y(ps, n0, g):
            cp(p, nxtY[:, n0:n0 + g, :], ps[0:L, 0:g * L].rearrange("p (g k) -> p g k", k=L), [ps], [nxtY], "dve")

        batched_chunk_mm(p, nch, lambda n: curY[:, n, :], lambda n: curX[:, n, :], [curX, curY], L, L, ev_x)
        batched_chunk_mm(p, nch, lambda n: curX[:, n, :], lambda n: curY[:, n, :], [curX, curY], L, L, ev_y)
        curX, curY, nxtX, nxtY = nxtX, nxtY, curX, curY

        pend = []

        def ev_p(ps, n0, g):
            pend.append(("P", ps, n0, g))

        def ev_q(ps, n0, g):
            pend.append(("Q", ps, n0, g))

        G = 8
        for n0 in range(0, nch, G):
            g = min(G, nch - n0)
            psP = p.next_ps()
            for j in range(g):
                n = n0 + j
                mm(p, psP[0:L, j * L:(j + 1) * L], Qm[:, n, :], curY[:, n, :], [Qm.sub(n0), curY], [psP])
            psQ = p.next_ps()
            for j in range(g):
                n = n0 + j
                mm(p, psQ[0:L, j * L:(j + 1) * L], Pm[:, n, :], curX[:, n, :], [Pm.sub(n0), curX], [psQ])
            tt(p, Pm[:, n0:n0 + g, :], Pm[:, n0:n0 + g, :], psP[0:L, 0:g * L].rearrange("p (g k) -> p g k", k=L), ALU.add,
               [psP, Pm.sub(n0)], [Pm.sub(n0)])
            if lvl < 4:
                tt(p, Qm[:, n0:n0 + g, :], Qm[:, n0:n0 + g, :], psQ[0:L, 0:g * L].rearrange("p (g k) -> p g k", k=L), ALU.add,
                   [psQ, Qm.sub(n0)], [Qm.sub(n0)])
    subs = [Pm.sub(n0) for n0 in range(0, nch, 8)]
    return Pm, subs


def chunk_loops(p, chains):
    nsteps = max(len(ch["order"]) for ch in chains)
    for ch in chains:
        if "cur" not in ch["st"]:
            ch["st"]["cur"] = 0
            p.op("dve", lambda e: e.memset(ch["M"][0][:, :], 0.0), writes=[ch["M"][0]])
    for i in range(nsteps):
        for ch in chains:
            if i >= len(ch["order"]):
                continue
            n = ch["order"][i]
            dk, dvp = ch["dk"], ch["dvp"]
            M = ch["M"][ch["st"]["cur"]]
            Mn = ch["M"][1 - ch["st"]["cur"]]
            R = list(ch.get("R", [])) + [ch["QT"], ch["A1T"], ch["BL"], ch["decay"], ch["U0"]]
            if ch.get("AhatT") is not None:
                R.append(ch["AhatT"])
            if ch.get("K0") is not None:
                R += list(ch["K0"])
            if ch.get("AhatT") is not None:
                psu = p.next_ps()
                mm(p, psu[0:L, 0:dvp], ch["AhatT"][:, n, :], M[:, :], [M] + R, [psu])
                u = ch["ub"][i % 2]
                tt(p, u[:, :], psu[0:L, 0:dvp], ch["U0"][:, n, :], ALU.add, [psu] + R, [u])
                u_ap, u_dep = u[:, :], [u]
            else:
                u_ap, u_dep = ch["U0"][:, n, :], []
            psy = p.next_ps()
            mm(p, psy[0:L, 0:dvp], ch["QT"][:, n * L:(n + 1) * L], M[:, :], [M] + R, [psy], start=True, stop=False)
            mm(p, psy[0:L, 0:dvp], ch["A1T"][:, n, :], u_ap, u_dep + R, [psy], start=False, stop=True)
            cp(p, ch["Y"][:, n, :], psy[0:L, 0:dvp], [psy], [ch["Y"].sub(n)], "act")
            psm = p.next_ps()
            if ch.get("K0") is not None:
                KL, V = ch["K0"]
                mm(p, psm[0:dk, 0:dvp], KL[:, n, :], V[:, n, :], R, [psm], start=True, stop=False)
                mm(p, psm[0:dk, 0:dvp], ch["BL"][:, n, :], u_ap, u_dep + R, [psm], start=False, stop=True)
            else:
                mm(p, psm[0:dk, 0:dvp], ch["BL"][:, n, :], u_ap, u_dep + R, [psm])
            stt(p, Mn[:, :], M[:, :], ch["decay"][:, n:n + 1], psm[0:dk, 0:dvp], ALU.mult, ALU.add, [M, psm] + R, [Mn])
            ch["st"]["cur"] = 1 - ch["st"]["cur"]


def emit_mlstm(p, c, io, nc_ctx, nc_lat, bs=16):
    nch = nc_ctx + nc_lat
    T = nch * L
    DH = 128
    with p.scope():
        H = p.sb("ml_H", [L, nch, DH])
        gb = p.sb("ml_gb", [L, 4]); ngb = p.sb("ml_ng", [L, DH]); ngate = p.sb("ml_ngb", [L, 4])
        p.dma("sp", gb[:, :], io["gb"], writes=[gb])
        p.dma("sp", ngb[:, :], io["ng"], writes=[ngb])
        ts(p, ngate[:, :], gb[:, :], -1.0, ALU.mult, [gb], [ngate])
        with p.scope():
            qT = p.sb("ml_qT", [128, T]); kT = p.sb("ml_kT", [128, T])
            k_tm = p.sb("ml_ktm", [L, nch, DH]); vp = p.sb("ml_vp", [L, nch, DH + 1])
            p.dma("sp", qT[:, :], io["qT"], writes=[qT])
            p.dma("sp", kT[:, :], io["kT"], writes=[kT])
            p.dma("sp", k_tm[:, :, :], io["k_tm"], writes=[k_tm])
            p.dma("sp", vp[:, :, 0:DH], io["v_tm"], writes=[vp.sub("v")])
            p.op("pool", lambda e: e.memset(vp[:, :, DH:DH + 1], 1.0), writes=[vp.sub("one")])
            act(p, qT[:, :], qT[:, :], AF.Copy, [qT], [qT], scale=DH ** -0.5)
            VP = [vp.sub("v"), vp.sub("one")]
            for d in range(2):
                with p.scope():
                    mT_incl, _, _, _ = dir_masks(c, d)
                    gi = p.sb("ml_gi", [L, nch]); gf = p.sb("ml_gf", [L, nch])
                    p.dma("sp", gi[:, :], io[f"gi{d}"], writes=[gi])
                    p.dma("sp", gf[:, :], io[f"gf{d}"], writes=[gf])
                    li = p.sb("ml_li", [L, nch]); lf = p.sb("ml_lf", [L, nch]); bt = p.sb("ml_bt", [L, nch])
                    blb = p.sb("ml_blb", [128, nch]); ecol = p.sb("ml_ecol", [L, nch]); eli = p.sb("ml_eli", [L, nch])
                    decay = p.sb("ml_decay", [128, nch])
                    ts(p, li[:, :], gi[:, :], gb[:, 2 * d:2 * d + 1], ALU.add, [gi, gb], [li])
                    act(p, lf[:, :], gf[:, :], AF.Exp, [gf, ngate], [lf], bias=ngate[:, 2 * d + 1:2 * d + 2], scale=-1.0)
                    act(p, lf[:, :], lf[:, :], AF.Ln, [lf], [lf], bias=1.0)
                    ts(p, lf[:, :], lf[:, :], -1.0, ALU.mult, [lf], [lf])
                    colmm(p, mT_incl[0:L, 0:L], lf, lf[:, :], bt, bt[:, :], L, nch, [mT_incl])
                    colmm(p, c["ones"][0:L, 0:128], lf, lf[:, :], blb, blb[:, :], 128, nch, [c["ones"]])
                    act(p, decay[:, :], blb[:, :], AF.Exp, [blb], [decay])
                    act(p, eli[:, :], li[:, :], AF.Exp, [li], [eli])
                    tt(p, ecol[:, :], li[:, :], bt[:, :], ALU.subtract, [li, bt], [ecol])
                    tt(p, ecol[:, :], ecol[:, :], blb[0:L, :], ALU.add, [ecol, blb], [ecol])
                    act(p, ecol[:, :], ecol[:, :], AF.Exp, [ecol], [ecol])
                    Ms = [p.sb(f"ml_M{i}", [128, DH + 1]) for i in range(2)]
                    st = {}
                    for (lo, hi) in blocks_for(d, nc_ctx, nc_lat, bs):
                        nb = hi - lo
                        TB = nb * L
                        with p.scope():
                            brow = p.sb("ml_brow", [128, TB]); scr = p.sb("ml_scr", [L, nb, L]); btb = p.sb("ml_btb", [L, nb])
                            cp(p, btb[:, :], bt[:, lo:hi], [bt], [btb], "dve")
                            rowbc(p, c, btb, brow, 128, nb, scr)
                            wT = p.sb("ml_wT", [L, nb, L])
                            tt(p, wT[:, :, :], brow[0:L, :].rearrange("p (n t) -> p n t", t=L), bc_last(btb[:, :], L), ALU.subtract, [brow, btb], [wT])
                            ts(p, wT[:, :, :], wT[:, :, :], 0.0, ALU.min, [wT], [wT])
                            act(p, wT[:, :, :], wT[:, :, :], AF.Exp, [wT], [wT])
                            tt(p, wT[:, :, :], wT[:, :, :], bc_mid(mT_incl[0:L, 0:L], nb), ALU.mult, [wT, mT_incl], [wT])
                            tt(p, wT[:, :, :], wT[:, :, :], bc_last(eli[:, lo:hi], L), ALU.mult, [wT, eli], [wT])
                            qdT = brow
                            act(p, brow[:, :], brow[:, :], AF.Exp, [brow], [brow])
                            tt(p, qdT[:, :], qT[:, lo * L:hi * L], brow[:, :], ALU.mult, [qT, brow], [qdT])
                            A1T = scr

                            def evac(ps, n0, g):
                                tt(p, A1T[:, n0:n0 + g, :], ps[0:L, 0:g * L].rearrange("p (g k) -> p g k", k=L), wT[:, n0:n0 + g, :], ALU.mult,
                                   [ps, wT], [A1T])

                            batched_chunk_mm(p, nb, lambda n: kT[:, (lo + n) * L:(lo + n + 1) * L], lambda n: qT[:, (lo + n) * L:(lo + n + 1) * L],
                                             [kT, qT], L, L, evac)
                            BL = p.sb("ml_BL", [L, nb, DH])
                            tt(p, BL[:, :, :], k_tm[:, lo:hi, :], bc_last(ecol[:, lo:hi], DH), ALU.mult, [k_tm, ecol], [BL])
                            Y = p.sb("ml_Y", [L, nb, DH + 1]); vpb = p.sb("ml_vpb", [L, nb, DH + 1]); dcy = p.sb("ml_dcy", [128, nb])
                            cp(p, vpb[:, :, :], vp[:, lo:hi, :], VP, [vpb], "dve")
                            cp(p, dcy[:, :], decay[:, lo:hi], [decay], [dcy], "dve")
                            order = list(range(nb)) if d == 0 else list(range(nb - 1, -1, -1))
                            chain = dict(dk=128, dvp=DH + 1, order=order, QT=qdT, A1T=A1T, BL=BL, decay=dcy,
                                         U0=vpb, AhatT=None, Y=Y, K0=None, M=Ms, R=[], st=st)
                            chunk_loops(p, [chain])
                            Ysubs = [Y.sub(n) for n in range(nb)]
                            den = p.sb("ml_den", [L, nb])
                            ts(p, den[:, :], Y[:, :, DH], -1.0, ALU.mult, Ysubs, [den])
                            tt(p, den[:, :], den[:, :], Y[:, :, DH], ALU.max, Ysubs + [den], [den])
                            ts(p, den[:, :], den[:, :], 1.0, ALU.max, [den], [den])
                            p.op("dve", lambda e: e.reciprocal(out=den[:, :], in_=den[:, :]), reads=[den], writes=[den])
                            if d == 0:
                                tt(p, H[:, lo:hi, :], Y[:, :, 0:DH], bc_last(den[:, :], DH), ALU.mult, Ysubs + [den], [H])
                            else:
                                tt(p, Y[:, :, 0:DH], Y[:, :, 0:DH], bc_last(den[:, :], DH), ALU.mult, Ysubs + [den], [Y])
                                tt(p, H[:, lo:hi, :], H[:, lo:hi, :], Y[:, :, 0:DH], ALU.add, [H, Y], [H])
        with p.scope():
            sq = p.sb("ml_sq", [L, nch, DH]); ss = p.sb("ml_ss", [L, nch]); o_tm = p.sb("ml_otm", [L, nch, DH])
            p.dma("sp", o_tm[:, :, :], io["o_tm"], writes=[o_tm])
            tt(p, sq[:, :, :], H[:, :, :], H[:, :, :], ALU.mult, [H], [sq])
            p.op("dve", lambda e: e.tensor_reduce(out=ss[:, :], in_=sq[:, :, :], axis=AX.X, op=ALU.add), reads=[sq], writes=[ss])
            act(p, ss[:, :], ss[:, :], AF.Sqrt, [ss, c["eps6"]], [ss], scale=1.0 / DH, bias=c["eps6"][0:L, 0:1])
            p.op("dve", lambda e: e.reciprocal(out=ss[:, :], in_=ss[:, :]), reads=[ss], writes=[ss])
            tt(p, H[:, :, :], H[:, :, :], bc_last(ss[:, :], DH), ALU.mult, [H, ss], [H])
            tt(p, H[:, :, :], H[:, :, :], bc_mid(ngb[:, :], nch), ALU.mult, [H, ngb], [H])
            act(p, o_tm[:, :, :], o_tm[:, :, :], AF.Sigmoid, [o_tm], [o_tm])
            tt(p, H[:, :, :], H[:, :, :], o_tm[:, :, :], ALU.mult, [H, o_tm], [H])
            p.dma("pool", io["yb"], H[:, :, :], reads=[H], is_output=True)


def blocks_for(d, nc_ctx, nc_lat, bs=24):
    bl = []
    if nc_ctx:
        bl.append((0, nc_ctx))
    lat = []
    for lo in range(nc_ctx, nc_ctx + nc_lat, bs):
        lat.append((lo, min(lo + bs, nc_ctx + nc_lat)))
    if d == 0:
        return bl + lat
    return bl + lat[::-1]


def l2norm_fm(p, c, x, T, scale, tmp):
    act(p, tmp[:, :], x[:, :], AF.Square, [x], [tmp])
    for i, c0 in enumerate(range(0, T, 512)):
        c1 = min(c0 + 512, T)
        ps = p.next_ps()
        mm(p, ps[:, 0:c1 - c0], c["ones"][:, :], tmp[:, c0:c1], [c["ones"], tmp], [ps])
        act(p, tmp[:, c0:c1], ps[:, 0:c1 - c0], AF.Sqrt, [ps, c["eps6"]], [tmp], bias=c["eps6"][:, 0:1])
    p.op("dve", lambda e: e.reciprocal(out=tmp[:, :], in_=tmp[:, :]), reads=[tmp], writes=[tmp])
    stt(p, x[:, :], x[:, :], float(scale), tmp[:, :], ALU.mult, ALU.mult, [x, tmp], [x])


def emit_gdn(p, c, io, nc_ctx, nc_lat, bs=24):
    nch = nc_ctx + nc_lat
    T = nch * L
    Tc, Tl = nc_ctx * L, nc_lat * L
    DH = 128
    with p.scope():
        O = p.sb("gd_O", [L, nch, DH])
        cw = p.sb("gd_cw", [128, 15]); gp = p.sb("gd_gp", [L, 4]); ngb = p.sb("gd_ng", [L, DH])
        p.dma("sp", cw[:, :], io["cw"], writes=[cw]); p.dma("sp", gp[:, :], io["gp"], writes=[gp]); p.dma("sp", ngb[:, :], io["ng"], writes=[ngb])
        p.gd_big = contextlib.ExitStack()
        p.gd_big.enter_context(p.scope())
        q_fm = p.sb("gd_q", [128, T]); k_fm = p.sb("gd_k", [128, T])
        k_tm = p.sb("gd_ktm", [L, nch, DH]); v_tm = p.sb("gd_vtm", [L, nch, DH])
        with p.scope():
            v_fm = p.sb("gd_v", [128, T]); raw = p.sb("gd_raw", [128, T + 8]); tmp = p.sb("gd_tmp", [128, T])
            for wi, (nm, dst) in enumerate((("qp", q_fm), ("kp", k_fm), ("vp", v_fm))):
                p.dma("sp", raw[:, :], io[nm], writes=[raw])
                for (o_in, o_out, tl) in ((0, 0, Tc), (Tc + 4, Tc, Tl)):
                    if tl == 0:
                        continue
                    ts(p, dst[:, o_out:o_out + tl], raw[:, o_in:o_in + tl], cw[:, wi * 5:wi * 5 + 1], ALU.mult, [raw, cw], [dst])
                    for j in range(1, 5):
                        stt(p, dst[:, o_out:o_out + tl], raw[:, o_in + j:o_in + j + tl], cw[:, wi * 5 + j:wi * 5 + j + 1],
                            dst[:, o_out:o_out + tl], ALU.mult, ALU.add, [raw, cw, dst], [dst])
                act(p, dst[:, :], dst[:, :], AF.Silu, [dst], [dst])
            l2norm_fm(p, c, q_fm, T, DH ** -0.5, tmp)
            l2norm_fm(p, c, k_fm, T, 1.0, tmp)
            transpose_chunks(p, c, lambda n: k_fm[:, n * L:(n + 1) * L], [k_fm], 128, k_tm, nch)
            transpose_chunks(p, c, lambda n: v_fm[:, n * L:(n + 1) * L], [v_fm], 128, v_tm, nch)
        for d in range(2):
            with p.scope():
                mT_incl, mT_str, m_incl, m_str = dir_masks(c, d)
                ar = p.sb("gd_ar", [L, nch]); br = p.sb("gd_br", [L, nch])
                p.dma("sp", ar[:, :], io[f"a{d}"], writes=[ar]); p.dma("sp", br[:, :], io[f"b{d}"], writes=[br])
                lg = p.sb("gd_lg", [L, nch]); beta = p.sb("gd_beta", [L, nch]); nbeta = p.sb("gd_nbeta", [L, nch])
                gc = p.sb("gd_gc", [L, nch]); glb = p.sb("gd_glb", [128, nch]); decay = p.sb("gd_decay", [128, nch])
                negA = p.sb("gd_negA", [L, 1]); egc = p.sb("gd_egc", [L, nch]); bege = p.sb("gd_bege", [L, nch]); ekd = p.sb("gd_ekd", [L, nch])
                act(p, negA[:, :], gp[:, 2 * d:2 * d + 1], AF.Exp, [gp], [negA])
                ts(p, negA[:, :], negA[:, :], -1.0, ALU.mult, [negA], [negA])
                act(p, lg[:, :], ar[:, :], AF.Exp, [ar, gp], [lg], bias=gp[:, 2 * d + 1:2 * d + 2])
                act(p, lg[:, :], lg[:, :], AF.Ln, [lg], [lg], bias=1.0)
                ts(p, lg[:, :], lg[:, :], negA[:, 0:1], ALU.mult, [lg, negA], [lg])
                act(p, beta[:, :], br[:, :], AF.Sigmoid, [br], [beta])
                ts(p, nbeta[:, :], beta[:, :], -1.0, ALU.mult, [beta], [nbeta])
                colmm(p, mT_incl[0:L, 0:L], lg, lg[:, :], gc, gc[:, :], L, nch, [mT_incl])
                colmm(p, c["ones"][0:L, 0:128], lg, lg[:, :], glb, glb[:, :], 128, nch, [c["ones"]])
                act(p, decay[:, :], glb[:, :], AF.Exp, [glb], [decay])
                act(p, egc[:, :], gc[:, :], AF.Exp, [gc], [egc])
                tt(p, bege[:, :], egc[:, :], beta[:, :], ALU.mult, [egc, beta], [bege])
                tt(p, ekd[:, :], glb[0:L, :], gc[:, :], ALU.subtract, [glb, gc], [ekd])
                act(p, ekd[:, :], ekd[:, :], AF.Exp, [ekd], [ekd])
                Ms = [p.sb(f"gd_M{i}", [128, DH]) for i in range(2)]
                ubs = [p.sb(f"gd_u{i}", [L, DH]) for i in range(2)]
                st = {}
                for (lo, hi) in blocks_for(d, nc_ctx, nc_lat, bs):
                    nb = hi - lo
                    TB = nb * L
                    with p.scope():
                        grow = p.sb("gd_grow", [128, TB]); brow = p.sb("gd_brow", [L, TB]); scr = p.sb("gd_scr", [L, nb, L])
                        gcb = p.sb("gd_gcb", [L, nb]); nbb = p.sb("gd_nbb", [L, nb])
                        cp(p, gcb[:, :], gc[:, lo:hi], [gc], [gcb], "dve")
                        cp(p, nbb[:, :], nbeta[:, lo:hi], [nbeta], [nbb], "dve")
                        rowbc(p, c, gcb, grow, 128, nb, scr)
                        rowbc(p, c, nbb, brow, L, nb, scr)
                        decT = p.sb("gd_decT", [L, nb, L]); dec = p.sb("gd_dec", [L, nb, L]); wA = p.sb("gd_wA", [L, nb, L])
                        g3 = grow[0:L, :].rearrange("p (n t) -> p n t", t=L)
                        tt(p, decT[:, :, :], g3, bc_last(gcb[:, :], L), ALU.subtract, [grow, gcb], [decT])
                        ts(p, decT[:, :, :], decT[:, :, :], 0.0, ALU.min, [decT], [decT])
                        act(p, decT[:, :, :], decT[:, :, :], AF.Exp, [decT], [decT])
                        tt(p, dec[:, :, :], bc_last(gcb[:, :], L), g3, ALU.subtract, [grow, gcb], [dec])
                        ts(p, dec[:, :, :], dec[:, :, :], 0.0, ALU.min, [dec], [dec])
                        act(p, dec[:, :, :], dec[:, :, :], AF.Exp, [dec], [dec])
                        tt(p, wA[:, :, :], decT[:, :, :], bc_mid(mT_incl[0:L, 0:L], nb), ALU.mult, [decT, mT_incl], [wA])
                        tt(p, decT[:, :, :], decT[:, :, :], bc_mid(mT_str[0:L, 0:L], nb), ALU.mult, [decT, mT_str], [decT])
                        tt(p, decT[:, :, :], decT[:, :, :], brow[:, :].rearrange("p (n t) -> p n t", t=L), ALU.mult, [decT, brow], [decT])
                        tt(p, dec[:, :, :], dec[:, :, :], bc_mid(m_str[0:L, 0:L], nb), ALU.mult, [dec, m_str], [dec])
                        tt(p, dec[:, :, :], dec[:, :, :], bc_last(nbb[:, :], L), ALU.mult, [dec, nbb], [dec])
                        X, Y = dec, decT

                        def ev_g(ps, n0, g):
                            v = ps[0:L, 0:g * L].rearrange("p (g k) -> p g k", k=L)
                            tt(p, X[:, n0:n0 + g, :], v, X[:, n0:n0 + g, :], ALU.mult, [ps, X], [X])
                            tt(p, Y[:, n0:n0 + g, :], v, Y[:, n0:n0 + g, :], ALU.mult, [ps, Y], [Y])

                        batched_chunk_mm(p, nb, lambda n: k_fm[:, (lo + n) * L:(lo + n + 1) * L], lambda n: k_fm[:, (lo + n) * L:(lo + n + 1) * L],
                                         [k_fm], L, L, ev_g)

                        def ev_a(ps, n0, g):
                            v = ps[0:L, 0:g * L].rearrange("p (g k) -> p g k", k=L)
                            tt(p, wA[:, n0:n0 + g, :], v, wA[:, n0:n0 + g, :], ALU.mult, [ps, wA], [wA])

                        batched_chunk_mm(p, nb, lambda n: k_fm[:, (lo + n) * L:(lo + n + 1) * L], lambda n: q_fm[:, (lo + n) * L:(lo + n + 1) * L],
                                         [k_fm, q_fm], L, L, ev_a)
                        tmp = {k_: p.sb("gd_ti" + k_, [L, nb, L]) for k_ in ("P", "Q", "X2", "Y2")}
                        Pm, Psubs = tri_inverse_T(p, c, X, Y, nb, tmp)
                        vb = p.sb("gd_vb", [L, nb, DH]); kbg = p.sb("gd_kbg", [L, nb, DH]); BL = p.sb("gd_BL", [L, nb, DH])
                        tt(p, vb[:, :, :], v_tm[:, lo:hi, :], bc_last(beta[:, lo:hi], DH), ALU.mult, [v_tm, beta], [vb])
                        tt(p, kbg[:, :, :], k_tm[:, lo:hi, :], bc_last(bege[:, lo:hi], DH), ALU.mult, [k_tm, bege], [kbg])
                        tt(p, BL[:, :, :], k_tm[:, lo:hi, :], bc_last(ekd[:, lo:hi], DH), ALU.mult, [k_tm, ekd], [BL], eng="pool")
                        U0 = p.sb("gd_U0", [L, nb, DH]); AhatT = p.sb("gd_Ah", [128, nb, L])

                        def ev_u(ps, n0, g):
                            cp(p, U0[:, n0:n0 + g, :], ps[0:L, 0:g * DH].rearrange("p (g k) -> p g k", k=DH), [ps], [U0], "act")

                        batched_chunk_mm(p, nb, lambda n: Pm[:, n, :], lambda n: vb[:, n, :], Psubs + [vb], L, DH, ev_u)

                        def ev_w(ps, n0, g):
                            ts(p, AhatT[:, n0:n0 + g, :], ps[0:128, 0:g * L].rearrange("p (g k) -> p g k", k=L), -1.0, ALU.mult, [ps], [AhatT])

                        batched_chunk_mm(p, nb, lambda n: kbg[:, n, :], lambda n: Pm[:, n, :], Psubs + [kbg], 128, L, ev_w)
                        act(p, grow[:, :], grow[:, :], AF.Exp, [grow], [grow])
                        tt(p, grow[:, :], grow[:, :], q_fm[:, lo * L:hi * L], ALU.mult, [grow, q_fm], [grow])
                        Yo = p.sb("gd_Yo", [L, nb, DH])
                        order = list(range(nb)) if d == 0 else list(range(nb - 1, -1, -1))
                        dcy = p.sb("gd_dcy", [128, nb])
                        cp(p, dcy[:, :], decay[:, lo:hi], [decay], [dcy], "dve")
                        chain = dict(dk=128, dvp=DH, order=order, QT=grow, A1T=wA, BL=BL, decay=dcy, U0=U0, AhatT=AhatT, Y=Yo,
                                     K0=None, M=Ms, ub=ubs, R=[], st=st)
                        chunk_loops(p, [chain])
                        Ysubs = [Yo.sub(n) for n in range(nb)]
                        if d == 0:
                            cp(p, O[:, lo:hi, :], Yo[:, :, :], Ysubs, [O], "dve")
                        else:
                            tt(p, O[:, lo:hi, :], O[:, lo:hi, :], Yo[:, :, :], ALU.add, Ysubs + [O], [O])
        p.gd_big.close()
        with p.scope():
            sq = p.sb("gd_sq", [L, nch, DH]); ss = p.sb("gd_ss", [L, nch]); z = p.sb("gd_z", [L, nch, DH])
            p.dma("sp", z[:, :, :], io["z_tm"], writes=[z])
            tt(p, sq[:, :, :], O[:, :, :], O[:, :, :], ALU.mult, [O], [sq])
            p.op("dve", lambda e: e.tensor_reduce(out=ss[:, :], in_=sq[:, :, :], axis=AX.X, op=ALU.add), reads=[sq], writes=[ss])
            act(p, ss[:, :], ss[:, :], AF.Sqrt, [ss, c["eps6"]], [ss], scale=1.0 / DH, bias=c["eps6"][0:L, 0:1])
            p.op("dve", lambda e: e.reciprocal(out=ss[:, :], in_=ss[:, :]), reads=[ss], writes=[ss])
            tt(p, O[:, :, :], O[:, :, :], bc_last(ss[:, :], DH), ALU.mult, [O, ss], [O])
            tt(p, O[:, :, :], O[:, :, :], bc_mid(ngb[:, :], nch), ALU.mult, [O, ngb], [O])
            act(p, z[:, :, :], z[:, :, :], AF.Silu, [z], [z])
            tt(p, O[:, :, :], O[:, :, :], z[:, :, :], ALU.mult, [O, z], [O])
            p.dma("pool", io["yc"], O[:, :, :], reads=[O], is_output=True)


def emit_rwkv(p, c, io, nc_ctx, nc_lat, bs=16, heads=(0, 1)):
    nch = nc_ctx + nc_lat
    T = nch * L
    Tc, Tl = nc_ctx * L, nc_lat * L
    N = 64

    def shift3(dst_ap, raw, P_, TB, mu0, mu1, c0, R, W):
        ts(p, dst_ap, raw[0:P_, 1:1 + TB], c0, ALU.mult, R, W)
        stt(p, dst_ap, raw[0:P_, 0:TB], mu0, dst_ap, ALU.mult, ALU.add, R + W, W)
        stt(p, dst_ap, raw[0:P_, 2:2 + TB], mu1, dst_ap, ALU.mult, ALU.add, R + W, W)

    def coef(mu_tn, ncol, P_, name):
        co = p.sb(name, [P_, ncol // 2])
        for i in range(ncol // 2):
            tt(p, co[:, i:i + 1], mu_tn[:, 2 * i:2 * i + 1], mu_tn[:, 2 * i + 1:2 * i + 2], ALU.add, [mu_tn], [co])
        ts(p, co[:, :], co[:, :], -1.0, ALU.mult, [co], [co], s2=1.0, op1=ALU.add)
        return co

    def poff(lo):
        return lo * L if lo < nc_ctx else (Tc + 2) + (lo - nc_ctx) * L

    with p.scope():
        muw = p.sb("rw_muw", [96, 2]); mua = p.sb("rw_mua", [96, 2]); mug = p.sb("rw_mug", [128, 4])
        p.dma("sp", muw[:, :], io["muw"], writes=[muw]); p.dma("sp", mua[:, :], io["mua"], writes=[mua])
        p.dma("sp", mug[:, :], io["mug"].rearrange("p a b -> p (a b)"), writes=[mug])
        cw_ = coef(muw, 2, 96, "rw_cw"); ca_ = coef(mua, 2, 96, "rw_ca"); cg_ = coef(mug, 4, 128, "rw_cg")
        for h in heads:
            with p.scope():
                mu = p.sb("rw_mu", [N, 6]); pr = p.sb("rw_pr", [N, 9]); w2 = p.sb("rw_w2", [96, 2, N]); a2 = p.sb("rw_a2", [96, 2, N])
                g2 = p.sb("rw_g2", [128, 2, N])
                p.dma("sp", mu[:, :], io[f"mu{h}"], writes=[mu]); p.dma("sp", pr[:, :], io[f"pr{h}"], writes=[pr])
                p.dma("sp", w2[:, :, :], io[f"w2_{h}"], writes=[w2]); p.dma("sp", a2[:, :, :], io[f"a2_{h}"], writes=[a2])
                p.dma("sp", g2[:, :, :], io[f"g2_{h}"], writes=[g2])
                cm = coef(mu, 6, N, "rw_cm")
                omka = p.sb("rw_omka", [N, 1])
                ts(p, omka[:, :], pr[:, 5:6], -1.0, ALU.mult, [pr], [omka], s2=1.0, op1=ALU.add)
                O = p.sb("rw_O", [L, nch, N]); aux0 = p.sb("rw_aux0", [N, T]); gg = p.sb("rw_gg", [N, T])
                for d in range(2):
                    mT_incl, mT_str, m_incl, m_str = dir_masks(c, d)
                    Ms = [p.sb(f"rw_M{i}", [N, N]) for i in range(2)]
                    ubs = [p.sb(f"rw_u{i}", [L, N]) for i in range(2)]
                    st = {}
                    for (lo, hi) in blocks_for(d, nc_ctx, nc_lat, bs):
                        nb = hi - lo
                        TB = nb * L
                        off = poff(lo)
                        with p.scope():
                            at = p.sb("rw_at", [N, TB]); bt = p.sb("rw_bt", [N, TB]); kt = p.sb("rw_kt", [N, TB]); rt = p.sb("rw_rt", [N, TB])
                            bL = p.sb("rw_bL", [N, TB]); kL = p.sb("rw_kL", [N, TB]); v = p.sb("rw_v", [N, TB]); WL = p.sb("rw_WL", [N, nb])
                            with p.scope():
                                raw = p.sb("rw_raw", [128, TB + 2]); raw2 = p.sb("rw_raw2", [128, TB + 2])
                                r = p.sb("rw_r", [N, TB]); k = p.sb("rw_k", [N, TB])
                                xw = p.sb("rw_xw", [96, TB]); xa = p.sb("rw_xa", [96, TB])
                                for nm, dst, ci in (("rp", r, 0), ("kp", k, 1), ("vp", v, 2)):
                                    p.dma("sp", raw[0:N, :], io[f"{nm}{h}"][:, off:off + TB + 2], writes=[raw])
                                    shift3(dst[:, :], raw, N, TB, mu[:, 2 * ci:2 * ci + 1], mu[:, 2 * ci + 1:2 * ci + 2], cm[:, ci:ci + 1], [raw, mu, cm], [dst])
                                p.dma("sp", raw[0:96, :], io["xwp"][:, off:off + TB + 2], writes=[raw])
                                shift3(xw[:, :], raw, 96, TB, muw[:, 0:1], muw[:, 1:2], cw_[:, 0:1], [raw, muw, cw_], [xw])
                                act(p, xw[:, :], xw[:, :], AF.Tanh, [xw], [xw])
                                p.dma("sp", raw2[0:96, :], io["xap"][:, off:off + TB + 2], writes=[raw2])
                                shift3(xa[:, :], raw2, 96, TB, mua[:, 0:1], mua[:, 1:2], ca_[:, 0:1], [raw2, mua, ca_], [xa])
                                logw = p.sb("rw_logw", [N, TB]); iclr = p.sb("rw_iclr", [N, TB]); kk = p.sb("rw_kk", [N, TB]); tmp = p.sb("rw_tmp", [N, TB])
                                kmod = p.sb("rw_kmod", [N, TB]); b_ = p.sb("rw_b", [N, TB])

                                def lora(dst, wt, dd, xin, bias_col):
                                    for c0 in range(0, TB, 512):
                                        c1 = min(c0 + 512, TB)
                                        ps = p.next_ps()
                                        mm(p, ps[0:N, 0:c1 - c0], wt[:, dd, :], xin[:, c0:c1], [wt, xin], [ps])
                                        act(p, dst[:, c0:c1], ps[0:N, 0:c1 - c0], AF.Sigmoid, [ps, pr], [dst], bias=pr[:, bias_col:bias_col + 1])

                                lora(logw, w2, d, xw, 0 + d)
                                ts(p, logw[:, :], logw[:, :], -0.6065306597126334, ALU.mult, [logw], [logw])
                                lora(iclr, a2, d, xa, 2 + d)
                                ts(p, kk[:, :], k[:, :], pr[:, 4:5], ALU.mult, [k, pr], [kk])
                                act(p, tmp[:, :], kk[:, :], AF.Square, [kk], [tmp])
                                for c0 in range(0, TB, 512):
                                    c1 = min(c0 + 512, TB)
                                    ps = p.next_ps()
                                    mm(p, ps[0:N, 0:c1 - c0], c["ones"][0:N, 0:N], tmp[:, c0:c1], [c["ones"], tmp], [ps])
                                    act(p, tmp[:, c0:c1], ps[0:N, 0:c1 - c0], AF.Sqrt, [ps, c["eps6"]], [tmp], bias=c["eps6"][0:N, 0:1])
                                p.op("dve", lambda e: e.reciprocal(out=tmp[:, :], in_=tmp[:, :]), reads=[tmp], writes=[tmp])
                                tt(p, kk[:, :], kk[:, :], tmp[:, :], ALU.mult, [kk, tmp], [kk])
                                ts(p, kmod[:, :], iclr[:, :], pr[:, 5:6], ALU.mult, [iclr, pr, omka], [kmod], s2=omka[:, 0:1], op1=ALU.add)
                                tt(p, kmod[:, :], kmod[:, :], k[:, :], ALU.mult, [kmod, k], [kmod])
                                tt(p, b_[:, :], kk[:, :], iclr[:, :], ALU.mult, [kk, iclr], [b_])
                                if d == 0:
                                    icl1 = p.sb("rw_icl1", [N, TB])
                                    lora(icl1, a2, 1, xa, 3)
                                    ts(p, icl1[:, :], icl1[:, :], pr[:, 5:6], ALU.mult, [icl1, pr, omka], [icl1], s2=omka[:, 0:1], op1=ALU.add)
                                    tt(p, icl1[:, :], icl1[:, :], k[:, :], ALU.mult, [icl1, k], [icl1])
                                    tt(p, icl1[:, :], icl1[:, :], kmod[:, :], ALU.add, [icl1, kmod], [icl1])
                                    stt(p, icl1[:, :], r[:, :], pr[:, 6:7], icl1[:, :], ALU.mult, ALU.mult, [r, pr, icl1], [icl1])
                                    xg = p.sb("rw_xg", [128, 2, TB])
                                    for kc in range(2):
                                        p.dma("sp", raw2[:, :], io["xgp"][:, kc, off:off + TB + 2], writes=[raw2])
                                        shift3(xg[:, kc, :], raw2, 128, TB, mug[:, 2 * kc:2 * kc + 1], mug[:, 2 * kc + 1:2 * kc + 2], cg_[:, kc:kc + 1],
                                               [raw2, mug, cg_], [xg])
                                    act(p, xg[:, :, :], xg[:, :, :], AF.Sigmoid, [xg], [xg])
                                    for c0 in range(0, TB, 512):
                                        c1 = min(c0 + 512, TB)
                                        ps = p.next_ps()
                                        mm(p, ps[0:N, 0:c1 - c0], c["ones"][0:N, 0:N], icl1[:, c0:c1], [c["ones"], icl1], [ps])
                                        tt(p, aux0[:, lo * L + c0:lo * L + c1], ps[0:N, 0:c1 - c0], v[:, c0:c1], ALU.mult, [ps, v], [aux0])
                                        ps2 = p.next_ps()
                                        for kc in range(2):
                                            mm(p, ps2[0:N, 0:c1 - c0], g2[:, kc, :], xg[:, kc, c0:c1], [g2, xg], [ps2], start=(kc == 0), stop=(kc == 1))
                                        cp(p, gg[:, lo * L + c0:lo * L + c1], ps2[0:N, 0:c1 - c0], [ps2], [gg], "act")
                                rmask = p.sb("rw_rmask", [N, nb, L]); lW = p.sb("rw_lW", [N, TB]); e1 = p.sb("rw_e1", [N, TB])
                                p.op("pool", lambda e: e.memset(rmask[:, :, :], 1.0), writes=[rmask])
                                p.op("pool", lambda e: e.memset(rmask[:, :, 0:1], 0.0), writes=[rmask])
                                p.op("dve", lambda e: e.tensor_tensor_scan(out=lW[:, :], data0=rmask[:, :, :].rearrange("p n t -> p (n t)"),
                                                                          data1=logw[:, :], initial=0.0, op0=ALU.mult, op1=ALU.add),
                                     reads=[rmask, logw], writes=[lW])
                                lW3 = lW[:, :].rearrange("p (n t) -> p n t", t=L)
                                tot = p.sb("rw_tot", [N, nb])
                                cp(p, tot[:, :], lW3[:, :, L - 1], [lW], [tot], "dve")
                                if d == 1:
                                    tt(p, lW3, bc_last(tot[:, :], L), lW3, ALU.subtract, [tot, lW], [lW])
                                    tt(p, lW[:, :], lW[:, :], logw[:, :], ALU.add, [lW, logw], [lW])
                                act(p, WL[:, :], tot[:, :], AF.Exp, [tot], [WL])
                                tt(p, e1[:, :], lW[:, :], logw[:, :], ALU.subtract, [lW, logw], [e1])
                                act(p, e1[:, :], e1[:, :], AF.Exp, [e1], [e1])
                                stt(p, at[:, :], kk[:, :], -1.0, e1[:, :], ALU.mult, ALU.mult, [kk, e1], [at])
                                act(p, e1[:, :], lW[:, :], AF.Exp, [lW], [e1], scale=-1.0)
                                tt(p, bt[:, :], b_[:, :], e1[:, :], ALU.mult, [b_, e1], [bt])
                                tt(p, kt[:, :], kmod[:, :], e1[:, :], ALU.mult, [kmod, e1], [kt])
                                act(p, e1[:, :], lW[:, :], AF.Exp, [lW], [e1])
                                tt(p, rt[:, :], r[:, :], e1[:, :], ALU.mult, [r, e1], [rt])
                                tt(p, e1[:, :].rearrange("p (n t) -> p n t", t=L), bc_last(tot[:, :], L), lW3, ALU.subtract, [tot, lW], [e1])
                                act(p, e1[:, :], e1[:, :], AF.Exp, [e1], [e1])
                                tt(p, bL[:, :], b_[:, :], e1[:, :], ALU.mult, [b_, e1], [bL])
                                tt(p, kL[:, :], kmod[:, :], e1[:, :], ALU.mult, [kmod, e1], [kL])
                            ch = lambda a, n: a[:, n * L:(n + 1) * L]
                            X = p.sb("rw_X", [L, nb, L]); Y = p.sb("rw_Y", [L, nb, L]); AakT = p.sb("rw_Aak", [L, nb, L])
                            ArbT = p.sb("rw_Arb", [L, nb, L]); ArkT = p.sb("rw_Ark", [L, nb, L])

                            def mk(dst, la, ra, mask):
                                def ev(ps, n0, g):
                                    tt(p, dst[:, n0:n0 + g, :], ps[0:L, 0:g * L].rearrange("p (g k) -> p g k", k=L), bc_mid(mask[0:L, 0:L], g), ALU.mult,
                                       [ps, mask], [dst])
                                batched_chunk_mm(p, nb, lambda n: ch(la, n), lambda n: ch(ra, n), [la, ra], L, L, ev)

                            mk(X, at, bt, m_str); mk(Y, bt, at, mT_str); mk(AakT, kt, at, mT_str); mk(ArbT, bt, rt, mT_incl); mk(ArkT, kt, rt, mT_incl)
                            at_tm = p.sb("rw_attm", [L, nb, N]); bL_tm = p.sb("rw_bLtm", [L, nb, N]); kL_tm = p.sb("rw_kLtm", [L, nb, N]); v_tm = p.sb("rw_vtm", [L, nb, N])
                            for src, dst in ((at, at_tm), (bL, bL_tm), (kL, kL_tm), (v, v_tm)):
                                transpose_chunks(p, c, (lambda s_: (lambda n: ch(s_, n)))(src), [src], N, dst, nb)
                            tmpi = {k_: p.sb("rw_ti" + k_, [L, nb, L]) for k_ in ("P", "Q", "X2", "Y2")}
                            Pm, Psubs = tri_inverse_T(p, c, X, Y, nb, tmpi)
                            cAk = p.sb("rw_cAk", [L, nb, N]); U0 = p.sb("rw_U0", [L, nb, N]); AhatT = p.sb("rw_Ah", [N, nb, L]); Y0 = p.sb("rw_Y0", [L, nb, N])

                            def evto(dst, rows, eng):
                                def ev(ps, n0, g):
                                    w_ = dst.t.shape[2]
                                    cp(p, dst[:, n0:n0 + g, :], ps[0:rows, 0:g * w_].rearrange("p (g k) -> p g k", k=w_), [ps], [dst], eng)
                                return ev

                            batched_chunk_mm(p, nb, lambda n: AakT[:, n, :], lambda n: v_tm[:, n, :], [AakT, v_tm], L, N, evto(cAk, L, "act"))
                            batched_chunk_mm(p, nb, lambda n: Pm[:, n, :], lambda n: cAk[:, n, :], Psubs + [cAk], L, N, evto(U0, L, "dve"))
                            batched_chunk_mm(p, nb, lambda n: at_tm[:, n, :], lambda n: Pm[:, n, :], Psubs + [at_tm], N, L, evto(AhatT, N, "act"))
                            batched_chunk_mm(p, nb, lambda n: ArkT[:, n, :], lambda n: v_tm[:, n, :], [ArkT, v_tm], L, N, evto(Y0, L, "dve"))
                            Yo = p.sb("rw_Yo", [L, nb, N])
                            order = list(range(nb)) if d == 0 else list(range(nb - 1, -1, -1))
                            chain = dict(dk=N, dvp=N, order=order, QT=rt, A1T=ArbT, BL=bL_tm, decay=WL, U0=U0, AhatT=AhatT, Y=Yo,
                                         K0=(kL_tm, v_tm), M=Ms, ub=ubs, R=[], st=st)
                            chunk_loops(p, [chain])
                            Ysubs = [Yo.sub(n) for n in range(nb)]
                            if d == 0:
                                tt(p, O[:, lo:hi, :], Yo[:, :, :], Y0[:, :, :], ALU.add, Ysubs + [Y0], [O])
                            else:
                                tt(p, Yo[:, :, :], Yo[:, :, :], Y0[:, :, :], ALU.add, Ysubs + [Y0], [Yo])
                                tt(p, O[:, lo:hi, :], O[:, lo:hi, :], Yo[:, :, :], ALU.add, [Yo, O], [O])
                with p.scope():
                    Of = p.sb("rw_Of", [N, nch, L]); sq = p.sb("rw_fsq", [N, T]); mean = p.sb("rw_mean", [N, T])
                    g5 = p.sb("rw_g5", [N, 1])
                    p.op("dve", lambda e: e.memset(g5[:, :], 64e-5), writes=[g5])
                    transpose_chunks(p, c, lambda n: O[:, n, :], [O], L, Of, nch)
                    Off = Of[:, :, :].rearrange("p n t -> p (n t)")
                    for c0 in range(0, T, 512):
                        c1 = min(c0 + 512, T)
                        ps = p.next_ps()
                        mm(p, ps[0:N, 0:c1 - c0], c["ones"][0:N, 0:N], Off[:, c0:c1], [c["ones"], Of], [ps])
                        stt(p, mean[:, c0:c1], ps[0:N, 0:c1 - c0], -1.0 / N, Off[:, c0:c1], ALU.mult, ALU.add, [ps, Of], [mean])
                    act(p, sq[:, :], mean[:, :], AF.Square, [mean], [sq])
                    for c0 in range(0, T, 512):
                        c1 = min(c0 + 512, T)
                        ps = p.next_ps()
                        mm(p, ps[0:N, 0:c1 - c0], c["ones"][0:N, 0:N], sq[:, c0:c1], [c["ones"], sq], [ps])
                        act(p, sq[:, c0:c1], ps[0:N, 0:c1 - c0], AF.Sqrt, [ps, g5], [sq], scale=1.0 / N, bias=g5[:, 0:1])
                    p.op("dve", lambda e: e.reciprocal(out=sq[:, :], in_=sq[:, :]), reads=[sq], writes=[sq])
                    tt(p, mean[:, :], mean[:, :], sq[:, :], ALU.mult, [mean, sq], [mean])
                    ts(p, mean[:, :], mean[:, :], pr[:, 7:8], ALU.mult, [mean, pr], [mean], s2=pr[:, 8:9], op1=ALU.add)
                    tt(p, mean[:, :], mean[:, :], aux0[:, :], ALU.add, [mean, aux0], [mean])
                    tt(p, mean[:, :], mean[:, :], gg[:, :], ALU.mult, [mean, gg], [mean])
                    p.dma("pool", io[f"yd{h}"], mean[:, :], reads=[mean], is_output=True)


D = 2048
KC = 16
DFF = 5632
FC = DFF // 128
TL = 1024
TCX = 64
TT = TL + TCX
SEGS = [(0, TL, 0), (TL, TT, 1)]


def ttiles(T, w=512):
    return [(t0, min(t0 + w, T)) for t0 in range(0, T, w)]


def load_fm(p, q, dst, src_ap, kc_n):
    v = src_ap.rearrange("(kc p) t -> p kc t", p=128)
    for kc in range(kc_n):
        p.dma(q, dst[:, kc, :], v[:, kc, :], writes=[dst.sub(kc)])


def rms_stats(p, c, x, xsub, T, sq, rstd, eps_ap, kc_n=KC, dim=D):
    for kc in range(kc_n):
        act(p, sq[:, kc, :], x[:, kc, :], AF.Square, [xsub(kc)], [sq.sub(kc)])
    for ti, (t0, t1) in enumerate(ttiles(T)):
        ps = p.next_ps()
        for kc in range(kc_n):
            mm(p, ps[:, 0:t1 - t0], c["ones"][:, :], sq[:, kc, t0:t1], [c["ones"], sq.sub(kc)], [ps], start=(kc == 0), stop=(kc == kc_n - 1))
        act(p, rstd[:, t0:t1], ps[:, 0:t1 - t0], AF.Sqrt, [ps], [rstd], scale=1.0 / dim, bias=eps_ap)
    p.op("dve", lambda e: e.reciprocal(out=rstd[:, :], in_=rstd[:, :]), reads=[rstd], writes=[rstd])


def mod_prep(p, mv, gs, sh, gi, sci, shi):
    for j in range(2):
        ts(p, gs[:, :, j:j + 1], mv[:, :, sci[j]:sci[j] + 1], 1.0, ALU.add, [mv], [gs])
        tt(p, gs[:, :, j:j + 1], gs[:, :, j:j + 1], mv[:, :, gi:gi + 1], ALU.mult, [mv, gs], [gs])
        cp(p, sh[:, :, j:j + 1], mv[:, :, shi[j]:shi[j] + 1], [mv], [sh], "dve")


def norm_mod(p, c, x, xsub, h, T, segs, gs, sh, rstd):
    for kc in range(KC):
        tt(p, h[:, kc, :], x[:, kc, :], rstd[:, 0:T], ALU.mult, [xsub(kc), rstd], [h.sub(kc)])
        for (t0, t1, col) in segs:
            act(p, h[:, kc, t0:t1], h[:, kc, t0:t1], AF.Identity, [h.sub(kc), gs, sh], [h.sub(kc)],
                scale=gs[:, kc, col:col + 1], bias=sh[:, kc, col:col + 1])


def gemm_fm(p, h, hsub, kc_n, tts, W_ap, N, wbufs, evac, q="sp", NB=256, wkey=[0], kc_off=0):
    Wv = W_ap.rearrange("(kc p) n -> p kc n", p=128)
    nblk = (N + NB - 1) // NB
    for nb in range(nblk):
        n0 = nb * NB
        nsz = min(NB, N - n0)
        wb = wbufs[wkey[0] % len(wbufs)]
        wkey[0] += 1
        half = kc_n // 2
        p.dma(q, wb[:, 0:half, 0:nsz], Wv[:, 0:half, n0:n0 + nsz], writes=[wb.sub(0)])
        p.dma(q, wb[:, half:kc_n, 0:nsz], Wv[:, half:kc_n, n0:n0 + nsz], writes=[wb.sub(1)])
        for mo in range(0, nsz, 128):
            msz = min(128, nsz - mo)
            for (t0, t1) in tts:
                ps = p.next_ps()
                for kc in range(kc_n):
                    mm(p, ps[0:msz, 0:t1 - t0], wb[:, kc, mo:mo + msz], h[:, kc_off + kc, t0:t1],
                       [wb.sub(0 if kc < half else 1), hsub(kc_off + kc)], [ps], start=(kc == 0), stop=(kc == kc_n - 1))
                evac(n0 + mo, msz, t0, t1, ps)


def build_mod(NCOL=6144):
    p = P()
    p.init_ps(8)
    c = make_consts(p)
    cT = p.dram("cT", [D, 3], "ExternalInput"); W = p.dram("W", [D, NCOL], "ExternalInput"); b = p.dram("b", [128, NCOL // 128], "ExternalInput")
    out = p.dram("out", [128, NCOL // 128, 3], "ExternalOutput")
    cs = p.sb("cs", [128, KC, 3]); bs = p.sb("bs", [128, NCOL // 128]); o = p.sb("o", [128, NCOL // 128, 3])
    wbufs = [p.sb(f"wb{i}", [128, KC, 256]) for i in range(3)]
    p.dma("sp", cs[:, :, :], cT.rearrange("(kc p) t -> p kc t", p=128), writes=[cs])
    p.dma("sp", bs[:, :], b, writes=[bs])
    sg = p.sb("sg", [128, KC, 3])
    act(p, sg[:, :, :], cs[:, :, :], AF.Sigmoid, [cs], [sg])
    tt(p, cs[:, :, :], cs[:, :, :], sg[:, :, :], ALU.mult, [cs, sg], [cs])

    def evac(m0, msz, t0, t1, ps):
        j = m0 // 128
        ts(p, o[:, j, :], ps[:, 0:3], bs[:, j:j + 1], ALU.add, [ps, bs], [o])

    gemm_fm(p, cs, lambda kc: cs, KC, [(0, 3)], W, NCOL, wbufs, evac)
    p.dma("pool", out, o[:, :, :], reads=[o], is_output=True)
    p.finish("sp"); p.close()
    return p


def build_pre(N=15328):
    p = P()
    p.init_ps(8)
    c = make_consts(p)
    xT = p.dram("xT", [D, TT], "ExternalInput"); W = p.dram("W", [D, N], "ExternalInput")
    mvd = p.dram("mv", [128, KC, 5], "ExternalInput")
    zT = p.dram("zT", [N, TT], "ExternalOutput")
    x = p.sb("x", [128, KC, TT]); h = p.sb("h", [128, KC, TT]); mv = p.sb("mvs", [128, KC, 5])
    gs = p.sb("gs", [128, KC, 2]); sh = p.sb("sh", [128, KC, 2]); rstd = p.sb("rstd", [128, TT])
    wbufs = [p.sb(f"wb{i}", [128, KC, 256]) for i in range(2)]
    obufs = [p.sb(f"ob{i}", [128, TT]) for i in range(2)]
    load_fm(p, "sp", x, xT, KC)
    p.dma("sp", mv[:, :, :], mvd, writes=[mv])
    mod_prep(p, mv, gs, sh, 0, (1, 3), (2, 4))
    rms_stats(p, c, x, x.sub, TT, h, rstd, c["eps6"][:, 0:1])
    norm_mod(p, c, x, x.sub, h, TT, SEGS, gs, sh, rstd)
    oi = [0]
    tts = ttiles(TT)

    def evac(m0, msz, t0, t1, ps):
        ob = obufs[oi[0] % 2]
        cp(p, ob[0:msz, t0:t1], ps[0:msz, 0:t1 - t0], [ps], [ob.sub(t0)], "dve" if (t0 // 512) % 2 == 0 else "act")
        if t1 == TT:
            p.dma("pool", zT[m0:m0 + msz, :], ob[0:msz, :], reads=[ob.sub(t[0]) for t in tts], is_output=True)
            oi[0] += 1

    gemm_fm(p, h, h.sub, KC, tts, W, N, wbufs, evac)
    p.finish("sp"); p.close()
    return p


def build_post():
    p = P()
    p.init_ps(8)
    c = make_consts(p)
    PADC = 15
    LA = TL + 2 * PADC; LC = TCX + 2 * PADC
    ap_d = p.dram("a_p", [512, LA + LC], "ExternalInput"); gp_d = p.dram("g_p", [512, LA + LC], "ExternalInput")
    cvw = p.dram("cvw", [128, 4, 34], "ExternalInput")
    ybcd = p.dram("ybcd", [1536, TT], "ExternalInput")
    gates = p.dram("gates", [8192, TT], "ExternalInput")
    xT = p.dram("xT", [D, TT], "ExternalInput")
    wbr = p.dram("wbr", [4, 512, D], "ExternalInput"); wout = p.dram("wout", [D, D], "ExternalInput")
    mvd = p.dram("mv", [128, KC, 3], "ExternalInput")
    x1T = p.dram("x1T", [D, TT], "ExternalOutput")
    tts = ttiles(TT)
    mv = p.sb("mvs", [128, KC, 3]); gg = p.sb("gg", [128, KC, 2])
    p.dma("sp", mv[:, :, :], mvd, writes=[mv])
    for j in range(2):
        tt(p, gg[:, :, j:j + 1], mv[:, :, 0:1], mv[:, :, 1 + j:2 + j], ALU.mult, [mv], [gg])
    ya = p.sb("ya", [128, 4, TT])
    with p.scope():
        cw = p.sb("cw", [128, 4, 34])
        p.dma("sp", cw[:, :, :], cvw, writes=[cw])
        a = p.sb("cv_a", [128, 4, LA + LC]); g = p.sb("cv_g", [128, 4, LA + LC])
        p.dma("sp", a[:, :, :], ap_d.rearrange("(kc p) t -> p kc t", p=128), writes=[a])
        p.dma("sp", g[:, :, :], gp_d.rearrange("(kc p) t -> p kc t", p=128), writes=[g])
        act(p, g[:, :, :], g[:, :, :], AF.Sigmoid, [g], [g])
        tt(p, a[:, :, :], a[:, :, :], g[:, :, :], ALU.mult, [a, g], [a])
        for kc in range(4):
            for (o_in, o_out, tl) in ((0, 0, TL), (LA, TL, TCX)):
                dst = ya[:, kc, o_out:o_out + tl]
                ts(p, dst, a[:, kc, o_in:o_in + tl], cw[:, kc, 0:1], ALU.mult, [a, cw], [ya.sub(kc)], s2=cw[:, kc, 31:32], op1=ALU.add)
                for j in range(1, 31):
                    stt(p, dst, a[:, kc, o_in + j:o_in + j + tl], cw[:, kc, j:j + 1], dst, ALU.mult, ALU.add, [a, cw, ya.sub(kc)], [ya.sub(kc)])
        xc = p.sb("cv_xc", [128, 4, TT]); sq = g; rs = p.sb("cv_rs", [128, TT])
        for (t0, t1) in tts:
            ps = p.next_ps()
            for kc in range(4):
                mm(p, ps[:, 0:t1 - t0], c["ones"][:, :], ya[:, kc, t0:t1], [c["ones"], ya.sub(kc)], [ps], start=(kc == 0), stop=(kc == 3))
            for kc in range(4):
                stt(p, xc[:, kc, t0:t1], ps[:, 0:t1 - t0], -1.0 / 512, ya[:, kc, t0:t1], ALU.mult, ALU.add, [ps, ya.sub(kc)], [xc.sub(kc)])
        for kc in range(4):
            act(p, sq[:, kc, 0:TT], xc[:, kc, :], AF.Square, [xc.sub(kc)], [sq])
        for (t0, t1) in tts:
            ps = p.next_ps()
            for kc in range(4):
                mm(p, ps[:, 0:t1 - t0], c["ones"][:, :], sq[:, kc, t0:t1], [c["ones"], sq], [ps], start=(kc == 0), stop=(kc == 3))
            act(p, rs[:, t0:t1], ps[:, 0:t1 - t0], AF.Sqrt, [ps], [rs], scale=1.0 / 512, bias=c["eps6"][:, 0:1])
        p.op("dve", lambda e: e.reciprocal(out=rs[:, :], in_=rs[:, :]), reads=[rs], writes=[rs])
        for kc in range(4):
            tt(p, xc[:, kc, :], xc[:, kc, :], rs[:, :], ALU.mult, [xc.sub(kc), rs], [xc.sub(kc)])
            act(p, xc[:, kc, :], xc[:, kc, :], AF.Identity, [xc.sub(kc), cw], [xc.sub(kc)], scale=cw[:, kc, 32:33], bias=cw[:, kc, 33:34])
            act(p, ya[:, kc, :], xc[:, kc, :], AF.Silu, [xc.sub(kc)], [ya.sub(kc)])
    merged = p.sb("merged", [128, KC, TT])
    with p.scope():
        Yb = p.sb("Ybcd", [128, 12, TT])
        yv = ybcd.rearrange("(kc p) t -> p kc t", p=128)
        for kc in range(12):
            p.dma("sp", Yb[:, kc, :], yv[:, kc, :], writes=[Yb.sub(kc)])
        wbufs = [p.sb(f"wb{i}", [128, 4, 256]) for i in range(3)]
        gbufs = [p.sb(f"gb{i}", [128, TT]) for i in range(3)]
        gv = gates.rearrange("(n kc p) t -> n p kc t", p=128, kc=KC)
        gi = [0]
        for n in range(4):
            def evac(m0, msz, t0, t1, ps, n=n):
                kc = m0 // 128
                gb = gbufs[gi[0] % 3]
                if t0 == 0:
                    p.dma("pool", gb[:, :], gv[n, :, kc, :], writes=[gb])
                    act(p, gb[:, :], gb[:, :], AF.Sigmoid, [gb], [gb])
                if n == 0:
                    tt(p, merged[:, kc, t0:t1], ps[:, 0:t1 - t0], gb[:, t0:t1], ALU.mult, [ps, gb], [merged.sub(kc)])
                else:
                    tt(p, gb[:, t0:t1], ps[:, 0:t1 - t0], gb[:, t0:t1], ALU.mult, [ps, gb], [gb])
                    tt(p, merged[:, kc, t0:t1], merged[:, kc, t0:t1], gb[:, t0:t1], ALU.add, [gb, merged.sub(kc)], [merged.sub(kc)], eng="pool")
                if t1 == TT:
                    gi[0] += 1

            if n == 0:
                gemm_fm(p, ya, ya.sub, 4, tts, wbr[n], D, wbufs, evac)
            else:
                gemm_fm(p, Yb, Yb.sub, 4, tts, wbr[n], D, wbufs, evac, kc_off=4 * (n - 1))
    with p.scope():
        mx = p.sb("mx", [128, KC, TT]); rstd = p.sb("rstd", [128, TT])
        wbufs = [p.sb(f"wo{i}", [128, KC, 128]) for i in range(2)]
        xbufs = [p.sb(f"xb{i}", [128, TT]) for i in range(2)]

        def evac2(m0, msz, t0, t1, ps):
            kc = m0 // 128
            cp(p, mx[:, kc, t0:t1], ps[:, 0:t1 - t0], [ps], [mx.sub(kc)], "act" if (t0 // 512) % 2 else "dve")

        gemm_fm(p, merged, merged.sub, KC, tts, wout, D, wbufs, evac2, NB=128)
        rms_stats(p, c, mx, mx.sub, TT, merged, rstd, c["eps6"][:, 0:1])
        xv = xT.rearrange("(kc p) t -> p kc t", p=128)
        ov = x1T.rearrange("(kc p) t -> p kc t", p=128)
        for kc in range(KC):
            xb = xbufs[kc % 2]
            p.dma("sp", xb[:, :], xv[:, kc, :], writes=[xb])
            tt(p, mx[:, kc, :], mx[:, kc, :], rstd[:, :], ALU.mult, [mx.sub(kc), rstd], [mx.sub(kc)])
            for (t0, t1, col) in SEGS:
                stt(p, xb[:, t0:t1], mx[:, kc, t0:t1], gg[:, kc, col:col + 1], xb[:, t0:t1], ALU.mult, ALU.add, [mx.sub(kc), gg, xb], [xb])
            p.dma("pool", ov[:, kc, :], xb[:, :], reads=[xb], is_output=True)
    p.finish("sp"); p.close()
    return p


TE_L = TL + 128
TE_C = TCX + 2
TE = TE_L + TE_C
ESEGS = [(0, TE_L, 0), (TE_L, TE, 1)]


def build_ffn():
    p = P()
    p.init_ps(8)
    c = make_consts(p)
    xe = p.dram("xe", [D, TE], "ExternalInput")
    mvd = p.dram("mv", [128, KC, 8], "ExternalInput")
    hmk = p.dram("hmask", [128, 4], "ExternalInput")
    wup = p.dram("wup", [D, 2 * DFF], "ExternalInput"); dwd = p.dram("dw", [128, FC, 9], "ExternalInput"); wdn = p.dram("wdn", [DFF, D], "ExternalInput")
    x2T = p.dram("x2T", [D, TT], "ExternalOutput")
    hm = p.dram("hm_scratch", [DFF, TT], "Internal")
    hmbufs = [Buf(f"hm{i}") for i in range(FC)]
    mv = p.sb("mvs", [128, KC, 8]); gs = p.sb("gs", [128, KC, 2]); sh = p.sb("sh", [128, KC, 2]); gg = p.sb("gg", [128, KC, 2])
    hmask = p.sb("hmask", [128, 4]); dw = p.sb("dw", [128, FC, 9])
    p.dma("sp", mv[:, :, :], mvd, writes=[mv]); p.dma("sp", hmask[:, :], hmk, writes=[hmask]); p.dma("sp", dw[:, :, :], dwd, writes=[dw])
    mod_prep(p, mv, gs, sh, 0, (1, 3), (2, 4))
    for j in range(2):
        tt(p, gg[:, :, j:j + 1], mv[:, :, 5:6], mv[:, :, 6 + j:7 + j], ALU.mult, [mv], [gg])
    GC = 1.5957691216057308
    with p.scope():
        h = p.sb("h", [128, KC, TE])
        with p.scope():
            x = p.sb("x", [128, KC, TE]); rstd = p.sb("rstd", [128, TE])
            load_fm(p, "sp", x, xe, KC)
            rms_stats(p, c, x, x.sub, TE, h, rstd, c["eps6"][:, 0:1])
            norm_mod(p, c, x, x.sub, h, TE, ESEGS, gs, sh, rstd)
        wbufs = [p.sb(f"wu{i}", [128, KC, 128]) for i in range(4)]
        upads = [p.sb(f"up{i}", [128, 18, 66]) for i in range(2)]
        ucs = [p.sb(f"uc{i}", [128, TE_C]) for i in range(2)]
        vbs = [p.sb(f"vb{i}", [128, TT]) for i in range(2)]
        accs = [p.sb(f"acc{i}", [128, TT]) for i in range(2)]
        tmps = [p.sb(f"tmp{i}", [128, TT]) for i in range(2)]
        for ub in upads:
            p.op("pool", lambda e: e.memset(ub[:, :, :], 0.0), writes=[ub])
        u_tts = [(0, 512), (512, 1024), (1024, 1152), (TE_L, TE)]
        v_tts = [(64, 576), (576, 1088), (TE_L + 1, TE_L + 1 + TCX)]
        for cc in range(FC):
            ub = upads[cc % 2]; uc = ucs[cc % 2]; vb = vbs[cc % 2]; acc = accs[cc % 2]; tmp = tmps[cc % 2]

            def evac_u(m0, msz, t0, t1, ps):
                if t0 < TE_L:
                    r0, r1 = t0 // 64, t1 // 64
                    cp(p, ub[:, r0:r1, 1:65], ps[:, 0:t1 - t0].rearrange("p (r w) -> p r w", w=64), [ps], [ub], "act")
                else:
                    cp(p, uc[:, :], ps[:, 0:TE_C], [ps], [uc], "act")

            def evac_v(m0, msz, t0, t1, ps):
                o0 = t0 - 64 if t0 < TE_L else TL
                cp(p, vb[:, o0:o0 + (t1 - t0)], ps[:, 0:t1 - t0], [ps], [vb], "dve")

            gemm_fm(p, h, h.sub, KC, u_tts, wup[:, cc * 128:(cc + 1) * 128], 128, wbufs, evac_u, NB=128)
            gemm_fm(p, h, h.sub, KC, v_tts, wup[:, DFF + cc * 128:DFF + (cc + 1) * 128], 128, wbufs, evac_v, NB=128)
            ts(p, ub[:, 0, :], ub[:, 0, :], hmask[:, 0:1], ALU.mult, [ub, hmask], [ub])
            ts(p, ub[:, 17, :], ub[:, 17, :], hmask[:, 1:2], ALU.mult, [ub, hmask], [ub])
            ts(p, uc[:, 0:1], uc[:, 0:1], hmask[:, 2:3], ALU.mult, [uc, hmask], [uc])
            ts(p, uc[:, TE_C - 1:TE_C], uc[:, TE_C - 1:TE_C], hmask[:, 3:4], ALU.mult, [uc, hmask], [uc])
            a3 = acc[:, 0:TL].rearrange("p (r w) -> p r w", w=64)
            first = True
            for i in range(3):
                for j in range(3):
                    src = ub[:, i:i + 16, j:j + 64]
                    if first:
                        ts(p, a3, src, dw[:, cc, 0:1], ALU.mult, [ub, dw], [acc])
                        first = False
                    else:
                        stt(p, a3, src, dw[:, cc, i * 3 + j:i * 3 + j + 1], a3, ALU.mult, ALU.add, [ub, dw, acc], [acc])
            ac = acc[:, TL:TT]
            ts(p, ac, uc[:, 0:TCX], dw[:, cc, 3:4], ALU.mult, [uc, dw], [acc])
            for j in range(1, 3):
                stt(p, ac, uc[:, j:j + TCX], dw[:, cc, 3 + j:4 + j], ac, ALU.mult, ALU.add, [uc, dw, acc], [acc])
            tt(p, tmp[:, :], acc[:, :], acc[:, :], ALU.mult, [acc], [tmp], eng="pool")
            ts(p, tmp[:, :], tmp[:, :], 0.044715, ALU.mult, [tmp], [tmp], s2=1.0, op1=ALU.add, eng="pool")
            tt(p, tmp[:, :], tmp[:, :], acc[:, :], ALU.mult, [tmp, acc], [tmp], eng="pool")
            act(p, tmp[:, :], tmp[:, :], AF.Sigmoid, [tmp], [tmp], scale=GC)
            tt(p, tmp[:, :], tmp[:, :], acc[:, :], ALU.mult, [tmp, acc], [tmp], eng="pool")
            tt(p, tmp[:, :], tmp[:, :], vb[:, :], ALU.mult, [tmp, vb], [tmp], eng="pool")
            p.dma("pool", hm[cc * 128:(cc + 1) * 128, :], tmp[:, :], reads=[tmp], writes=[hmbufs[cc]])
    with p.scope():
        hsb = p.sb("hsb", [128, FC, 512]); fx = p.sb("fx", [128, KC, 512]); sq = p.sb("sqf", [128, KC, 512]); rstd = p.sb("rstd2", [128, 512])
        wds = [p.sb(f"wd{i}", [128, FC, 128]) for i in range(2)]
        xbufs = [p.sb(f"xb{i}", [128, 512]) for i in range(2)]
        hv = hm.rearrange("(kc p) t -> p kc t", p=128)
        wv = wdn.rearrange("(kc p) n -> p kc n", p=128)
        xv = xe.rearrange("(kc p) t -> p kc t", p=128)
        ov = x2T.rearrange("(kc p) t -> p kc t", p=128)
        wi = 0
        for (t0, t1) in ttiles(TT):
            tw = t1 - t0
            for q4 in range(4):
                p.dma("sp", hsb[:, q4 * 11:(q4 + 1) * 11, 0:tw], hv[:, q4 * 11:(q4 + 1) * 11, t0:t1], reads=hmbufs[q4 * 11:(q4 + 1) * 11], writes=[hsb.sub(q4)])
            for m in range(KC):
                wd = wds[wi % 2]; wi += 1
                p.dma("sp", wd[:, 0:22, :], wv[:, 0:22, m * 128:(m + 1) * 128], writes=[wd.sub(0)])
                p.dma("sp", wd[:, 22:44, :], wv[:, 22:44, m * 128:(m + 1) * 128], writes=[wd.sub(1)])
                ps = p.next_ps()
                for kc in range(FC):
                    mm(p, ps[:, 0:tw], wd[:, kc, :], hsb[:, kc, 0:tw], [wd.sub(kc // 22), hsb.sub(kc // 11)], [ps], start=(kc == 0), stop=(kc == FC - 1))
                cp(p, fx[:, m, 0:tw], ps[:, 0:tw], [ps], [fx.sub(m)], "act" if m % 2 else "dve")
            for kc in range(KC):
                act(p, sq[:, kc, 0:tw], fx[:, kc, 0:tw], AF.Square, [fx.sub(kc)], [sq])
            ps = p.next_ps()
            for kc in range(KC):
                mm(p, ps[:, 0:tw], c["ones"][:, :], sq[:, kc, 0:tw], [c["ones"], sq], [ps], start=(kc == 0), stop=(kc == KC - 1))
            act(p, rstd[:, 0:tw], ps[:, 0:tw], AF.Sqrt, [ps], [rstd], scale=1.0 / D, bias=c["eps6"][:, 0:1])
            p.op("dve", lambda e: e.reciprocal(out=rstd[:, 0:tw], in_=rstd[:, 0:tw]), reads=[rstd], writes=[rstd])
            e0 = t0 + 64 if t0 < TL else TE_L + 1
            col = 0 if t0 < TL else 1
            for kc in range(KC):
                xb = xbufs[kc % 2]
                p.dma("sp", xb[:, 0:tw], xv[:, kc, e0:e0 + tw], writes=[xb])
                tt(p, fx[:, kc, 0:tw], fx[:, kc, 0:tw], rstd[:, 0:tw], ALU.mult, [fx.sub(kc), rstd], [fx.sub(kc)])
                stt(p, xb[:, 0:tw], fx[:, kc, 0:tw], gg[:, kc, col:col + 1], xb[:, 0:tw], ALU.mult, ALU.add, [fx.sub(kc), gg, xb], [xb])
                p.dma("pool", ov[:, kc, t0:t1], xb[:, 0:tw], reads=[xb], is_output=True)
    p.finish("sp"); p.close()
    return p


from concourse.bass_utils import run_bass_kernel_spmd

NC_CTX, NC_LAT = 4, 64
NCH = NC_CTX + NC_LAT
TSEQ = NCH * 64
_PROGS = {}


def _prog(name):
    if name not in _PROGS:
        if name == "mod":
            _PROGS[name] = build_mod()
        elif name == "pre":
            _PROGS[name] = build_pre()
        elif name == "mix":
            _PROGS[name] = build_mix()
        elif name == "post":
            _PROGS[name] = build_post()
        elif name == "ffn":
            _PROGS[name] = build_ffn()
    return _PROGS[name]


def build_mix():
    p = P()
    p.init_ps(8)
    c = make_consts(p)
    io = {}
    T = TSEQ
    shapes = {"ml_qT": [128, T], "ml_kT": [128, T], "ml_k_tm": [64, NCH, 128], "ml_v_tm": [64, NCH, 128], "ml_o_tm": [64, NCH, 128],
              "ml_gi0": [64, NCH], "ml_gf0": [64, NCH], "ml_gi1": [64, NCH], "ml_gf1": [64, NCH], "ml_gb": [64, 4], "ml_ng": [64, 128],
              "gd_qp": [128, T + 8], "gd_kp": [128, T + 8], "gd_vp": [128, T + 8], "gd_cw": [128, 15],
              "gd_a0": [64, NCH], "gd_b0": [64, NCH], "gd_a1": [64, NCH], "gd_b1": [64, NCH], "gd_gp": [64, 4], "gd_ng": [64, 128],
              "gd_z_tm": [64, NCH, 128],
              "rw_xwp": [96, T + 4], "rw_xap": [96, T + 4], "rw_xgp": [128, 2, T + 4], "rw_muw": [96, 2], "rw_mua": [96, 2], "rw_mug": [128, 2, 2]}
    for h in range(2):
        shapes.update({f"rw_rp{h}": [64, T + 4], f"rw_kp{h}": [64, T + 4], f"rw_vp{h}": [64, T + 4], f"rw_mu{h}": [64, 6],
                       f"rw_w2_{h}": [96, 2, 64], f"rw_a2_{h}": [96, 2, 64], f"rw_g2_{h}": [128, 2, 64], f"rw_pr{h}": [64, 9]})
    for k_, s_ in shapes.items():
        io[k_] = p.dram(k_, s_, "ExternalInput")
    io["ml_yb"] = p.dram("ml_yb", [64, NCH, 128], "ExternalOutput")
    io["gd_yc"] = p.dram("gd_yc", [64, NCH, 128], "ExternalOutput")
    io["rw_yd0"] = p.dram("rw_yd0", [64, T], "ExternalOutput")
    io["rw_yd1"] = p.dram("rw_yd1", [64, T], "ExternalOutput")
    sub = lambda pre: {k_[len(pre):]: v_ for k_, v_ in io.items() if k_.startswith(pre)}
    emit_mlstm(p, c, sub("ml_"), NC_CTX, NC_LAT)
    emit_gdn(p, c, sub("gd_"), NC_CTX, NC_LAT, 8)
    emit_rwkv(p, c, sub("rw_"), NC_CTX, NC_LAT, 16)
    p.finish("sp"); p.close()
    p.in_names = list(shapes.keys())
    return p


def _run(name, in_maps):
    import time as _t
    t0 = _t.time()
    p = _prog(name)
    t1 = _t.time()
    res = run_bass_kernel_spmd(p.nc, in_maps, core_ids=list(range(8)))
    print(f"[kernel] launch {name}: build {t1 - t0:.1f}s run {_t.time() - t1:.1f}s", flush=True)
    return res.results


def _fm(v):
    return np.ascontiguousarray(np.asarray(v, np.float32).reshape(-1, 128).T)


def _tm(a):
    return np.ascontiguousarray(a.reshape(-1, 64, a.shape[-1]).transpose(1, 0, 2))


def _col(a):
    return np.ascontiguousarray(a.reshape(-1, 64).T)


def _rep(v, n):
    v = np.asarray(v, np.float32)
    return np.ascontiguousarray(np.broadcast_to(v, (n,) + v.shape))


def _padseg(zf, rows, pad):
    a = zf[rows]
    return np.ascontiguousarray(np.concatenate([np.pad(a[:, 0:256], ((0, 0), (pad, pad))), np.pad(a[:, 256:], ((0, 0), (pad, pad)))], axis=1))


def kernel(x, c, ctx, c_ctx, w_mod, b_mod, norm_g, w_in, conv_dw, conv_b, conv_ln_g, conv_ln_b,
           ml_gate_b, ml_norm_g, gd_conv, gd_a_log, gd_dt_bias, gd_norm_g,
           rw_mu, rw_w0, rw_w2, rw_a0, rw_a2, rw_g2, rw_kk, rw_ka, rw_rk, rw_ln_g, rw_ln_b,
           w_br, w_out, ffn_up, ffn_dw, ffn_down):
    f32 = lambda a: np.asarray(a, np.float32)
    x = f32(x).copy(); cx = f32(ctx).copy()
    (c, c_ctx, w_mod, b_mod, norm_g, w_in, conv_dw, conv_b, conv_ln_g, conv_ln_b, ml_gate_b, ml_norm_g, gd_conv, gd_a_log, gd_dt_bias,
     gd_norm_g, rw_mu, rw_w0, rw_w2, rw_a0, rw_a2, rw_g2, rw_kk, rw_ka, rw_rk, rw_ln_g, rw_ln_b, w_br, w_out, ffn_up, ffn_dw, ffn_down) = [
        f32(a) for a in (c, c_ctx, w_mod, b_mod, norm_g, w_in, conv_dw, conv_b, conv_ln_g, conv_ln_b, ml_gate_b, ml_norm_g, gd_conv, gd_a_log,
                         gd_dt_bias, gd_norm_g, rw_mu, rw_w0, rw_w2, rw_a0, rw_a2, rw_g2, rw_kk, rw_ka, rw_rk, rw_ln_g, rw_ln_b, w_br, w_out,
                         ffn_up, ffn_dw, ffn_down)]
    DEPTH = w_in.shape[0]
    cT = np.ascontiguousarray(np.stack([c[0], c[1], c_ctx], axis=1))
    ims = []
    for j in range(8):
        l, hh = j // 2, j % 2
        cols = slice(hh * 6144, (hh + 1) * 6144)
        ims.append({"cT": cT, "W": np.ascontiguousarray(w_mod[l][:, cols]), "b": _fm(b_mod[l][cols])})
    r = _run("mod", ims)
    mod = np.zeros((DEPTH, 12288, 3), np.float32)
    for j in range(8):
        l, hh = j // 2, j % 2
        o = r[j]["out"]
        mod[l, hh * 6144:(hh + 1) * 6144] = o.transpose(1, 0, 2).reshape(6144, 3)
    for l in range(DEPTH):
        sh1, sc1, gt1, sh2, sc2, gt2 = [mod[l, i * 2048:(i + 1) * 2048] for i in range(6)]
        xTs = []
        ims = []
        for j in range(8):
            b, g = j // 4, j % 4
            xT = np.ascontiguousarray(np.concatenate([x[b, g * 1024:(g + 1) * 1024], cx[b, g * 64:(g + 1) * 64]], axis=0).T)
            xTs.append(xT)
            mv = np.ascontiguousarray(np.stack([_fm(norm_g[l, 0]), _fm(sc1[:, b]), _fm(sh1[:, b]), _fm(sc1[:, 2]), _fm(sh1[:, 2])], axis=2))
            ims.append({"xT": xT, "W": w_in[l], "mv": mv})
        r = _run("pre", ims)
        zTs = [r[j]["zT"] for j in range(8)]
        zf = []
        for b in range(2):
            zf.append(np.concatenate([np.concatenate([zTs[4 * b + g][:, 1024:1088] for g in range(4)], axis=1),
                                      np.concatenate([zTs[4 * b + g][:, 0:1024] for g in range(4)], axis=1)], axis=1))
        ims = []
        for j in range(8):
            b, g = j // 4, j % 4
            z = zf[b]
            hs = slice(g * 128, (g + 1) * 128)
            im = {}
            q = z[1024:1536][hs]; k = z[1536:2048][hs]; v = z[2048:2560][hs]; o = z[2560:3072][hs]
            im["ml_qT"] = np.ascontiguousarray(q); im["ml_kT"] = np.ascontiguousarray(k)
            im["ml_k_tm"] = _tm(k.T); im["ml_v_tm"] = _tm(v.T); im["ml_o_tm"] = _tm(o.T)
            for d in range(2):
                im[f"ml_gi{d}"] = _col(z[3072 + d * 8 + g]); im[f"ml_gf{d}"] = _col(z[3072 + d * 8 + 4 + g])
            im["ml_gb"] = _rep(np.array([ml_gate_b[l, 0, 0, g], ml_gate_b[l, 0, 1, g], ml_gate_b[l, 1, 0, g], ml_gate_b[l, 1, 1, g]], np.float32), 64)
            im["ml_ng"] = _rep(ml_norm_g[l, hs], 64)
            for nm, c0 in (("q", 3088), ("k", 3600), ("v", 4112)):
                im[f"gd_{nm}p"] = _padseg(z, np.arange(c0 + g * 128, c0 + (g + 1) * 128), 2)
            im["gd_cw"] = np.ascontiguousarray(np.concatenate([gd_conv[l][:, c0 + g * 128:c0 + (g + 1) * 128].T for c0 in (0, 512, 1024)], axis=1))
            for d in range(2):
                im[f"gd_a{d}"] = _col(z[5136 + d * 8 + g]); im[f"gd_b{d}"] = _col(z[5136 + d * 8 + 4 + g])
            im["gd_gp"] = _rep(np.array([gd_a_log[l, 0, g], gd_dt_bias[l, 0, g], gd_a_log[l, 1, g], gd_dt_bias[l, 1, g]], np.float32), 64)
            im["gd_ng"] = _rep(gd_norm_g[l, hs], 64)
            im["gd_z_tm"] = _tm(z[4624:5136][hs].T)
            R0 = 5152
            mu = rw_mu[l]
            for li in range(2):
                h = 2 * g + li
                h64 = slice(h * 64, (h + 1) * 64)
                rows = [np.arange(R0 + o_ + h * 64, R0 + o_ + (h + 1) * 64) for o_ in (0, 512, 1024)]
                im[f"rw_rp{li}"] = _padseg(z, rows[0], 1); im[f"rw_kp{li}"] = _padseg(z, rows[1], 1); im[f"rw_vp{li}"] = _padseg(z, rows[2], 1)
                sl = [slice(o_ + h * 64, o_ + (h + 1) * 64) for o_ in (0, 512, 1024)]
                im[f"rw_mu{li}"] = np.ascontiguousarray(np.stack([mu[0, sl[0]], mu[1, sl[0]], mu[0, sl[1]], mu[1, sl[1]], mu[0, sl[2]], mu[1, sl[2]]], axis=1))
                im[f"rw_w2_{li}"] = np.ascontiguousarray(rw_w2[l][:, :, h64].transpose(1, 0, 2))
                im[f"rw_a2_{li}"] = np.ascontiguousarray(rw_a2[l][:, :, h64].transpose(1, 0, 2))
                im[f"rw_g2_{li}"] = np.ascontiguousarray(rw_g2[l][:, h64].reshape(2, 128, 64).transpose(1, 0, 2))
                im[f"rw_pr{li}"] = np.ascontiguousarray(np.stack([rw_w0[l, 0, h64], rw_w0[l, 1, h64], rw_a0[l, 0, h64], rw_a0[l, 1, h64], rw_kk[l, h64],
                                                                  rw_ka[l, h64], rw_rk[l, h], rw_ln_g[l, h64], rw_ln_b[l, h64]], axis=1))
            im["rw_xwp"] = _padseg(z, np.arange(R0 + 1536, R0 + 1632), 1); im["rw_xap"] = _padseg(z, np.arange(R0 + 1632, R0 + 1728), 1)
            xg = _padseg(z, np.arange(R0 + 1728, R0 + 1984), 1)
            im["rw_xgp"] = np.ascontiguousarray(xg.reshape(2, 128, -1).transpose(1, 0, 2))
            im["rw_muw"] = np.ascontiguousarray(mu[:, 1536:1632].T); im["rw_mua"] = np.ascontiguousarray(mu[:, 1632:1728].T)
            im["rw_mug"] = np.ascontiguousarray(mu[:, 1728:1984].reshape(2, 2, 128).transpose(2, 1, 0))
            ims.append(im)
        r = _run("mix", ims)
        yf = []
        for b in range(2):
            yb_ = np.concatenate([r[4 * b + g]["ml_yb"].transpose(1, 0, 2).reshape(TSEQ, 128).T for g in range(4)], axis=0)
            yc_ = np.concatenate([r[4 * b + g]["gd_yc"].transpose(1, 0, 2).reshape(TSEQ, 128).T for g in range(4)], axis=0)
            yd_ = np.concatenate([r[4 * b + g][f"rw_yd{li}"] for g in range(4) for li in range(2)], axis=0)
            yf.append(np.concatenate([yb_, yc_, yd_], axis=0))
        ims = []
        for j in range(8):
            b, g = j // 4, j % 4
            z = zf[b]

            def seg(rows):
                a = z[rows]
                lat = np.pad(a[:, 256:], ((0, 0), (15, 15)))[:, g * 1024:(g + 1) * 1024 + 30]
                cc = np.pad(a[:, 0:256], ((0, 0), (15, 15)))[:, g * 64:(g + 1) * 64 + 30]
                return np.ascontiguousarray(np.concatenate([lat, cc], axis=1))

            cvw = np.concatenate([conv_dw[l].T, conv_b[l][:, None], conv_ln_g[l][:, None], conv_ln_b[l][:, None]], axis=1)
            y = yf[b]
            ybcd = np.ascontiguousarray(np.concatenate([y[:, 256 + g * 1024:256 + (g + 1) * 1024], y[:, g * 64:(g + 1) * 64]], axis=1))
            ims.append({"a_p": seg(np.arange(0, 512)), "g_p": seg(np.arange(512, 1024)),
                        "cvw": np.ascontiguousarray(cvw.reshape(4, 128, 34).transpose(1, 0, 2)), "ybcd": ybcd,
                        "gates": np.ascontiguousarray(zTs[j][7136:15328]), "xT": xTs[j], "wbr": w_br[l], "wout": w_out[l],
                        "mv": np.ascontiguousarray(np.stack([_fm(norm_g[l, 1]), _fm(gt1[:, b]), _fm(gt1[:, 2])], axis=2))})
        r = _run("post", ims)
        for j in range(8):
            b, g = j // 4, j % 4
            o = r[j]["x1T"].T
            x[b, g * 1024:(g + 1) * 1024] = o[0:1024]; cx[b, g * 64:(g + 1) * 64] = o[1024:1088]
        ims = []
        for j in range(8):
            b, g = j // 4, j % 4
            xl = np.pad(x[b], ((64, 64), (0, 0)))[g * 1024:(g + 1) * 1024 + 128]
            xc = np.pad(cx[b], ((1, 1), (0, 0)))[g * 64:(g + 1) * 64 + 2]
            mv = np.ascontiguousarray(np.stack([_fm(norm_g[l, 2]), _fm(sc2[:, b]), _fm(sh2[:, b]), _fm(sc2[:, 2]), _fm(sh2[:, 2]),
                                                _fm(norm_g[l, 3]), _fm(gt2[:, b]), _fm(gt2[:, 2])], axis=2))
            hmask = _rep(np.array([g > 0, g < 3, g > 0, g < 3], np.float32), 128)
            ims.append({"xe": np.ascontiguousarray(np.concatenate([xl, xc], axis=0).T), "mv": mv, "hmask": hmask, "wup": ffn_up[l],
                        "dw": np.ascontiguousarray(ffn_dw[l].reshape(9, 44, 128).transpose(2, 1, 0)), "wdn": ffn_down[l]})
        r = _run("ffn", ims)
        for j in range(8):
            b, g = j // 4, j % 4
            o = r[j]["x2T"].T
            x[b, g * 1024:(g + 1) * 1024] = o[0:1024]; cx[b, g * 64:(g + 1) * 64] = o[1024:1088]
    return x
```

```python
import contextlib
import numpy as np
import concourse.bass as bass
import concourse.mybir as mybir

F32 = mybir.dt.float32
BF16 = mybir.dt.bfloat16
AF = mybir.ActivationFunctionType
ALU = mybir.AluOpType
AX = mybir.AxisListType

SAFE_SAME_ENGINE = True


class Buf:
    __slots__ = ("name", "w", "r")

    def __init__(self, name):
        self.name = name
        self.w = None
        self.r = []


class Tn:
    def __init__(self, t, name):
        self.t = t
        self.b = Buf(name)
        self.subs = {}

    def __getitem__(self, idx):
        return self.t[idx]

    def sub(self, key):
        if key not in self.subs:
            self.subs[key] = Buf(f"{self.b.name}/{key}")
        return self.subs[key]


class P:
    def __init__(self, n_dma_sems=24):
        self.nc = bass.Bass("TRN2", target_bir_lowering=False)
        nc = self.nc
        self.st = contextlib.ExitStack()
        self.eng = {"pe": nc.tensor, "act": nc.scalar, "dve": nc.vector, "pool": nc.gpsimd, "sp": nc.sync}
        self.sem = {}
        self.cnt = {}
        for e in ("pe", "act", "dve", "pool"):
            self.sem[e] = self.st.enter_context(nc.semaphore(f"sem_{e}"))
            self.cnt[e] = 0
        self.waited = {e: {} for e in self.eng}
        self.dsem = [self.st.enter_context(nc.semaphore(f"dsem{i}")) for i in range(n_dma_sems)]
        self.dtgt = [0] * n_dma_sems
        self.drr = 0
        self.n_inst = 0
        self.out_events = []

    def sb(self, name, shape, dt=F32):
        self._uid = getattr(self, "_uid", 0) + 1
        name = f"{name}_{self._uid}"
        return Tn(self.st.enter_context(self.nc.sbuf_tensor(name, list(shape), dt)), name)

    def ps(self, name, shape, dt=F32):
        return Tn(self.st.enter_context(self.nc.psum_tensor(name, list(shape), dt)), name)

    def dram(self, name, shape, kind, dt=F32):
        return self.nc.dram_tensor(name, list(shape), dt, kind=kind).ap()

    def _wait(self, w, ev):
        if ev is None:
            return
        kind = ev[0]
        if kind == "eng":
            _, e, n = ev
            if e == w and (e == "pe" or not SAFE_SAME_ENGINE):
                return
            key = ("eng", e)
            if self.waited[w].get(key, 0) >= n:
                return
            self.eng[w].wait_ge(self.sem[e], n)
            self.waited[w][key] = n
        else:
            _, i, tgt = ev
            key = ("dma", i)
            if self.waited[w].get(key, 0) >= tgt:
                return
            self.eng[w].wait_ge(self.dsem[i], tgt)
            self.waited[w][key] = tgt

    @staticmethod
    def _b(x):
        return x.b if isinstance(x, Tn) else x

    def _deps(self, w, reads, writes):
        for r in reads:
            self._wait(w, self._b(r).w)
        for x in writes:
            b = self._b(x)
            self._wait(w, b.w)
            for ev in b.r:
                self._wait(w, ev)

    def _record(self, ev, reads, writes):
        for r in reads:
            self._b(r).r.append(ev)
            if len(self._b(r).r) > 64:
                self._b(r).r = self._compact(self._b(r).r)
        for x in writes:
            b = self._b(x)
            b.w = ev
            b.r = []

    @staticmethod
    def _compact(evs):
        best = {}
        for ev in evs:
            k = (ev[0], ev[1])
            if k not in best or best[k][2] < ev[2]:
                best[k] = ev
        return list(best.values())

    def op(self, e, fn, reads=(), writes=()):
        self._deps(e, reads, writes)
        inst = fn(self.eng[e])
        self.cnt[e] += 1
        inst.then_inc(self.sem[e], 1)
        ev = ("eng", e, self.cnt[e])
        self._record(ev, reads, writes)
        self.n_inst += 1
        return ev

    def dma(self, q, out, in_, reads=(), writes=(), is_output=False, **kw):
        self._deps(q, reads, writes)
        i = self.drr
        self.drr = (self.drr + 1) % len(self.dsem)
        if self.dtgt[i] > 0:
            self._wait(q, ("dma", i, self.dtgt[i]))
        inst = self.eng[q].dma_start(out=out, in_=in_, **kw)
        self.dtgt[i] += 16
        inst.then_inc(self.dsem[i], 16)
        ev = ("dma", i, self.dtgt[i])
        self._record(ev, reads, writes)
        if is_output:
            self.out_events.append(ev)
        self.n_inst += 1
        return ev

    def finish(self, w="sp"):
        for ev in self.out_events:
            self._wait(w, ev)
        for i, t in enumerate(self.dtgt):
            if t:
                self._wait(w, ("dma", i, t))
        for e in self.cnt:
            if self.cnt[e]:
                self._wait(w, ("eng", e, self.cnt[e]))

    def close(self):
        self.st.close()


def _p_barrier(self):
    engs = list(self.eng.keys())
    for w in engs:
        for e in self.cnt:
            if self.cnt[e]:
                self._wait(w, ("eng", e, self.cnt[e]))
        for i, t in enumerate(self.dtgt):
            if t:
                self._wait(w, ("dma", i, t))


P.barrier = _p_barrier


@contextlib.contextmanager
def _p_scope(self):
    outer = self.st
    self.st = contextlib.ExitStack()
    try:
        yield
    finally:
        self.barrier()
        self.st.close()
        self.st = outer


P.scope = _p_scope


def _p_init_ps(self, n=8):
    self.pstiles = [self.ps(f"psr{i}", [128, 512]) for i in range(n)]
    self.psi = 0


def _p_next_ps(self):
    t = self.pstiles[self.psi % len(self.pstiles)]
    self.psi += 1
    return t


P.init_ps = _p_init_ps
P.next_ps = _p_next_ps


def bc_last(ap2d, L):
    P_, n = ap2d.shape
    return ap2d.unsqueeze(2).to_broadcast([P_, n, L])


def bc_mid(ap2d, n):
    P_, L = ap2d.shape
    return ap2d.unsqueeze(1).to_broadcast([P_, n, L])


def _p_coll(self, kind, in_ap, out_ap, groups, reads=(), writes=(), op=None):
    q = "pool"
    self._deps(q, reads, writes)
    i = self.drr
    self.drr = (self.drr + 1) % len(self.dsem)
    if self.dtgt[i] > 0:
        self._wait(q, ("dma", i, self.dtgt[i]))
    inst = self.nc.gpsimd.collective_compute(kind, op if op is not None else ALU.bypass, replica_groups=groups, ins=[in_ap], outs=[out_ap])
    self.dtgt[i] += 16
    inst.then_inc(self.dsem[i], 16)
    ev = ("dma", i, self.dtgt[i])
    self._record(ev, reads, writes)
    self.n_inst += 1
    return ev


P.coll = _p_coll


L = 64


def tt(p, out, in0, in1, op, R, W, eng="dve"):
    return p.op(eng, lambda e: e.tensor_tensor(out=out, in0=in0, in1=in1, op=op), reads=R, writes=W)


def ts(p, out, in0, s1, op0, R, W, s2=None, op1=None, eng="dve"):
    if op1 is None:
        return p.op(eng, lambda e: e.tensor_scalar(out=out, in0=in0, scalar1=s1, scalar2=None, op0=op0), reads=R, writes=W)
    return p.op(eng, lambda e: e.tensor_scalar(out=out, in0=in0, scalar1=s1, scalar2=s2, op0=op0, op1=op1), reads=R, writes=W)


def stt(p, out, in0, scalar, in1, op0, op1, R, W):
    return p.op("dve", lambda e: e.scalar_tensor_tensor(out=out, in0=in0, scalar=scalar, in1=in1, op0=op0, op1=op1), reads=R, writes=W)


def act(p, out, in_, func, R, W, bias=None, scale=None):
    kw = {}
    if bias is not None:
        kw["bias"] = bias
    if scale is not None:
        kw["scale"] = scale
    return p.op("act", lambda e: e.activation(out=out, in_=in_, func=func, **kw), reads=R, writes=W)


def mm(p, out, lhsT, rhs, R, W, start=True, stop=True):
    return p.op("pe", lambda e: e.matmul(out, lhsT=lhsT, rhs=rhs, start=start, stop=stop), reads=R, writes=W)


def cp(p, out, in_, R, W, eng):
    if eng == "act":
        return act(p, out, in_, AF.Copy, R, W)
    return p.op(eng, lambda e: e.tensor_copy(out=out, in_=in_), reads=R, writes=W)


def make_consts(p):
    c = {}
    c["ones"] = p.sb("c_ones", [128, 128])
    p.op("dve", lambda e: e.memset(c["ones"][:, :], 1.0), writes=[c["ones"]])
    for name, pat, cm, cmp_ in (("UT", 1, -1, ALU.is_ge), ("LT", -1, 1, ALU.is_ge), ("UTs", 1, -1, ALU.is_gt),
                                ("LTs", -1, 1, ALU.is_gt), ("I", 1, -1, ALU.is_equal)):
        t = p.sb("c_" + name, [128, 128])
        p.op("pool", lambda e: e.affine_select(out=t[:, :], in_=c["ones"][:, :], pattern=[[pat, 128]], compare_op=cmp_,
                                               fill=0.0, base=0, channel_multiplier=cm), reads=[c["ones"]], writes=[t])
        c[name] = t
    c["eps6"] = p.sb("c_eps6", [128, 1])
    p.op("dve", lambda e: e.memset(c["eps6"][:, :], 1e-6), writes=[c["eps6"]])
    return c


def dir_masks(c, d):
    if d == 0:
        return c["UT"], c["UTs"], c["LT"], c["LTs"]
    return c["LT"], c["LTs"], c["UT"], c["UTs"]


def chunk_order(d, nc_ctx, nc_lat):
    n = nc_ctx + nc_lat
    if d == 0:
        return list(range(n))
    return list(range(nc_ctx - 1, -1, -1)) + list(range(n - 1, nc_ctx - 1, -1))


def rowbc(p, c, col, out, pout, nch, scratch):
    tt(p, scratch[:, :, :], bc_last(col[0:L, 0:nch], L), bc_mid(c["I"][0:L, 0:L], nch), ALU.mult, [col, c["I"]], [scratch])
    flat = scratch[:, :, :].rearrange("p n t -> p (n t)")
    tot = nch * L
    for i, c0 in enumerate(range(0, tot, 512)):
        c1 = min(c0 + 512, tot)
        ps = p.next_ps()
        mm(p, ps[0:pout, 0:c1 - c0], c["ones"][0:L, 0:pout], flat[:, c0:c1], [c["ones"], scratch], [ps])
        cp(p, out[0:pout, c0:c1], ps[0:pout, 0:c1 - c0], [ps], [out], "act" if i % 2 else "dve")


def colmm(p, lhsT_ap, rhs_tn, rhs_ap, out_tn, out_ap, pout, ncol, Rextra=()):
    ps = p.next_ps()
    mm(p, ps[0:pout, 0:ncol], lhsT_ap, rhs_ap, [rhs_tn] + list(Rextra), [ps])
    cp(p, out_ap, ps[0:pout, 0:ncol], [ps], [out_tn], "dve")


def batched_chunk_mm(p, nch, lhsT_fn, rhs_fn, R, mrows, ncols, evac):
    G = 512 // ncols
    for n0 in range(0, nch, G):
        g = min(G, nch - n0)
        ps = p.next_ps()
        for j in range(g):
            n = n0 + j
            mm(p, ps[0:mrows, j * ncols:(j + 1) * ncols], lhsT_fn(n), rhs_fn(n), R, [ps])
        evac(ps, n0, g)


def transpose_chunks(p, c, src_fn, Rsrc, kin, dst, nch, eng_alt=True):
    m = None
    cnt = [0]

    def evac(ps, n0, g):
        eng = "act" if (cnt[0] % 2 and eng_alt) else "dve"
        cnt[0] += 1
        cp(p, dst[:, n0:n0 + g, :], ps[0:dst.t.shape[0], 0:g * kin].rearrange("p (g k) -> p g k", k=kin), [ps], [dst], eng)

    G = 512 // kin
    for n0 in range(0, nch, G):
        g = min(G, nch - n0)
        ps = p.next_ps()
        for j in range(g):
            src = src_fn(n0 + j)
            mm_out = ps[0:src.shape[1], j * kin:(j + 1) * kin]
            p.op("pe", lambda e: e.transpose(mm_out, src, c["I"][0:kin, 0:kin]), reads=list(Rsrc) + [c["I"]], writes=[ps])
        evac(ps, n0, g)


def tri_inverse_T(p, c, X, Y, nch, tmp):
    Pm, Qm, X2, Y2 = tmp["P"], tmp["Q"], tmp["X2"], tmp["Y2"]
    I3 = bc_mid(c["I"][0:L, 0:L], nch)
    tt(p, Pm[:, :, :], Y[:, :, :], I3, ALU.add, [Y, c["I"]], [Pm.sub(n0) for n0 in range(0, nch, 8)])
    tt(p, Qm[:, :, :], X[:, :, :], I3, ALU.add, [X, c["I"]], [Qm.sub(n0) for n0 in range(0, nch, 8)])
    curX, curY, nxtX, nxtY = X, Y, X2, Y2
    for lvl in range(5):
        def ev_x(ps, n0, g):
            cp(p, nxtX[:, n0:n0 + g, :], ps[0:L, 0:g * L].rearrange("p (g k) -> p g k", k=L), [ps], [nxtX], "act")

        def ev_y(ps, n0, g):
            cp(p, nxtY[:, n0:n0 + g, :], ps[0:L, 0:g * L].rearrange("p (g k) -> p g k", k=L), [ps], [nxtY], "dve")

        batched_chunk_mm(p, nch, lambda n: curY[:, n, :], lambda n: curX[:, n, :], [curX, curY], L, L, ev_x)
        batched_chunk_mm(p, nch, lambda n: curX[:, n, :], lambda n: curY[:, n, :], [curX, curY], L, L, ev_y)
        curX, curY, nxtX, nxtY = nxtX, nxtY, curX, curY

        pend = []

        def ev_p(ps, n0, g):
            pend.append(("P", ps, n0, g))

        def ev_q(ps, n0, g):
            pend.append(("Q", ps, n0, g))

        G = 8
        for n0 in range(0, nch, G):
            g = min(G, nch - n0)
            psP = p.next_ps()
            for j in range(g):
                n = n0 + j
                mm(p, psP[0:L, j * L:(j + 1) * L], Qm[:, n, :], curY[:, n, :], [Qm.sub(n0), curY], [psP])
            psQ = p.next_ps()
            for j in range(g):
                n = n0 + j
                mm(p, psQ[0:L, j * L:(j + 1) * L], Pm[:, n, :], curX[:, n, :], [Pm.sub(n0), curX], [psQ])
            tt(p, Pm[:, n0:n0 + g, :], Pm[:, n0:n0 + g, :], psP[0:L, 0:g * L].rearrange("p (g k) -> p g k", k=L), ALU.add,
               [psP, Pm.sub(n0)], [Pm.sub(n0)])
            if lvl < 4:
                tt(p, Qm[:, n0:n0 + g, :], Qm[:, n0:n0 + g, :], psQ[0:L, 0:g * L].rearrange("p (g k) -> p g k", k=L), ALU.add,
                   [psQ, Qm.sub(n0)], [Qm.sub(n0)])
    subs = [Pm.sub(n0) for n0 in range(0, nch, 8)]
    return Pm, subs


def chunk_loops(p, chains):
    nsteps = max(len(ch["order"]) for ch in chains)
    if globals().get("SKIP_LOOPS"):
        nsteps = 1
    for ch in chains:
        if "cur" not in ch["st"]:
            ch["st"]["cur"] = 0
            p.op("dve", lambda e: e.memset(ch["M"][0][:, :], 0.0), writes=[ch["M"][0]])
    for i in range(nsteps):
        for ch in chains:
            if i >= len(ch["order"]):
                continue
            n = ch["order"][i]
            dk, dvp = ch["dk"], ch["dvp"]
            M = ch["M"][ch["st"]["cur"]]
            Mn = ch["M"][1 - ch["st"]["cur"]]
            R = list(ch.get("R", [])) + [ch["QT"], ch["A1T"], ch["BL"], ch["decay"], ch["U0"]]
            if ch.get("AhatT") is not None:
                R.append(ch["AhatT"])
            if ch.get("K0") is not None:
                R += list(ch["K0"])
            if ch.get("AhatT") is not None:
                psu = p.next_ps()
                mm(p, psu[0:L, 0:dvp], ch["AhatT"][:, n, :], M[:, :], [M] + R, [psu])
                u = ch["ub"][i % 2]
                tt(p, u[:, :], psu[0:L, 0:dvp], ch["U0"][:, n, :], ALU.add, [psu] + R, [u])
                u_ap, u_dep = u[:, :], [u]
            else:
                u_ap, u_dep = ch["U0"][:, n, :], []
            psy = p.next_ps()
            mm(p, psy[0:L, 0:dvp], ch["QT"][:, n * L:(n + 1) * L], M[:, :], [M] + R, [psy], start=True, stop=False)
            mm(p, psy[0:L, 0:dvp], ch["A1T"][:, n, :], u_ap, u_dep + R, [psy], start=False, stop=True)
            cp(p, ch["Y"][:, n, :], psy[0:L, 0:dvp], [psy], [ch["Y"].sub(n)], "act")
            psm = p.next_ps()
            if ch.get("K0") is not None:
                KL, V = ch["K0"]
                mm(p, psm[0:dk, 0:dvp], KL[:, n, :], V[:, n, :], R, [psm], start=True, stop=False)
                mm(p, psm[0:dk, 0:dvp], ch["BL"][:, n, :], u_ap, u_dep + R, [psm], start=False, stop=True)
            else:
                mm(p, psm[0:dk, 0:dvp], ch["BL"][:, n, :], u_ap, u_dep + R, [psm])
            stt(p, Mn[:, :], M[:, :], ch["decay"][:, n:n + 1], psm[0:dk, 0:dvp], ALU.mult, ALU.add, [M, psm] + R, [Mn])
            ch["st"]["cur"] = 1 - ch["st"]["cur"]


def emit_mlstm(p, c, io, nc_ctx, nc_lat, bs=16):
    nch = nc_ctx + nc_lat
    T = nch * L
    DH = 128
    with p.scope():
        H = p.sb("ml_H", [L, nch, DH])
        gb = p.sb("ml_gb", [L, 4]); ngb = p.sb("ml_ng", [L, DH]); ngate = p.sb("ml_ngb", [L, 4])
        p.dma("sp", gb[:, :], io["gb"], writes=[gb])
        p.dma("sp", ngb[:, :], io["ng"], writes=[ngb])
        ts(p, ngate[:, :], gb[:, :], -1.0, ALU.mult, [gb], [ngate])
        with p.scope():
            qT = p.sb("ml_qT", [128, T]); kT = p.sb("ml_kT", [128, T])
            k_tm = p.sb("ml_ktm", [L, nch, DH]); vp = p.sb("ml_vp", [L, nch, DH + 1])
            p.dma("sp", qT[:, :], io["qT"], writes=[qT])
            p.dma("sp", kT[:, :], io["kT"], writes=[kT])
            p.dma("sp", k_tm[:, :, :], io["k_tm"], writes=[k_tm])
            p.dma("sp", vp[:, :, 0:DH], io["v_tm"], writes=[vp.sub("v")])
            p.op("pool", lambda e: e.memset(vp[:, :, DH:DH + 1], 1.0), writes=[vp.sub("one")])
            act(p, qT[:, :], qT[:, :], AF.Copy, [qT], [qT], scale=DH ** -0.5)
            VP = [vp.sub("v"), vp.sub("one")]
            for d in range(2):
                with p.scope():
                    mT_incl, _, _, _ = dir_masks(c, d)
                    gi = p.sb("ml_gi", [L, nch]); gf = p.sb("ml_gf", [L, nch])
                    p.dma("sp", gi[:, :], io[f"gi{d}"], writes=[gi])
                    p.dma("sp", gf[:, :], io[f"gf{d}"], writes=[gf])
                    li = p.sb("ml_li", [L, nch]); lf = p.sb("ml_lf", [L, nch]); bt = p.sb("ml_bt", [L, nch])
                    blb = p.sb("ml_blb", [128, nch]); ecol = p.sb("ml_ecol", [L, nch]); eli = p.sb("ml_eli", [L, nch])
                    decay = p.sb("ml_decay", [128, nch])
                    ts(p, li[:, :], gi[:, :], gb[:, 2 * d:2 * d + 1], ALU.add, [gi, gb], [li])
                    act(p, lf[:, :], gf[:, :], AF.Exp, [gf, ngate], [lf], bias=ngate[:, 2 * d + 1:2 * d + 2], scale=-1.0)
                    act(p, lf[:, :], lf[:, :], AF.Ln, [lf], [lf], bias=1.0)
                    ts(p, lf[:, :], lf[:, :], -1.0, ALU.mult, [lf], [lf])
                    colmm(p, mT_incl[0:L, 0:L], lf, lf[:, :], bt, bt[:, :], L, nch, [mT_incl])
                    colmm(p, c["ones"][0:L, 0:128], lf, lf[:, :], blb, blb[:, :], 128, nch, [c["ones"]])
                    act(p, decay[:, :], blb[:, :], AF.Exp, [blb], [decay])
                    act(p, eli[:, :], li[:, :], AF.Exp, [li], [eli])
                    tt(p, ecol[:, :], li[:, :], bt[:, :], ALU.subtract, [li, bt], [ecol])
                    tt(p, ecol[:, :], ecol[:, :], blb[0:L, :], ALU.add, [ecol, blb], [ecol])
                    act(p, ecol[:, :], ecol[:, :], AF.Exp, [ecol], [ecol])
                    Ms = [p.sb(f"ml_M{i}", [128, DH + 1]) for i in range(2)]
                    st = {}
                    for (lo, hi) in blocks_for(d, nc_ctx, nc_lat, bs):
                        nb = hi - lo
                        TB = nb * L
                        with p.scope():
                            brow = p.sb("ml_brow", [128, TB]); scr = p.sb("ml_scr", [L, nb, L]); btb = p.sb("ml_btb", [L, nb])
                            cp(p, btb[:, :], bt[:, lo:hi], [bt], [btb], "dve")
                            rowbc(p, c, btb, brow, 128, nb, scr)
                            wT = p.sb("ml_wT", [L, nb, L])
                            tt(p, wT[:, :, :], brow[0:L, :].rearrange("p (n t) -> p n t", t=L), bc_last(btb[:, :], L), ALU.subtract, [brow, btb], [wT])
                            ts(p, wT[:, :, :], wT[:, :, :], 0.0, ALU.min, [wT], [wT])
                            act(p, wT[:, :, :], wT[:, :, :], AF.Exp, [wT], [wT])
                            tt(p, wT[:, :, :], wT[:, :, :], bc_mid(mT_incl[0:L, 0:L], nb), ALU.mult, [wT, mT_incl], [wT])
                            tt(p, wT[:, :, :], wT[:, :, :], bc_last(eli[:, lo:hi], L), ALU.mult, [wT, eli], [wT])
                            qdT = brow
                            act(p, brow[:, :], brow[:, :], AF.Exp, [brow], [brow])
                            tt(p, qdT[:, :], qT[:, lo * L:hi * L], brow[:, :], ALU.mult, [qT, brow], [qdT])
                            A1T = scr

                            def evac(ps, n0, g):
                                tt(p, A1T[:, n0:n0 + g, :], ps[0:L, 0:g * L].rearrange("p (g k) -> p g k", k=L), wT[:, n0:n0 + g, :], ALU.mult,
                                   [ps, wT], [A1T])

                            batched_chunk_mm(p, nb, lambda n: kT[:, (lo + n) * L:(lo + n + 1) * L], lambda n: qT[:, (lo + n) * L:(lo + n + 1) * L],
                                             [kT, qT], L, L, evac)
                            BL = p.sb("ml_BL", [L, nb, DH])
                            tt(p, BL[:, :, :], k_tm[:, lo:hi, :], bc_last(ecol[:, lo:hi], DH), ALU.mult, [k_tm, ecol], [BL])
                            Y = p.sb("ml_Y", [L, nb, DH + 1]); vpb = p.sb("ml_vpb", [L, nb, DH + 1]); dcy = p.sb("ml_dcy", [128, nb])
                            cp(p, vpb[:, :, :], vp[:, lo:hi, :], VP, [vpb], "dve")
                            cp(p, dcy[:, :], decay[:, lo:hi], [decay], [dcy], "dve")
                            order = list(range(nb)) if d == 0 else list(range(nb - 1, -1, -1))
                            chain = dict(dk=128, dvp=DH + 1, order=order, QT=qdT, A1T=A1T, BL=BL, decay=dcy,
                                         U0=vpb, AhatT=None, Y=Y, K0=None, M=Ms, R=[], st=st)
                            chunk_loops(p, [chain])
                            Ysubs = [Y.sub(n) for n in range(nb)]
                            den = p.sb("ml_den", [L, nb])
                            ts(p, den[:, :], Y[:, :, DH], -1.0, ALU.mult, Ysubs, [den])
                            tt(p, den[:, :], den[:, :], Y[:, :, DH], ALU.max, Ysubs + [den], [den])
                            ts(p, den[:, :], den[:, :], 1.0, ALU.max, [den], [den])
                            p.op("dve", lambda e: e.reciprocal(out=den[:, :], in_=den[:, :]), reads=[den], writes=[den])
                            if d == 0:
                                tt(p, H[:, lo:hi, :], Y[:, :, 0:DH], bc_last(den[:, :], DH), ALU.mult, Ysubs + [den], [H])
                            else:
                                tt(p, Y[:, :, 0:DH], Y[:, :, 0:DH], bc_last(den[:, :], DH), ALU.mult, Ysubs + [den], [Y])
                                tt(p, H[:, lo:hi, :], H[:, lo:hi, :], Y[:, :, 0:DH], ALU.add, [H, Y], [H])
        with p.scope():
            sq = p.sb("ml_sq", [L, nch, DH]); ss = p.sb("ml_ss", [L, nch]); o_tm = p.sb("ml_otm", [L, nch, DH])
            p.dma("sp", o_tm[:, :, :], io["o_tm"], writes=[o_tm])
            tt(p, sq[:, :, :], H[:, :, :], H[:, :, :], ALU.mult, [H], [sq])
            p.op("dve", lambda e: e.tensor_reduce(out=ss[:, :], in_=sq[:, :, :], axis=AX.X, op=ALU.add), reads=[sq], writes=[ss])
            act(p, ss[:, :], ss[:, :], AF.Sqrt, [ss, c["eps6"]], [ss], scale=1.0 / DH, bias=c["eps6"][0:L, 0:1])
            p.op("dve", lambda e: e.reciprocal(out=ss[:, :], in_=ss[:, :]), reads=[ss], writes=[ss])
            tt(p, H[:, :, :], H[:, :, :], bc_last(ss[:, :], DH), ALU.mult, [H, ss], [H])
            tt(p, H[:, :, :], H[:, :, :], bc_mid(ngb[:, :], nch), ALU.mult, [H, ngb], [H])
            act(p, o_tm[:, :, :], o_tm[:, :, :], AF.Sigmoid, [o_tm], [o_tm])
            tt(p, H[:, :, :], H[:, :, :], o_tm[:, :, :], ALU.mult, [H, o_tm], [H])
            p.dma("pool", io["yb"], H[:, :, :], reads=[H], is_output=True)


def blocks_for(d, nc_ctx, nc_lat, bs=24):
    bl = []
    if nc_ctx:
        bl.append((0, nc_ctx))
    lat = []
    for lo in range(nc_ctx, nc_ctx + nc_lat, bs):
        lat.append((lo, min(lo + bs, nc_ctx + nc_lat)))
    if d == 0:
        return bl + lat
    return bl + lat[::-1]


def l2norm_fm(p, c, x, T, scale, tmp):
    act(p, tmp[:, :], x[:, :], AF.Square, [x], [tmp])
    for i, c0 in enumerate(range(0, T, 512)):
        c1 = min(c0 + 512, T)
        ps = p.next_ps()
        mm(p, ps[:, 0:c1 - c0], c["ones"][:, :], tmp[:, c0:c1], [c["ones"], tmp], [ps])
        act(p, tmp[:, c0:c1], ps[:, 0:c1 - c0], AF.Sqrt, [ps, c["eps6"]], [tmp], bias=c["eps6"][:, 0:1])
    p.op("dve", lambda e: e.reciprocal(out=tmp[:, :], in_=tmp[:, :]), reads=[tmp], writes=[tmp])
    stt(p, x[:, :], x[:, :], float(scale), tmp[:, :], ALU.mult, ALU.mult, [x, tmp], [x])


def emit_gdn(p, c, io, nc_ctx, nc_lat, bs=24):
    nch = nc_ctx + nc_lat
    T = nch * L
    Tc, Tl = nc_ctx * L, nc_lat * L
    DH = 128
    with p.scope():
        O = p.sb("gd_O", [L, nch, DH])
        cw = p.sb("gd_cw", [128, 15]); gp = p.sb("gd_gp", [L, 4]); ngb = p.sb("gd_ng", [L, DH])
        p.dma("sp", cw[:, :], io["cw"], writes=[cw]); p.dma("sp", gp[:, :], io["gp"], writes=[gp]); p.dma("sp", ngb[:, :], io["ng"], writes=[ngb])
        p.gd_big = contextlib.ExitStack()
        p.gd_big.enter_context(p.scope())
        q_fm = p.sb("gd_q", [128, T]); k_fm = p.sb("gd_k", [128, T])
        k_tm = p.sb("gd_ktm", [L, nch, DH]); v_tm = p.sb("gd_vtm", [L, nch, DH])
        with p.scope():
            v_fm = p.sb("gd_v", [128, T]); raw = p.sb("gd_raw", [128, T + 8]); tmp = p.sb("gd_tmp", [128, T])
            for wi, (nm, dst) in enumerate((("qp", q_fm), ("kp", k_fm), ("vp", v_fm))):
                p.dma("sp", raw[:, :], io[nm], writes=[raw])
                for (o_in, o_out, tl) in ((0, 0, Tc), (Tc + 4, Tc, Tl)):
                    if tl == 0:
                        continue
                    ts(p, dst[:, o_out:o_out + tl], raw[:, o_in:o_in + tl], cw[:, wi * 5:wi * 5 + 1], ALU.mult, [raw, cw], [dst])
                    for j in range(1, 5):
                        stt(p, dst[:, o_out:o_out + tl], raw[:, o_in + j:o_in + j + tl], cw[:, wi * 5 + j:wi * 5 + j + 1],
                            dst[:, o_out:o_out + tl], ALU.mult, ALU.add, [raw, cw, dst], [dst])
                act(p, dst[:, :], dst[:, :], AF.Silu, [dst], [dst])
            l2norm_fm(p, c, q_fm, T, DH ** -0.5, tmp)
            l2norm_fm(p, c, k_fm, T, 1.0, tmp)
            transpose_chunks(p, c, lambda n: k_fm[:, n * L:(n + 1) * L], [k_fm], 128, k_tm, nch)
            transpose_chunks(p, c, lambda n: v_fm[:, n * L:(n + 1) * L], [v_fm], 128, v_tm, nch)
        for d in range(2):
            with p.scope():
                mT_incl, mT_str, m_incl, m_str = dir_masks(c, d)
                ar = p.sb("gd_ar", [L, nch]); br = p.sb("gd_br", [L, nch])
                p.dma("sp", ar[:, :], io[f"a{d}"], writes=[ar]); p.dma("sp", br[:, :], io[f"b{d}"], writes=[br])
                lg = p.sb("gd_lg", [L, nch]); beta = p.sb("gd_beta", [L, nch]); nbeta = p.sb("gd_nbeta", [L, nch])
                gc = p.sb("gd_gc", [L, nch]); glb = p.sb("gd_glb", [128, nch]); decay = p.sb("gd_decay", [128, nch])
                negA = p.sb("gd_negA", [L, 1]); egc = p.sb("gd_egc", [L, nch]); bege = p.sb("gd_bege", [L, nch]); ekd = p.sb("gd_ekd", [L, nch])
                act(p, negA[:, :], gp[:, 2 * d:2 * d + 1], AF.Exp, [gp], [negA])
                ts(p, negA[:, :], negA[:, :], -1.0, ALU.mult, [negA], [negA])
                act(p, lg[:, :], ar[:, :], AF.Exp, [ar, gp], [lg], bias=gp[:, 2 * d + 1:2 * d + 2])
                act(p, lg[:, :], lg[:, :], AF.Ln, [lg], [lg], bias=1.0)
                ts(p, lg[:, :], lg[:, :], negA[:, 0:1], ALU.mult, [lg, negA], [lg])
                act(p, beta[:, :], br[:, :], AF.Sigmoid, [br], [beta])
                ts(p, nbeta[:, :], beta[:, :], -1.0, ALU.mult, [beta], [nbeta])
                colmm(p, mT_incl[0:L, 0:L], lg, lg[:, :], gc, gc[:, :], L, nch, [mT_incl])
                colmm(p, c["ones"][0:L, 0:128], lg, lg[:, :], glb, glb[:, :], 128, nch, [c["ones"]])
                act(p, decay[:, :], glb[:, :], AF.Exp, [glb], [decay])
                act(p, egc[:, :], gc[:, :], AF.Exp, [gc], [egc])
                tt(p, bege[:, :], egc[:, :], beta[:, :], ALU.mult, [egc, beta], [bege])
                tt(p, ekd[:, :], glb[0:L, :], gc[:, :], ALU.subtract, [glb, gc], [ekd])
                act(p, ekd[:, :], ekd[:, :], AF.Exp, [ekd], [ekd])
                Ms = [p.sb(f"gd_M{i}", [128, DH]) for i in range(2)]
                ubs = [p.sb(f"gd_u{i}", [L, DH]) for i in range(2)]
                st = {}
                for (lo, hi) in blocks_for(d, nc_ctx, nc_lat, bs):
                    nb = hi - lo
                    TB = nb * L
                    with p.scope():
                        grow = p.sb("gd_grow", [128, TB]); brow = p.sb("gd_brow", [L, TB]); scr = p.sb("gd_scr", [L, nb, L])
                        gcb = p.sb("gd_gcb", [L, nb]); nbb = p.sb("gd_nbb", [L, nb])
                        cp(p, gcb[:, :], gc[:, lo:hi], [gc], [gcb], "dve")
                        cp(p, nbb[:, :], nbeta[:, lo:hi], [nbeta], [nbb], "dve")
                        rowbc(p, c, gcb, grow, 128, nb, scr)
                        rowbc(p, c, nbb, brow, L, nb, scr)
                        decT = p.sb("gd_decT", [L, nb, L]); dec = p.sb("gd_dec", [L, nb, L]); wA = p.sb("gd_wA", [L, nb, L])
                        g3 = grow[0:L, :].rearrange("p (n t) -> p n t", t=L)
                        tt(p, decT[:, :, :], g3, bc_last(gcb[:, :], L), ALU.subtract, [grow, gcb], [decT])
                        ts(p, decT[:, :, :], decT[:, :, :], 0.0, ALU.min, [decT], [decT])
                        act(p, decT[:, :, :], decT[:, :, :], AF.Exp, [decT], [decT])
                        tt(p, dec[:, :, :], bc_last(gcb[:, :], L), g3, ALU.subtract, [grow, gcb], [dec])
                        ts(p, dec[:, :, :], dec[:, :, :], 0.0, ALU.min, [dec], [dec])
                        act(p, dec[:, :, :], dec[:, :, :], AF.Exp, [dec], [dec])
                        tt(p, wA[:, :, :], decT[:, :, :], bc_mid(mT_incl[0:L, 0:L], nb), ALU.mult, [decT, mT_incl], [wA])
                        tt(p, decT[:, :, :], decT[:, :, :], bc_mid(mT_str[0:L, 0:L], nb), ALU.mult, [decT, mT_str], [decT])
                        tt(p, decT[:, :, :], decT[:, :, :], brow[:, :].rearrange("p (n t) -> p n t", t=L), ALU.mult, [decT, brow], [decT])
                        tt(p, dec[:, :, :], dec[:, :, :], bc_mid(m_str[0:L, 0:L], nb), ALU.mult, [dec, m_str], [dec])
                        tt(p, dec[:, :, :], dec[:, :, :], bc_last(nbb[:, :], L), ALU.mult, [dec, nbb], [dec])
                        X, Y = dec, decT

                        def ev_g(ps, n0, g):
                            v = ps[0:L, 0:g * L].rearrange("p (g k) -> p g k", k=L)
                            tt(p, X[:, n0:n0 + g, :], v, X[:, n0:n0 + g, :], ALU.mult, [ps, X], [X])
                            tt(p, Y[:, n0:n0 + g, :], v, Y[:, n0:n0 + g, :], ALU.mult, [ps, Y], [Y])

                        batched_chunk_mm(p, nb, lambda n: k_fm[:, (lo + n) * L:(lo + n + 1) * L], lambda n: k_fm[:, (lo + n) * L:(lo + n + 1) * L],
                                         [k_fm], L, L, ev_g)

                        def ev_a(ps, n0, g):
                            v = ps[0:L, 0:g * L].rearrange("p (g k) -> p g k", k=L)
                            tt(p, wA[:, n0:n0 + g, :], v, wA[:, n0:n0 + g, :], ALU.mult, [ps, wA], [wA])

                        batched_chunk_mm(p, nb, lambda n: k_fm[:, (lo + n) * L:(lo + n + 1) * L], lambda n: q_fm[:, (lo + n) * L:(lo + n + 1) * L],
                                         [k_fm, q_fm], L, L, ev_a)
                        tmp = {k_: p.sb("gd_ti" + k_, [L, nb, L]) for k_ in ("P", "Q", "X2", "Y2")}
                        Pm, Psubs = tri_inverse_T(p, c, X, Y, nb, tmp)
                        vb = p.sb("gd_vb", [L, nb, DH]); kbg = p.sb("gd_kbg", [L, nb, DH]); BL = p.sb("gd_BL", [L, nb, DH])
                        tt(p, vb[:, :, :], v_tm[:, lo:hi, :], bc_last(beta[:, lo:hi], DH), ALU.mult, [v_tm, beta], [vb])
                        tt(p, kbg[:, :, :], k_tm[:, lo:hi, :], bc_last(bege[:, lo:hi], DH), ALU.mult, [k_tm, bege], [kbg])
                        tt(p, BL[:, :, :], k_tm[:, lo:hi, :], bc_last(ekd[:, lo:hi], DH), ALU.mult, [k_tm, ekd], [BL], eng="pool")
                        U0 = p.sb("gd_U0", [L, nb, DH]); AhatT = p.sb("gd_Ah", [128, nb, L])

                        def ev_u(ps, n0, g):
                            cp(p, U0[:, n0:n0 + g, :], ps[0:L, 0:g * DH].rearrange("p (g k) -> p g k", k=DH), [ps], [U0], "act")

                        batched_chunk_mm(p, nb, lambda n: Pm[:, n, :], lambda n: vb[:, n, :], Psubs + [vb], L, DH, ev_u)

                        def ev_w(ps, n0, g):
                            ts(p, AhatT[:, n0:n0 + g, :], ps[0:128, 0:g * L].rearrange("p (g k) -> p g k", k=L), -1.0, ALU.mult, [ps], [AhatT])

                        batched_chunk_mm(p, nb, lambda n: kbg[:, n, :], lambda n: Pm[:, n, :], Psubs + [kbg], 128, L, ev_w)
                        act(p, grow[:, :], grow[:, :], AF.Exp, [grow], [grow])
                        tt(p, grow[:, :], grow[:, :], q_fm[:, lo * L:hi * L], ALU.mult, [grow, q_fm], [grow])
                        Yo = p.sb("gd_Yo", [L, nb, DH])
                        order = list(range(nb)) if d == 0 else list(range(nb - 1, -1, -1))
                        dcy = p.sb("gd_dcy", [128, nb])
                        cp(p, dcy[:, :], decay[:, lo:hi], [decay], [dcy], "dve")
                        chain = dict(dk=128, dvp=DH, order=order, QT=grow, A1T=wA, BL=BL, decay=dcy, U0=U0, AhatT=AhatT, Y=Yo,
                                     K0=None, M=Ms, ub=ubs, R=[], st=st)
                        chunk_loops(p, [chain])
                        Ysubs = [Yo.sub(n) for n in range(nb)]
                        if d == 0:
                            cp(p, O[:, lo:hi, :], Yo[:, :, :], Ysubs, [O], "dve")
                        else:
                            tt(p, O[:, lo:hi, :], O[:, lo:hi, :], Yo[:, :, :], ALU.add, Ysubs + [O], [O])
        p.gd_big.close()
        with p.scope():
            sq = p.sb("gd_sq", [L, nch, DH]); ss = p.sb("gd_ss", [L, nch]); z = p.sb("gd_z", [L, nch, DH])
            p.dma("sp", z[:, :, :], io["z_tm"], writes=[z])
            tt(p, sq[:, :, :], O[:, :, :], O[:, :, :], ALU.mult, [O], [sq])
            p.op("dve", lambda e: e.tensor_reduce(out=ss[:, :], in_=sq[:, :, :], axis=AX.X, op=ALU.add), reads=[sq], writes=[ss])
            act(p, ss[:, :], ss[:, :], AF.Sqrt, [ss, c["eps6"]], [ss], scale=1.0 / DH, bias=c["eps6"][0:L, 0:1])
            p.op("dve", lambda e: e.reciprocal(out=ss[:, :], in_=ss[:, :]), reads=[ss], writes=[ss])
            tt(p, O[:, :, :], O[:, :, :], bc_last(ss[:, :], DH), ALU.mult, [O, ss], [O])
            tt(p, O[:, :, :], O[:, :, :], bc_mid(ngb[:, :], nch), ALU.mult, [O, ngb], [O])
            act(p, z[:, :, :], z[:, :, :], AF.Silu, [z], [z])
            tt(p, O[:, :, :], O[:, :, :], z[:, :, :], ALU.mult, [O, z], [O])
            p.dma("pool", io["yc"], O[:, :, :], reads=[O], is_output=True)


def emit_rwkv(p, c, io, nc_ctx, nc_lat, bs=16, heads=(0, 1)):
    nch = nc_ctx + nc_lat
    T = nch * L
    Tc, Tl = nc_ctx * L, nc_lat * L
    N = 64

    def shift3(dst_ap, raw, P_, TB, mu0, mu1, c0, R, W):
        ts(p, dst_ap, raw[0:P_, 1:1 + TB], c0, ALU.mult, R, W)
        stt(p, dst_ap, raw[0:P_, 0:TB], mu0, dst_ap, ALU.mult, ALU.add, R + W, W)
        stt(p, dst_ap, raw[0:P_, 2:2 + TB], mu1, dst_ap, ALU.mult, ALU.add, R + W, W)

    def coef(mu_tn, ncol, P_, name):
        co = p.sb(name, [P_, ncol // 2])
        for i in range(ncol // 2):
            tt(p, co[:, i:i + 1], mu_tn[:, 2 * i:2 * i + 1], mu_tn[:, 2 * i + 1:2 * i + 2], ALU.add, [mu_tn], [co])
        ts(p, co[:, :], co[:, :], -1.0, ALU.mult, [co], [co], s2=1.0, op1=ALU.add)
        return co

    def poff(lo):
        return lo * L if lo < nc_ctx else (Tc + 2) + (lo - nc_ctx) * L

    with p.scope():
        muw = p.sb("rw_muw", [96, 2]); mua = p.sb("rw_mua", [96, 2]); mug = p.sb("rw_mug", [128, 4])
        p.dma("sp", muw[:, :], io["muw"], writes=[muw]); p.dma("sp", mua[:, :], io["mua"], writes=[mua])
        p.dma("sp", mug[:, :], io["mug"].rearrange("p a b -> p (a b)"), writes=[mug])
        cw_ = coef(muw, 2, 96, "rw_cw"); ca_ = coef(mua, 2, 96, "rw_ca"); cg_ = coef(mug, 4, 128, "rw_cg")
        for h in heads:
            with p.scope():
                mu = p.sb("rw_mu", [N, 6]); pr = p.sb("rw_pr", [N, 9]); w2 = p.sb("rw_w2", [96, 2, N]); a2 = p.sb("rw_a2", [96, 2, N])
                g2 = p.sb("rw_g2", [128, 2, N])
                p.dma("sp", mu[:, :], io[f"mu{h}"], writes=[mu]); p.dma("sp", pr[:, :], io[f"pr{h}"], writes=[pr])
                p.dma("sp", w2[:, :, :], io[f"w2_{h}"], writes=[w2]); p.dma("sp", a2[:, :, :], io[f"a2_{h}"], writes=[a2])
                p.dma("sp", g2[:, :, :], io[f"g2_{h}"], writes=[g2])
                cm = coef(mu, 6, N, "rw_cm")
                omka = p.sb("rw_omka", [N, 1])
                ts(p, omka[:, :], pr[:, 5:6], -1.0, ALU.mult, [pr], [omka], s2=1.0, op1=ALU.add)
                O = p.sb("rw_O", [L, nch, N]); aux0 = p.sb("rw_aux0", [N, T]); gg = p.sb("rw_gg", [N, T])
                for d in range(2):
                    mT_incl, mT_str, m_incl, m_str = dir_masks(c, d)
                    Ms = [p.sb(f"rw_M{i}", [N, N]) for i in range(2)]
                    ubs = [p.sb(f"rw_u{i}", [L, N]) for i in range(2)]
                    st = {}
                    for (lo, hi) in blocks_for(d, nc_ctx, nc_lat, bs):
                        nb = hi - lo
                        TB = nb * L
                        off = poff(lo)
                        with p.scope():
                            at = p.sb("rw_at", [N, TB]); bt = p.sb("rw_bt", [N, TB]); kt = p.sb("rw_kt", [N, TB]); rt = p.sb("rw_rt", [N, TB])
                            bL = p.sb("rw_bL", [N, TB]); kL = p.sb("rw_kL", [N, TB]); v = p.sb("rw_v", [N, TB]); WL = p.sb("rw_WL", [N, nb])
                            with p.scope():
                                raw = p.sb("rw_raw", [128, TB + 2]); raw2 = p.sb("rw_raw2", [128, TB + 2])
                                r = p.sb("rw_r", [N, TB]); k = p.sb("rw_k", [N, TB])
                                xw = p.sb("rw_xw", [96, TB]); xa = p.sb("rw_xa", [96, TB])
                                for nm, dst, ci in (("rp", r, 0), ("kp", k, 1), ("vp", v, 2)):
                                    p.dma("sp", raw[0:N, :], io[f"{nm}{h}"][:, off:off + TB + 2], writes=[raw])
                                    shift3(dst[:, :], raw, N, TB, mu[:, 2 * ci:2 * ci + 1], mu[:, 2 * ci + 1:2 * ci + 2], cm[:, ci:ci + 1], [raw, mu, cm], [dst])
                                p.dma("sp", raw[0:96, :], io["xwp"][:, off:off + TB + 2], writes=[raw])
                                shift3(xw[:, :], raw, 96, TB, muw[:, 0:1], muw[:, 1:2], cw_[:, 0:1], [raw, muw, cw_], [xw])
                                act(p, xw[:, :], xw[:, :], AF.Tanh, [xw], [xw])
                                p.dma("sp", raw2[0:96, :], io["xap"][:, off:off + TB + 2], writes=[raw2])
                                shift3(xa[:, :], raw2, 96, TB, mua[:, 0:1], mua[:, 1:2], ca_[:, 0:1], [raw2, mua, ca_], [xa])
                                logw = p.sb("rw_logw", [N, TB]); iclr = p.sb("rw_iclr", [N, TB]); kk = p.sb("rw_kk", [N, TB]); tmp = p.sb("rw_tmp", [N, TB])
                                kmod = p.sb("rw_kmod", [N, TB]); b_ = p.sb("rw_b", [N, TB])

                                def lora(dst, wt, dd, xin, bias_col):
                                    for c0 in range(0, TB, 512):
                                        c1 = min(c0 + 512, TB)
                                        ps = p.next_ps()
                                        mm(p, ps[0:N, 0:c1 - c0], wt[:, dd, :], xin[:, c0:c1], [wt, xin], [ps])
                                        act(p, dst[:, c0:c1], ps[0:N, 0:c1 - c0], AF.Sigmoid, [ps, pr], [dst], bias=pr[:, bias_col:bias_col + 1])

                                lora(logw, w2, d, xw, 0 + d)
                                ts(p, logw[:, :], logw[:, :], -0.6065306597126334, ALU.mult, [logw], [logw])
                                lora(iclr, a2, d, xa, 2 + d)
                                ts(p, kk[:, :], k[:, :], pr[:, 4:5], ALU.mult, [k, pr], [kk])
                                act(p, tmp[:, :], kk[:, :], AF.Square, [kk], [tmp])
                                for c0 in range(0, TB, 512):
                                    c1 = min(c0 + 512, TB)
                                    ps = p.next_ps()
                                    mm(p, ps[0:N, 0:c1 - c0], c["ones"][0:N, 0:N], tmp[:, c0:c1], [c["ones"], tmp], [ps])
                                    act(p, tmp[:, c0:c1], ps[0:N, 0:c1 - c0], AF.Sqrt, [ps, c["eps6"]], [tmp], bias=c["eps6"][0:N, 0:1])
                                p.op("dve", lambda e: e.reciprocal(out=tmp[:, :], in_=tmp[:, :]), reads=[tmp], writes=[tmp])
                                tt(p, kk[:, :], kk[:, :], tmp[:, :], ALU.mult, [kk, tmp], [kk])
                                ts(p, kmod[:, :], iclr[:, :], pr[:, 5:6], ALU.mult, [iclr, pr, omka], [kmod], s2=omka[:, 0:1], op1=ALU.add)
                                tt(p, kmod[:, :], kmod[:, :], k[:, :], ALU.mult, [kmod, k], [kmod])
                                tt(p, b_[:, :], kk[:, :], iclr[:, :], ALU.mult, [kk, iclr], [b_])
                                if d == 0:
                                    icl1 = p.sb("rw_icl1", [N, TB])
                                    lora(icl1, a2, 1, xa, 3)
                                    ts(p, icl1[:, :], icl1[:, :], pr[:, 5:6], ALU.mult, [icl1, pr, omka], [icl1], s2=omka[:, 0:1], op1=ALU.add)
                                    tt(p, icl1[:, :], icl1[:, :], k[:, :], ALU.mult, [icl1, k], [icl1])
                                    tt(p, icl1[:, :], icl1[:, :], kmod[:, :], ALU.add, [icl1, kmod], [icl1])
                                    stt(p, icl1[:, :], r[:, :], pr[:, 6:7], icl1[:, :], ALU.mult, ALU.mult, [r, pr, icl1], [icl1])
                                    xg = p.sb("rw_xg", [128, 2, TB])
                                    for kc in range(2):
                                        p.dma("sp", raw2[:, :], io["xgp"][:, kc, off:off + TB + 2], writes=[raw2])
                                        shift3(xg[:, kc, :], raw2, 128, TB, mug[:, 2 * kc:2 * kc + 1], mug[:, 2 * kc + 1:2 * kc + 2], cg_[:, kc:kc + 1],
                                               [raw2, mug, cg_], [xg])
                                    act(p, xg[:, :, :], xg[:, :, :], AF.Sigmoid, [xg], [xg])
                                    for c0 in range(0, TB, 512):
                                        c1 = min(c0 + 512, TB)
                                        ps = p.next_ps()
                                        mm(p, ps[0:N, 0:c1 - c0], c["ones"][0:N, 0:N], icl1[:, c0:c1], [c["ones"], icl1], [ps])
                                        tt(p, aux0[:, lo * L + c0:lo * L + c1], ps[0:N, 0:c1 - c0], v[:, c0:c1], ALU.mult, [ps, v], [aux0])
                                        ps2 = p.next_ps()
                                        for kc in range(2):
                                            mm(p, ps2[0:N, 0:c1 - c0], g2[:, kc, :], xg[:, kc, c0:c1], [g2, xg], [ps2], start=(kc == 0), stop=(kc == 1))
                                        cp(p, gg[:, lo * L + c0:lo * L + c1], ps2[0:N, 0:c1 - c0], [ps2], [gg], "act")
                                rmask = p.sb("rw_rmask", [N, nb, L]); lW = p.sb("rw_lW", [N, TB]); e1 = p.sb("rw_e1", [N, TB])
                                p.op("pool", lambda e: e.memset(rmask[:, :, :], 1.0), writes=[rmask])
                                p.op("pool", lambda e: e.memset(rmask[:, :, 0:1], 0.0), writes=[rmask])
                                p.op("dve", lambda e: e.tensor_tensor_scan(out=lW[:, :], data0=rmask[:, :, :].rearrange("p n t -> p (n t)"),
                                                                          data1=logw[:, :], initial=0.0, op0=ALU.mult, op1=ALU.add),
                                     reads=[rmask, logw], writes=[lW])
                                lW3 = lW[:, :].rearrange("p (n t) -> p n t", t=L)
                                tot = p.sb("rw_tot", [N, nb])
                                cp(p, tot[:, :], lW3[:, :, L - 1], [lW], [tot], "dve")
                                if d == 1:
                                    tt(p, lW3, bc_last(tot[:, :], L), lW3, ALU.subtract, [tot, lW], [lW])
                                    tt(p, lW[:, :], lW[:, :], logw[:, :], ALU.add, [lW, logw], [lW])
                                act(p, WL[:, :], tot[:, :], AF.Exp, [tot], [WL])
                                tt(p, e1[:, :], lW[:, :], logw[:, :], ALU.subtract, [lW, logw], [e1])
                                act(p, e1[:, :], e1[:, :], AF.Exp, [e1], [e1])
                                stt(p, at[:, :], kk[:, :], -1.0, e1[:, :], ALU.mult, ALU.mult, [kk, e1], [at])
                                act(p, e1[:, :], lW[:, :], AF.Exp, [lW], [e1], scale=-1.0)
                                tt(p, bt[:, :], b_[:, :], e1[:, :], ALU.mult, [b_, e1], [bt])
                                tt(p, kt[:, :], kmod[:, :], e1[:, :], ALU.mult, [kmod, e1], [kt])
                                act(p, e1[:, :], lW[:, :], AF.Exp, [lW], [e1])
                                tt(p, rt[:, :], r[:, :], e1[:, :], ALU.mult, [r, e1], [rt])
                                tt(p, e1[:, :].rearrange("p (n t) -> p n t", t=L), bc_last(tot[:, :], L), lW3, ALU.subtract, [tot, lW], [e1])
                                act(p, e1[:, :], e1[:, :], AF.Exp, [e1], [e1])
                                tt(p, bL[:, :], b_[:, :], e1[:, :], ALU.mult, [b_, e1], [bL])
                                tt(p, kL[:, :], kmod[:, :], e1[:, :], ALU.mult, [kmod, e1], [kL])
                            ch = lambda a, n: a[:, n * L:(n + 1) * L]
                            X = p.sb("rw_X", [L, nb, L]); Y = p.sb("rw_Y", [L, nb, L]); AakT = p.sb("rw_Aak", [L, nb, L])
                            ArbT = p.sb("rw_Arb", [L, nb, L]); ArkT = p.sb("rw_Ark", [L, nb, L])

                            def mk(dst, la, ra, mask):
                                def ev(ps, n0, g):
                                    tt(p, dst[:, n0:n0 + g, :], ps[0:L, 0:g * L].rearrange("p (g k) -> p g k", k=L), bc_mid(mask[0:L, 0:L], g), ALU.mult,
                                       [ps, mask], [dst])
                                batched_chunk_mm(p, nb, lambda n: ch(la, n), lambda n: ch(ra, n), [la, ra], L, L, ev)

                            mk(X, at, bt, m_str); mk(Y, bt, at, mT_str); mk(AakT, kt, at, mT_str); mk(ArbT, bt, rt, mT_incl); mk(ArkT, kt, rt, mT_incl)
                            at_tm = p.sb("rw_attm", [L, nb, N]); bL_tm = p.sb("rw_bLtm", [L, nb, N]); kL_tm = p.sb("rw_kLtm", [L, nb, N]); v_tm = p.sb("rw_vtm", [L, nb, N])
                            for src, dst in ((at, at_tm), (bL, bL_tm), (kL, kL_tm), (v, v_tm)):
                                transpose_chunks(p, c, (lambda s_: (lambda n: ch(s_, n)))(src), [src], N, dst, nb)
                            tmpi = {k_: p.sb("rw_ti" + k_, [L, nb, L]) for k_ in ("P", "Q", "X2", "Y2")}
                            Pm, Psubs = tri_inverse_T(p, c, X, Y, nb, tmpi)
                            cAk = p.sb("rw_cAk", [L, nb, N]); U0 = p.sb("rw_U0", [L, nb, N]); AhatT = p.sb("rw_Ah", [N, nb, L]); Y0 = p.sb("rw_Y0", [L, nb, N])

                            def evto(dst, rows, eng):
                                def ev(ps, n0, g):
                                    w_ = dst.t.shape[2]
                                    cp(p, dst[:, n0:n0 + g, :], ps[0:rows, 0:g * w_].rearrange("p (g k) -> p g k", k=w_), [ps], [dst], eng)
                                return ev

                            batched_chunk_mm(p, nb, lambda n: AakT[:, n, :], lambda n: v_tm[:, n, :], [AakT, v_tm], L, N, evto(cAk, L, "act"))
                            batched_chunk_mm(p, nb, lambda n: Pm[:, n, :], lambda n: cAk[:, n, :], Psubs + [cAk], L, N, evto(U0, L, "dve"))
                            batched_chunk_mm(p, nb, lambda n: at_tm[:, n, :], lambda n: Pm[:, n, :], Psubs + [at_tm], N, L, evto(AhatT, N, "act"))
                            batched_chunk_mm(p, nb, lambda n: ArkT[:, n, :], lambda n: v_tm[:, n, :], [ArkT, v_tm], L, N, evto(Y0, L, "dve"))
                            Yo = p.sb("rw_Yo", [L, nb, N])
                            order = list(range(nb)) if d == 0 else list(range(nb - 1, -1, -1))
                            chain = dict(dk=N, dvp=N, order=order, QT=rt, A1T=ArbT, BL=bL_tm, decay=WL, U0=U0, AhatT=AhatT, Y=Yo,
                                         K0=(kL_tm, v_tm), M=Ms, ub=ubs, R=[], st=st)
                            chunk_loops(p, [chain])
                            Ysubs = [Yo.sub(n) for n in range(nb)]
                            if d == 0:
                                tt(p, O[:, lo:hi, :], Yo[:, :, :], Y0[:, :, :], ALU.add, Ysubs + [Y0], [O])
                            else:
                                tt(p, Yo[:, :, :], Yo[:, :, :], Y0[:, :, :], ALU.add, Ysubs + [Y0], [Yo])
                                tt(p, O[:, lo:hi, :], O[:, lo:hi, :], Yo[:, :, :], ALU.add, [Yo, O], [O])
                with p.scope():
                    Of = p.sb("rw_Of", [N, nch, L]); sq = p.sb("rw_fsq", [N, T]); mean = p.sb("rw_mean", [N, T])
                    g5 = p.sb("rw_g5", [N, 1])
                    p.op("dve", lambda e: e.memset(g5[:, :], 64e-5), writes=[g5])
                    transpose_chunks(p, c, lambda n: O[:, n, :], [O], L, Of, nch)
                    Off = Of[:, :, :].rearrange("p n t -> p (n t)")
                    for c0 in range(0, T, 512):
                        c1 = min(c0 + 512, T)
                        ps = p.next_ps()
                        mm(p, ps[0:N, 0:c1 - c0], c["ones"][0:N, 0:N], Off[:, c0:c1], [c["ones"], Of], [ps])
                        stt(p, mean[:, c0:c1], ps[0:N, 0:c1 - c0], -1.0 / N, Off[:, c0:c1], ALU.mult, ALU.add, [ps, Of], [mean])
                    act(p, sq[:, :], mean[:, :], AF.Square, [mean], [sq])
                    for c0 in range(0, T, 512):
                        c1 = min(c0 + 512, T)
                        ps = p.next_ps()
                        mm(p, ps[0:N, 0:c1 - c0], c["ones"][0:N, 0:N], sq[:, c0:c1], [c["ones"], sq], [ps])
                        act(p, sq[:, c0:c1], ps[0:N, 0:c1 - c0], AF.Sqrt, [ps, g5], [sq], scale=1.0 / N, bias=g5[:, 0:1])
                    p.op("dve", lambda e: e.reciprocal(out=sq[:, :], in_=sq[:, :]), reads=[sq], writes=[sq])
                    tt(p, mean[:, :], mean[:, :], sq[:, :], ALU.mult, [mean, sq], [mean])
                    ts(p, mean[:, :], mean[:, :], pr[:, 7:8], ALU.mult, [mean, pr], [mean], s2=pr[:, 8:9], op1=ALU.add)
                    tt(p, mean[:, :], mean[:, :], aux0[:, :], ALU.add, [mean, aux0], [mean])
                    tt(p, mean[:, :], mean[:, :], gg[:, :], ALU.mult, [mean, gg], [mean])
                    p.dma("pool", io[f"yd{h}"], mean[:, :], reads=[mean], is_output=True)


D = 2048
KC = 16
DFF = 5632
FC = DFF // 128
TL = 1024
TCX = 64
TT = TL + TCX
SEGS = [(0, TL, 0), (TL, TT, 1)]


def ttiles(T, w=512):
    return [(t0, min(t0 + w, T)) for t0 in range(0, T, w)]


def load_fm(p, q, dst, src_ap, kc_n):
    v = src_ap.rearrange("(kc p) t -> p kc t", p=128)
    for kc in range(kc_n):
        p.dma(q, dst[:, kc, :], v[:, kc, :], writes=[dst.sub(kc)])


def rms_stats(p, c, x, xsub, T, sq, rstd, eps_ap, kc_n=KC, dim=D):
    for kc in range(kc_n):
        act(p, sq[:, kc, :], x[:, kc, :], AF.Square, [xsub(kc)], [sq.sub(kc)])
    for ti, (t0, t1) in enumerate(ttiles(T)):
        ps = p.next_ps()
        for kc in range(kc_n):
            mm(p, ps[:, 0:t1 - t0], c["ones"][:, :], sq[:, kc, t0:t1], [c["ones"], sq.sub(kc)], [ps], start=(kc == 0), stop=(kc == kc_n - 1))
        act(p, rstd[:, t0:t1], ps[:, 0:t1 - t0], AF.Sqrt, [ps], [rstd], scale=1.0 / dim, bias=eps_ap)
    p.op("dve", lambda e: e.reciprocal(out=rstd[:, :], in_=rstd[:, :]), reads=[rstd], writes=[rstd])


def mod_prep(p, mv, gs, sh, gi, sci, shi):
    for j in range(2):
        ts(p, gs[:, :, j:j + 1], mv[:, :, sci[j]:sci[j] + 1], 1.0, ALU.add, [mv], [gs])
        tt(p, gs[:, :, j:j + 1], gs[:, :, j:j + 1], mv[:, :, gi:gi + 1], ALU.mult, [mv, gs], [gs])
        cp(p, sh[:, :, j:j + 1], mv[:, :, shi[j]:shi[j] + 1], [mv], [sh], "dve")


def norm_mod(p, c, x, xsub, h, T, segs, gs, sh, rstd, out=None):
    out = h if out is None else out
    for kc in range(KC):
        tt(p, h[:, kc, :], x[:, kc, :], rstd[:, 0:T], ALU.mult, [xsub(kc), rstd], [h.sub(kc)])
        for (t0, t1, col) in segs:
            act(p, out[:, kc, t0:t1], h[:, kc, t0:t1], AF.Identity, [h.sub(kc), gs, sh], [out.sub(kc)],
                scale=gs[:, kc, col:col + 1], bias=sh[:, kc, col:col + 1])


def rms_stats2(p, c, x, xsub, T, rstd, eps_ap, sqbufs, kc_n=KC, dim=D, t_off=0):
    tts = ttiles(T)
    pss = [p.next_ps() for _ in tts]
    for kc in range(kc_n):
        sq = sqbufs[kc % len(sqbufs)]
        act(p, sq[:, 0:T], x[:, kc, t_off:t_off + T], AF.Square, [xsub(kc)], [sq])
        for ps, (t0, t1) in zip(pss, tts):
            mm(p, ps[:, 0:t1 - t0], c["ones"][:, :], sq[:, t0:t1], [c["ones"], sq], [ps], start=(kc == 0), stop=(kc == kc_n - 1))
    for ps, (t0, t1) in zip(pss, tts):
        act(p, rstd[:, t0:t1], ps[:, 0:t1 - t0], AF.Sqrt, [ps], [rstd], scale=1.0 / dim, bias=eps_ap)
    p.op("dve", lambda e: e.reciprocal(out=rstd[:, 0:T], in_=rstd[:, 0:T]), reads=[rstd], writes=[rstd])


def gemm_fm(p, h, hsub, kc_n, tts, W_ap, N, wbufs, evac, q="sp", NB=256, wkey=[0], kc_off=0):
    Wv = W_ap.rearrange("(kc p) n -> p kc n", p=128)
    nblk = (N + NB - 1) // NB
    for nb in range(nblk):
        n0 = nb * NB
        nsz = min(NB, N - n0)
        wb = wbufs[wkey[0] % len(wbufs)]
        wkey[0] += 1
        half = kc_n // 2
        p.dma(q, wb[:, 0:half, 0:nsz], Wv[:, 0:half, n0:n0 + nsz], writes=[wb.sub(0)])
        p.dma(q, wb[:, half:kc_n, 0:nsz], Wv[:, half:kc_n, n0:n0 + nsz], writes=[wb.sub(1)])
        for mo in range(0, nsz, 128):
            msz = min(128, nsz - mo)
            for (t0, t1) in tts:
                ps = p.next_ps()
                for kc in range(kc_n):
                    mm(p, ps[0:msz, 0:t1 - t0], wb[:, kc, mo:mo + msz], h[:, kc_off + kc, t0:t1],
                       [wb.sub(0 if kc < half else 1), hsub(kc_off + kc)], [ps], start=(kc == 0), stop=(kc == kc_n - 1))
                evac(n0 + mo, msz, t0, t1, ps)


def cast_engine(i):
    return ("act", "dve")[i % 2]


def gemm_bf(p, h, hsub, kc_n, tts, W_ap, N, wst, wbf, evac, q="sp", NB=256, wkey=[0], kc_off=0):
    Wv = W_ap.rearrange("(kc p) n -> p kc n", p=128)
    nblk = (N + NB - 1) // NB
    for nb in range(nblk):
        n0 = nb * NB
        nsz = min(NB, N - n0)
        ws = wst[wkey[0] % len(wst)]
        wb = wbf[wkey[0] % len(wbf)]
        half = kc_n // 2
        p.dma(q, ws[:, 0:half, 0:nsz], Wv[:, 0:half, n0:n0 + nsz], writes=[ws.sub(0)])
        p.dma(q, ws[:, half:kc_n, 0:nsz], Wv[:, half:kc_n, n0:n0 + nsz], writes=[ws.sub(1)])
        cp(p, wb[:, 0:half, 0:nsz], ws[:, 0:half, 0:nsz], [ws.sub(0)], [wb.sub(0)], cast_engine(wkey[0]))
        cp(p, wb[:, half:kc_n, 0:nsz], ws[:, half:kc_n, 0:nsz], [ws.sub(1)], [wb.sub(1)], cast_engine(wkey[0] + 1))
        wkey[0] += 1
        for mo in range(0, nsz, 128):
            msz = min(128, nsz - mo)
            for (t0, t1) in tts:
                ps = p.next_ps()
                for kc in range(kc_n):
                    mm(p, ps[0:msz, 0:t1 - t0], wb[:, kc, mo:mo + msz], h[:, kc_off + kc, t0:t1],
                       [wb.sub(0 if kc < half else 1), hsub(kc_off + kc)], [ps], start=(kc == 0), stop=(kc == kc_n - 1))
                evac(n0 + mo, msz, t0, t1, ps)


def build_mod(NCOL=6144):
    p = P()
    p.init_ps(8)
    c = make_consts(p)
    cT = p.dram("cT", [D, 3], "ExternalInput"); W = p.dram("W", [D, NCOL], "ExternalInput"); b = p.dram("b", [128, NCOL // 128], "ExternalInput")
    out = p.dram("out", [128, NCOL // 128, 3], "ExternalOutput")
    cs = p.sb("cs", [128, KC, 3]); bs = p.sb("bs", [128, NCOL // 128]); o = p.sb("o", [128, NCOL // 128, 3])
    wbufs = [p.sb(f"wb{i}", [128, KC, 256]) for i in range(3)]
    p.dma("sp", cs[:, :, :], cT.rearrange("(kc p) t -> p kc t", p=128), writes=[cs])
    p.dma("sp", bs[:, :], b, writes=[bs])
    sg = p.sb("sg", [128, KC, 3])
    act(p, sg[:, :, :], cs[:, :, :], AF.Sigmoid, [cs], [sg])
    tt(p, cs[:, :, :], cs[:, :, :], sg[:, :, :], ALU.mult, [cs, sg], [cs])

    def evac(m0, msz, t0, t1, ps):
        j = m0 // 128
        ts(p, o[:, j, :], ps[:, 0:3], bs[:, j:j + 1], ALU.add, [ps, bs], [o])

    gemm_fm(p, cs, lambda kc: cs, KC, [(0, 3)], W, NCOL, wbufs, evac)
    p.dma("pool", out, o[:, :, :], reads=[o], is_output=True)
    p.finish("sp"); p.close()
    return p


def build_pre(N=15328):
    p = P()
    p.init_ps(8)
    c = make_consts(p)
    xT = p.dram("xT", [D, TT], "ExternalInput"); W = p.dram("W", [D, N], "ExternalInput")
    mvd = p.dram("mv", [128, KC, 5], "ExternalInput")
    zT = p.dram("zT", [N, TT], "ExternalOutput")
    hb = p.sb("hb", [128, KC, TT], BF16)
    mv = p.sb("mvs", [128, KC, 5]); gs = p.sb("gs", [128, KC, 2]); sh = p.sb("sh", [128, KC, 2])
    p.dma("sp", mv[:, :, :], mvd, writes=[mv])
    mod_prep(p, mv, gs, sh, 0, (1, 3), (2, 4))
    with p.scope():
        x = p.sb("x", [128, KC, TT]); rstd = p.sb("rstd", [128, TT]); sqb = [p.sb(f"sqb{i}", [128, TT]) for i in range(2)]
        load_fm(p, "sp", x, xT, KC)
        rms_stats2(p, c, x, x.sub, TT, rstd, c["eps6"][:, 0:1], sqb)
        norm_mod(p, c, x, x.sub, x, TT, SEGS, gs, sh, rstd, out=hb)
    wst = [p.sb(f"ws{i}", [128, KC, 256]) for i in range(3)]
    wbf = [p.sb(f"wb{i}", [128, KC, 256], BF16) for i in range(3)]
    obufs = [p.sb(f"ob{i}", [128, TT]) for i in range(3)]
    oi = [0]
    tts = ttiles(TT)

    def evac(m0, msz, t0, t1, ps):
        ob = obufs[oi[0] % 3]
        cp(p, ob[0:msz, t0:t1], ps[0:msz, 0:t1 - t0], [ps], [ob.sub(t0)], "dve" if (t0 // 512) % 2 == 0 else "act")
        if t1 == TT:
            p.dma("pool", zT[m0:m0 + msz, :], ob[0:msz, :], reads=[ob.sub(t[0]) for t in tts], is_output=True)
            oi[0] += 1

    gemm_bf(p, hb, hb.sub, KC, tts, W, N, wst, wbf, evac)
    p.finish("sp"); p.close()
    return p


def build_post():
    p = P()
    p.init_ps(8)
    c = make_consts(p)
    PADC = 15
    LA = TL + 2 * PADC; LC = TCX + 2 * PADC
    ap_d = p.dram("a_p", [512, LA + LC], "ExternalInput"); gp_d = p.dram("g_p", [512, LA + LC], "ExternalInput")
    cvw = p.dram("cvw", [128, 4, 34], "ExternalInput")
    ybcd = p.dram("ybcd", [1536, TT], "ExternalInput")
    gates = p.dram("gates", [8192, TT], "ExternalInput")
    xT = p.dram("xT", [D, TT], "ExternalInput")
    wbr = p.dram("wbr", [4, 512, D], "ExternalInput"); wout = p.dram("wout", [D, D], "ExternalInput")
    mvd = p.dram("mv", [128, KC, 3], "ExternalInput")
    x1T = p.dram("x1T", [D, TT], "ExternalOutput")
    tts = ttiles(TT)
    mv = p.sb("mvs", [128, KC, 3]); gg = p.sb("gg", [128, KC, 2])
    p.dma("sp", mv[:, :, :], mvd, writes=[mv])
    for j in range(2):
        tt(p, gg[:, :, j:j + 1], mv[:, :, 0:1], mv[:, :, 1 + j:2 + j], ALU.mult, [mv], [gg])
    Yb = p.sb("Yall", [128, 16, TT], BF16)
    merged_b = p.sb("merged_b", [128, KC, TT], BF16)
    with p.scope():
        ya = p.sb("ya", [128, 4, TT])
        cw = p.sb("cw", [128, 4, 34])
        p.dma("sp", cw[:, :, :], cvw, writes=[cw])
        a = p.sb("cv_a", [128, 4, LA + LC]); g = p.sb("cv_g", [128, 4, LA + LC])
        p.dma("sp", a[:, :, :], ap_d.rearrange("(kc p) t -> p kc t", p=128), writes=[a])
        p.dma("sp", g[:, :, :], gp_d.rearrange("(kc p) t -> p kc t", p=128), writes=[g])
        act(p, g[:, :, :], g[:, :, :], AF.Sigmoid, [g], [g])
        tt(p, a[:, :, :], a[:, :, :], g[:, :, :], ALU.mult, [a, g], [a])
        for kc in range(4):
            for (o_in, o_out, tl) in ((0, 0, TL), (LA, TL, TCX)):
                dst = ya[:, kc, o_out:o_out + tl]
                ts(p, dst, a[:, kc, o_in:o_in + tl], cw[:, kc, 0:1], ALU.mult, [a, cw], [ya.sub(kc)], s2=cw[:, kc, 31:32], op1=ALU.add)
                for j in range(1, 31):
                    stt(p, dst, a[:, kc, o_in + j:o_in + j + tl], cw[:, kc, j:j + 1], dst, ALU.mult, ALU.add, [a, cw, ya.sub(kc)], [ya.sub(kc)])
        xc = p.sb("cv_xc", [128, 4, TT]); sq = g; rs = p.sb("cv_rs", [128, TT])
        for (t0, t1) in tts:
            ps = p.next_ps()
            for kc in range(4):
                mm(p, ps[:, 0:t1 - t0], c["ones"][:, :], ya[:, kc, t0:t1], [c["ones"], ya.sub(kc)], [ps], start=(kc == 0), stop=(kc == 3))
            for kc in range(4):
                stt(p, xc[:, kc, t0:t1], ps[:, 0:t1 - t0], -1.0 / 512, ya[:, kc, t0:t1], ALU.mult, ALU.add, [ps, ya.sub(kc)], [xc.sub(kc)])
        for kc in range(4):
            act(p, sq[:, kc, 0:TT], xc[:, kc, :], AF.Square, [xc.sub(kc)], [sq])
        for (t0, t1) in tts:
            ps = p.next_ps()
            for kc in range(4):
                mm(p, ps[:, 0:t1 - t0], c["ones"][:, :], sq[:, kc, t0:t1], [c["ones"], sq], [ps], start=(kc == 0), stop=(kc == 3))
            act(p, rs[:, t0:t1], ps[:, 0:t1 - t0], AF.Sqrt, [ps], [rs], scale=1.0 / 512, bias=c["eps6"][:, 0:1])
        p.op("dve", lambda e: e.reciprocal(out=rs[:, :], in_=rs[:, :]), reads=[rs], writes=[rs])
        for kc in range(4):
            tt(p, xc[:, kc, :], xc[:, kc, :], rs[:, :], ALU.mult, [xc.sub(kc), rs], [xc.sub(kc)])
            act(p, xc[:, kc, :], xc[:, kc, :], AF.Identity, [xc.sub(kc), cw], [xc.sub(kc)], scale=cw[:, kc, 32:33], bias=cw[:, kc, 33:34])
            act(p, Yb[:, kc, :], xc[:, kc, :], AF.Silu, [xc.sub(kc)], [Yb.sub(kc)])
    with p.scope():
        merged = p.sb("merged", [128, KC, TT])
        stg = [p.sb(f"ystg{i}", [128, TT]) for i in range(2)]
        yv = ybcd.rearrange("(kc p) t -> p kc t", p=128)
        for kc in range(12):
            sb_ = stg[kc % 2]
            p.dma("sp", sb_[:, :], yv[:, kc, :], writes=[sb_])
            cp(p, Yb[:, 4 + kc, :], sb_[:, :], [sb_], [Yb.sub(4 + kc)], cast_engine(kc))
        wst = [p.sb(f"ws{i}", [128, 4, 256]) for i in range(3)]
        wbf = [p.sb(f"wb{i}", [128, 4, 256], BF16) for i in range(3)]
        gbufs = [p.sb(f"gb{i}", [128, TT]) for i in range(3)]
        gv = gates.rearrange("(n kc p) t -> n p kc t", p=128, kc=KC)
        gi = [0]
        for n in range(4):
            def evac(m0, msz, t0, t1, ps, n=n):
                kc = m0 // 128
                gb = gbufs[gi[0] % 3]
                if t0 == 0:
                    p.dma("pool", gb[:, :], gv[n, :, kc, :], writes=[gb])
                    act(p, gb[:, :], gb[:, :], AF.Sigmoid, [gb], [gb])
                if n == 0:
                    tt(p, merged[:, kc, t0:t1], ps[:, 0:t1 - t0], gb[:, t0:t1], ALU.mult, [ps, gb], [merged.sub(kc)])
                else:
                    tt(p, gb[:, t0:t1], ps[:, 0:t1 - t0], gb[:, t0:t1], ALU.mult, [ps, gb], [gb])
                    if n < 3:
                        tt(p, merged[:, kc, t0:t1], merged[:, kc, t0:t1], gb[:, t0:t1], ALU.add, [gb, merged.sub(kc)], [merged.sub(kc)], eng="pool")
                    else:
                        tt(p, merged_b[:, kc, t0:t1], merged[:, kc, t0:t1], gb[:, t0:t1], ALU.add, [gb, merged.sub(kc)], [merged_b.sub(kc)], eng="pool")
                if t1 == TT:
                    gi[0] += 1

            gemm_bf(p, Yb, Yb.sub, 4, tts, wbr[n], D, wst, wbf, evac, kc_off=4 * n)
    with p.scope():
        mx = p.sb("mx", [128, KC, TT]); rstd = p.sb("rstd", [128, TT])
        wst = [p.sb(f"wos{i}", [128, KC, 128]) for i in range(3)]
        wbf = [p.sb(f"wob{i}", [128, KC, 128], BF16) for i in range(3)]
        xbufs = [p.sb(f"xb{i}", [128, TT]) for i in range(2)]
        sqb = [p.sb(f"sqb{i}", [128, TT]) for i in range(2)]

        def evac2(m0, msz, t0, t1, ps):
            kc = m0 // 128
            cp(p, mx[:, kc, t0:t1], ps[:, 0:t1 - t0], [ps], [mx.sub(kc)], "act" if (t0 // 512) % 2 else "dve")

        gemm_bf(p, merged_b, merged_b.sub, KC, tts, wout, D, wst, wbf, evac2, NB=128)
        rms_stats2(p, c, mx, mx.sub, TT, rstd, c["eps6"][:, 0:1], sqb)
        xv = xT.rearrange("(kc p) t -> p kc t", p=128)
        ov = x1T.rearrange("(kc p) t -> p kc t", p=128)
        for kc in range(KC):
            xb = xbufs[kc % 2]
            p.dma("sp", xb[:, :], xv[:, kc, :], writes=[xb])
            tt(p, mx[:, kc, :], mx[:, kc, :], rstd[:, :], ALU.mult, [mx.sub(kc), rstd], [mx.sub(kc)])
            for (t0, t1, col) in SEGS:
                stt(p, xb[:, t0:t1], mx[:, kc, t0:t1], gg[:, kc, col:col + 1], xb[:, t0:t1], ALU.mult, ALU.add, [mx.sub(kc), gg, xb], [xb])
            p.dma("pool", ov[:, kc, :], xb[:, :], reads=[xb], is_output=True)
    p.finish("sp"); p.close()
    return p


TE_L = TL + 128
TE_C = TCX + 2
TE = TE_L + TE_C
ESEGS = [(0, TE_L, 0), (TE_L, TE, 1)]


def build_ffn():
    p = P()
    p.init_ps(8)
    c = make_consts(p)
    xe = p.dram("xe", [D, TE], "ExternalInput")
    mvd = p.dram("mv", [128, KC, 8], "ExternalInput")
    hmk = p.dram("hmask", [128, 4], "ExternalInput")
    wup = p.dram("wup", [D, 2 * DFF], "ExternalInput"); dwd = p.dram("dw", [128, FC, 9], "ExternalInput"); wdn = p.dram("wdn", [DFF, D], "ExternalInput")
    x2T = p.dram("x2T", [D, TT], "ExternalOutput")
    hm = p.nc.dram_tensor("hm_scratch", [DFF, TT], BF16, kind="Internal").ap()
    hmbufs = [Buf(f"hm{i}") for i in range(FC)]
    mv = p.sb("mvs", [128, KC, 8]); gs = p.sb("gs", [128, KC, 2]); sh = p.sb("sh", [128, KC, 2]); gg = p.sb("gg", [128, KC, 2])
    hmask = p.sb("hmask", [128, 4]); dw = p.sb("dw", [128, FC, 9])
    p.dma("sp", mv[:, :, :], mvd, writes=[mv]); p.dma("sp", hmask[:, :], hmk, writes=[hmask]); p.dma("sp", dw[:, :, :], dwd, writes=[dw])
    mod_prep(p, mv, gs, sh, 0, (1, 3), (2, 4))
    for j in range(2):
        tt(p, gg[:, :, j:j + 1], mv[:, :, 5:6], mv[:, :, 6 + j:7 + j], ALU.mult, [mv], [gg])
    GC = 1.5957691216057308
    with p.scope():
        hb = p.sb("hb", [128, KC, TE], BF16)
        with p.scope():
            x = p.sb("x", [128, KC, TE]); rstd = p.sb("rstd", [128, TE]); sqb = [p.sb(f"sqb{i}", [128, TE]) for i in range(2)]
            load_fm(p, "sp", x, xe, KC)
            rms_stats2(p, c, x, x.sub, TE, rstd, c["eps6"][:, 0:1], sqb)
            norm_mod(p, c, x, x.sub, x, TE, ESEGS, gs, sh, rstd, out=hb)
        wst = [p.sb(f"wus{i}", [128, KC, 128]) for i in range(4)]
        wbf = [p.sb(f"wub{i}", [128, KC, 128], BF16) for i in range(4)]
        upads = [p.sb(f"up{i}", [128, 18, 66]) for i in range(2)]
        ucs = [p.sb(f"uc{i}", [128, TE_C]) for i in range(2)]
        vbs = [p.sb(f"vb{i}", [128, TT]) for i in range(2)]
        accs = [p.sb(f"acc{i}", [128, TT]) for i in range(2)]
        tmps = [p.sb(f"tmp{i}", [128, TT]) for i in range(2)]
        hmbs = [p.sb(f"hmb{i}", [128, TT], BF16) for i in range(2)]
        for ub in upads:
            p.op("pool", lambda e: e.memset(ub[:, :, :], 0.0), writes=[ub])
        u_tts = [(0, 512), (512, 1024), (1024, 1152), (TE_L, TE)]
        v_tts = [(64, 576), (576, 1088), (TE_L + 1, TE_L + 1 + TCX)]
        for cc in range(FC):
            ub = upads[cc % 2]; uc = ucs[cc % 2]; vb = vbs[cc % 2]; acc = accs[cc % 2]; tmp = tmps[cc % 2]; hmb = hmbs[cc % 2]

            def evac_u(m0, msz, t0, t1, ps):
                if t0 < TE_L:
                    r0, r1 = t0 // 64, t1 // 64
                    cp(p, ub[:, r0:r1, 1:65], ps[:, 0:t1 - t0].rearrange("p (r w) -> p r w", w=64), [ps], [ub], "act")
                else:
                    cp(p, uc[:, :], ps[:, 0:TE_C], [ps], [uc], "act")

            def evac_v(m0, msz, t0, t1, ps):
                o0 = t0 - 64 if t0 < TE_L else TL
                cp(p, vb[:, o0:o0 + (t1 - t0)], ps[:, 0:t1 - t0], [ps], [vb], "dve")

            gemm_bf(p, hb, hb.sub, KC, u_tts, wup[:, cc * 128:(cc + 1) * 128], 128, wst, wbf, evac_u, NB=128)
            gemm_bf(p, hb, hb.sub, KC, v_tts, wup[:, DFF + cc * 128:DFF + (cc + 1) * 128], 128, wst, wbf, evac_v, NB=128)
            ts(p, ub[:, 0, :], ub[:, 0, :], hmask[:, 0:1], ALU.mult, [ub, hmask], [ub])
            ts(p, ub[:, 17, :], ub[:, 17, :], hmask[:, 1:2], ALU.mult, [ub, hmask], [ub])
            ts(p, uc[:, 0:1], uc[:, 0:1], hmask[:, 2:3], ALU.mult, [uc, hmask], [uc])
            ts(p, uc[:, TE_C - 1:TE_C], uc[:, TE_C - 1:TE_C], hmask[:, 3:4], ALU.mult, [uc, hmask], [uc])
            a3 = acc[:, 0:TL].rearrange("p (r w) -> p r w", w=64)
            first = True
            for i in range(3):
                for j in range(3):
                    src = ub[:, i:i + 16, j:j + 64]
                    if first:
                        ts(p, a3, src, dw[:, cc, 0:1], ALU.mult, [ub, dw], [acc])
                        first = False
                    else:
                        stt(p, a3, src, dw[:, cc, i * 3 + j:i * 3 + j + 1], a3, ALU.mult, ALU.add, [ub, dw, acc], [acc])
            ac = acc[:, TL:TT]
            ts(p, ac, uc[:, 0:TCX], dw[:, cc, 3:4], ALU.mult, [uc, dw], [acc])
            for j in range(1, 3):
                stt(p, ac, uc[:, j:j + TCX], dw[:, cc, 3 + j:4 + j], ac, ALU.mult, ALU.add, [uc, dw, acc], [acc])
            tt(p, tmp[:, :], acc[:, :], acc[:, :], ALU.mult, [acc], [tmp], eng="pool")
            ts(p, tmp[:, :], tmp[:, :], 0.044715, ALU.mult, [tmp], [tmp], s2=1.0, op1=ALU.add, eng="pool")
            tt(p, tmp[:, :], tmp[:, :], acc[:, :], ALU.mult, [tmp, acc], [tmp], eng="pool")
            act(p, tmp[:, :], tmp[:, :], AF.Sigmoid, [tmp], [tmp], scale=GC)
            tt(p, tmp[:, :], tmp[:, :], acc[:, :], ALU.mult, [tmp, acc], [tmp], eng="pool")
            tt(p, hmb[:, :], tmp[:, :], vb[:, :], ALU.mult, [tmp, vb], [hmb], eng="pool")
            p.dma("pool", hm[cc * 128:(cc + 1) * 128, :], hmb[:, :], reads=[hmb], writes=[hmbufs[cc]])
    with p.scope():
        hsb = p.sb("hsb", [128, FC, 512], BF16); fx = p.sb("fx", [128, KC, 512]); rstd = p.sb("rstd2", [128, 512])
        sqb = [p.sb(f"sqf{i}", [128, 512]) for i in range(2)]
        wds = [p.sb(f"wds{i}", [128, FC, 128]) for i in range(2)]
        wdb = [p.sb(f"wdb{i}", [128, FC, 128], BF16) for i in range(2)]
        xbufs = [p.sb(f"xb{i}", [128, 512]) for i in range(2)]
        hv = hm.rearrange("(kc p) t -> p kc t", p=128)
        wv = wdn.rearrange("(kc p) n -> p kc n", p=128)
        xv = xe.rearrange("(kc p) t -> p kc t", p=128)
        ov = x2T.rearrange("(kc p) t -> p kc t", p=128)
        wi = 0
        for (t0, t1) in ttiles(TT):
            tw = t1 - t0
            for q4 in range(4):
                p.dma("sp", hsb[:, q4 * 11:(q4 + 1) * 11, 0:tw], hv[:, q4 * 11:(q4 + 1) * 11, t0:t1], reads=hmbufs[q4 * 11:(q4 + 1) * 11], writes=[hsb.sub(q4)])
            for m in range(KC):
                ws = wds[wi % 2]; wd = wdb[wi % 2]; wi += 1
                for hf in range(2):
                    p.dma("sp", ws[:, hf * 22:(hf + 1) * 22, :], wv[:, hf * 22:(hf + 1) * 22, m * 128:(m + 1) * 128], writes=[ws.sub(hf)])
                    cp(p, wd[:, hf * 22:(hf + 1) * 22, :], ws[:, hf * 22:(hf + 1) * 22, :], [ws.sub(hf)], [wd.sub(hf)], cast_engine(wi + hf))
                ps = p.next_ps()
                for kc in range(FC):
                    mm(p, ps[:, 0:tw], wd[:, kc, :], hsb[:, kc, 0:tw], [wd.sub(kc // 22), hsb.sub(kc // 11)], [ps], start=(kc == 0), stop=(kc == FC - 1))
                cp(p, fx[:, m, 0:tw], ps[:, 0:tw], [ps], [fx.sub(m)], "act" if m % 2 else "dve")
            rms_stats2(p, c, fx, fx.sub, tw, rstd, c["eps6"][:, 0:1], sqb)
            e0 = t0 + 64 if t0 < TL else TE_L + 1
            col = 0 if t0 < TL else 1
            for kc in range(KC):
                xb = xbufs[kc % 2]
                p.dma("sp", xb[:, 0:tw], xv[:, kc, e0:e0 + tw], writes=[xb])
                tt(p, fx[:, kc, 0:tw], fx[:, kc, 0:tw], rstd[:, 0:tw], ALU.mult, [fx.sub(kc), rstd], [fx.sub(kc)])
                stt(p, xb[:, 0:tw], fx[:, kc, 0:tw], gg[:, kc, col:col + 1], xb[:, 0:tw], ALU.mult, ALU.add, [fx.sub(kc), gg, xb], [xb])
                p.dma("pool", ov[:, kc, t0:t1], xb[:, 0:tw], reads=[xb], is_output=True)
    p.finish("sp"); p.close()
    return p


from concourse.bass_utils import run_bass_kernel_spmd

NC_CTX, NC_LAT = 4, 64
NCH = NC_CTX + NC_LAT
TSEQ = NCH * 64
_PROGS = {}


def _prog(name):
    if name not in _PROGS:
        if name == "mod":
            _PROGS[name] = build_mod()
        elif name == "pre":
            _PROGS[name] = build_pre()
        elif name == "mix":
            _PROGS[name] = build_mix()
        elif name == "post":
            _PROGS[name] = build_post()
        elif name == "ffn":
            _PROGS[name] = build_ffn()
    return _PROGS[name]


def build_mix():
    p = P()
    p.init_ps(8)
    c = make_consts(p)
    io = {}
    T = TSEQ
    shapes = {"ml_qT": [128, T], "ml_kT": [128, T], "ml_k_tm": [64, NCH, 128], "ml_v_tm": [64, NCH, 128], "ml_o_tm": [64, NCH, 128],
              "ml_gi0": [64, NCH], "ml_gf0": [64, NCH], "ml_gi1": [64, NCH], "ml_gf1": [64, NCH], "ml_gb": [64, 4], "ml_ng": [64, 128],
              "gd_qp": [128, T + 8], "gd_kp": [128, T + 8], "gd_vp": [128, T + 8], "gd_cw": [128, 15],
              "gd_a0": [64, NCH], "gd_b0": [64, NCH], "gd_a1": [64, NCH], "gd_b1": [64, NCH], "gd_gp": [64, 4], "gd_ng": [64, 128],
              "gd_z_tm": [64, NCH, 128],
              "rw_xwp": [96, T + 4], "rw_xap": [96, T + 4], "rw_xgp": [128, 2, T + 4], "rw_muw": [96, 2], "rw_mua": [96, 2], "rw_mug": [128, 2, 2]}
    for h in range(2):
        shapes.update({f"rw_rp{h}": [64, T + 4], f"rw_kp{h}": [64, T + 4], f"rw_vp{h}": [64, T + 4], f"rw_mu{h}": [64, 6],
                       f"rw_w2_{h}": [96, 2, 64], f"rw_a2_{h}": [96, 2, 64], f"rw_g2_{h}": [128, 2, 64], f"rw_pr{h}": [64, 9]})
    for k_, s_ in shapes.items():
        io[k_] = p.dram(k_, s_, "ExternalInput")
    io["ml_yb"] = p.dram("ml_yb", [64, NCH, 128], "ExternalOutput")
    io["gd_yc"] = p.dram("gd_yc", [64, NCH, 128], "ExternalOutput")
    io["rw_yd0"] = p.dram("rw_yd0", [64, T], "ExternalOutput")
    io["rw_yd1"] = p.dram("rw_yd1", [64, T], "ExternalOutput")
    sub = lambda pre: {k_[len(pre):]: v_ for k_, v_ in io.items() if k_.startswith(pre)}
    emit_mlstm(p, c, sub("ml_"), NC_CTX, NC_LAT)
    emit_gdn(p, c, sub("gd_"), NC_CTX, NC_LAT, 8)
    emit_rwkv(p, c, sub("rw_"), NC_CTX, NC_LAT, 16)
    p.finish("sp"); p.close()
    p.in_names = list(shapes.keys())
    return p


def _run(name, in_maps):
    import time as _t
    t0 = _t.time()
    p = _prog(name)
    t1 = _t.time()
    res = run_bass_kernel_spmd(p.nc, in_maps, core_ids=list(range(8)))
    print(f"[kernel] launch {name}: build {t1 - t0:.1f}s run {_t.time() - t1:.1f}s", flush=True)
    return res.results


def _fm(v):
    return np.ascontiguousarray(np.asarray(v, np.float32).reshape(-1, 128).T)


def _tm(a):
    return np.ascontiguousarray(a.reshape(-1, 64, a.shape[-1]).transpose(1, 0, 2))


def _col(a):
    return np.ascontiguousarray(a.reshape(-1, 64).T)


def _rep(v, n):
    v = np.asarray(v, np.float32)
    return np.ascontiguousarray(np.broadcast_to(v, (n,) + v.shape))


def _padseg(zf, rows, pad):
    a = zf[rows]
    return np.ascontiguousarray(np.concatenate([np.pad(a[:, 0:256], ((0, 0), (pad, pad))), np.pad(a[:, 256:], ((0, 0), (pad, pad)))], axis=1))


def kernel(x, c, ctx, c_ctx, w_mod, b_mod, norm_g, w_in, conv_dw, conv_b, conv_ln_g, conv_ln_b,
           ml_gate_b, ml_norm_g, gd_conv, gd_a_log, gd_dt_bias, gd_norm_g,
           rw_mu, rw_w0, rw_w2, rw_a0, rw_a2, rw_g2, rw_kk, rw_ka, rw_rk, rw_ln_g, rw_ln_b,
           w_br, w_out, ffn_up, ffn_dw, ffn_down):
    f32 = lambda a: np.asarray(a, np.float32)
    x = f32(x).copy(); cx = f32(ctx).copy()
    (c, c_ctx, w_mod, b_mod, norm_g, w_in, conv_dw, conv_b, conv_ln_g, conv_ln_b, ml_gate_b, ml_norm_g, gd_conv, gd_a_log, gd_dt_bias,
     gd_norm_g, rw_mu, rw_w0, rw_w2, rw_a0, rw_a2, rw_g2, rw_kk, rw_ka, rw_rk, rw_ln_g, rw_ln_b, w_br, w_out, ffn_up, ffn_dw, ffn_down) = [
        f32(a) for a in (c, c_ctx, w_mod, b_mod, norm_g, w_in, conv_dw, conv_b, conv_ln_g, conv_ln_b, ml_gate_b, ml_norm_g, gd_conv, gd_a_log,
                         gd_dt_bias, gd_norm_g, rw_mu, rw_w0, rw_w2, rw_a0, rw_a2, rw_g2, rw_kk, rw_ka, rw_rk, rw_ln_g, rw_ln_b, w_br, w_out,
                         ffn_up, ffn_dw, ffn_down)]
    DEPTH = w_in.shape[0]
    cT = np.ascontiguousarray(np.stack([c[0], c[1], c_ctx], axis=1))
    ims = []
    for j in range(8):
        l, hh = j // 2, j % 2
        cols = slice(hh * 6144, (hh + 1) * 6144)
        ims.append({"cT": cT, "W": np.ascontiguousarray(w_mod[l][:, cols]), "b": _fm(b_mod[l][cols])})
    r = _run("mod", ims)
    mod = np.zeros((DEPTH, 12288, 3), np.float32)
    for j in range(8):
        l, hh = j // 2, j % 2
        o = r[j]["out"]
        mod[l, hh * 6144:(hh + 1) * 6144] = o.transpose(1, 0, 2).reshape(6144, 3)
    for l in range(DEPTH):
        sh1, sc1, gt1, sh2, sc2, gt2 = [mod[l, i * 2048:(i + 1) * 2048] for i in range(6)]
        xTs = []
        ims = []
        for j in range(8):
            b, g = j // 4, j % 4
            xT = np.ascontiguousarray(np.concatenate([x[b, g * 1024:(g + 1) * 1024], cx[b, g * 64:(g + 1) * 64]], axis=0).T)
            xTs.append(xT)
            mv = np.ascontiguousarray(np.stack([_fm(norm_g[l, 0]), _fm(sc1[:, b]), _fm(sh1[:, b]), _fm(sc1[:, 2]), _fm(sh1[:, 2])], axis=2))
            ims.append({"xT": xT, "W": w_in[l], "mv": mv})
        r = _run("pre", ims)
        zTs = [r[j]["zT"] for j in range(8)]
        zf = []
        for b in range(2):
            zf.append(np.concatenate([np.concatenate([zTs[4 * b + g][:, 1024:1088] for g in range(4)], axis=1),
                                      np.concatenate([zTs[4 * b + g][:, 0:1024] for g in range(4)], axis=1)], axis=1))
        ims = []
        for j in range(8):
            b, g = j // 4, j % 4
            z = zf[b]
            hs = slice(g * 128, (g + 1) * 128)
            im = {}
            q = z[1024:1536][hs]; k = z[1536:2048][hs]; v = z[2048:2560][hs]; o = z[2560:3072][hs]
            im["ml_qT"] = np.ascontiguousarray(q); im["ml_kT"] = np.ascontiguousarray(k)
            im["ml_k_tm"] = _tm(k.T); im["ml_v_tm"] = _tm(v.T); im["ml_o_tm"] = _tm(o.T)
            for d in range(2):
                im[f"ml_gi{d}"] = _col(z[3072 + d * 8 + g]); im[f"ml_gf{d}"] = _col(z[3072 + d * 8 + 4 + g])
            im["ml_gb"] = _rep(np.array([ml_gate_b[l, 0, 0, g], ml_gate_b[l, 0, 1, g], ml_gate_b[l, 1, 0, g], ml_gate_b[l, 1, 1, g]], np.float32), 64)
            im["ml_ng"] = _rep(ml_norm_g[l, hs], 64)
            for nm, c0 in (("q", 3088), ("k", 3600), ("v", 4112)):
                im[f"gd_{nm}p"] = _padseg(z, np.arange(c0 + g * 128, c0 + (g + 1) * 128), 2)
            im["gd_cw"] = np.ascontiguousarray(np.concatenate([gd_conv[l][:, c0 + g * 128:c0 + (g + 1) * 128].T for c0 in (0, 512, 1024)], axis=1))
            for d in range(2):
                im[f"gd_a{d}"] = _col(z[5136 + d * 8 + g]); im[f"gd_b{d}"] = _col(z[5136 + d * 8 + 4 + g])
            im["gd_gp"] = _rep(np.array([gd_a_log[l, 0, g], gd_dt_bias[l, 0, g], gd_a_log[l, 1, g], gd_dt_bias[l, 1, g]], np.float32), 64)
            im["gd_ng"] = _rep(gd_norm_g[l, hs], 64)
            im["gd_z_tm"] = _tm(z[4624:5136][hs].T)
            R0 = 5152
            mu = rw_mu[l]
            for li in range(2):
                h = 2 * g + li
                h64 = slice(h * 64, (h + 1) * 64)
                rows = [np.arange(R0 + o_ + h * 64, R0 + o_ + (h + 1) * 64) for o_ in (0, 512, 1024)]
                im[f"rw_rp{li}"] = _padseg(z, rows[0], 1); im[f"rw_kp{li}"] = _padseg(z, rows[1], 1); im[f"rw_vp{li}"] = _padseg(z, rows[2], 1)
                sl = [slice(o_ + h * 64, o_ + (h + 1) * 64) for o_ in (0, 512, 1024)]
                im[f"rw_mu{li}"] = np.ascontiguousarray(np.stack([mu[0, sl[0]], mu[1, sl[0]], mu[0, sl[1]], mu[1, sl[1]], mu[0, sl[2]], mu[1, sl[2]]], axis=1))
                im[f"rw_w2_{li}"] = np.ascontiguousarray(rw_w2[l][:, :, h64].transpose(1, 0, 2))
                im[f"rw_a2_{li}"] = np.ascontiguousarray(rw_a2[l][:, :, h64].transpose(1, 0, 2))
                im[f"rw_g2_{li}"] = np.ascontiguousarray(rw_g2[l][:, h64].reshape(2, 128, 64).transpose(1, 0, 2))
                im[f"rw_pr{li}"] = np.ascontiguousarray(np.stack([rw_w0[l, 0, h64], rw_w0[l, 1, h64], rw_a0[l, 0, h64], rw_a0[l, 1, h64], rw_kk[l, h64],
                                                                  rw_ka[l, h64], rw_rk[l, h], rw_ln_g[l, h64], rw_ln_b[l, h64]], axis=1))
            im["rw_xwp"] = _padseg(z, np.arange(R0 + 1536, R0 + 1632), 1); im["rw_xap"] = _padseg(z, np.arange(R0 + 1632, R0 + 1728), 1)
            xg = _padseg(z, np.arange(R0 + 1728, R0 + 1984), 1)
            im["rw_xgp"] = np.ascontiguousarray(xg.reshape(2, 128, -1).transpose(1, 0, 2))
            im["rw_muw"] = np.ascontiguousarray(mu[:, 1536:1632].T); im["rw_mua"] = np.ascontiguousarray(mu[:, 1632:1728].T)
            im["rw_mug"] = np.ascontiguousarray(mu[:, 1728:1984].reshape(2, 2, 128).transpose(2, 1, 0))
            ims.append(im)
        r = _run("mix", ims)
        yf = []
        for b in range(2):
            yb_ = np.concatenate([r[4 * b + g]["ml_yb"].transpose(1, 0, 2).reshape(TSEQ, 128).T for g in range(4)], axis=0)
            yc_ = np.concatenate([r[4 * b + g]["gd_yc"].transpose(1, 0, 2).reshape(TSEQ, 128).T for g in range(4)], axis=0)
            yd_ = np.concatenate([r[4 * b + g][f"rw_yd{li}"] for g in range(4) for li in range(2)], axis=0)
            yf.append(np.concatenate([yb_, yc_, yd_], axis=0))
        ims = []
        for j in range(8):
            b, g = j // 4, j % 4
            z = zf[b]

            def seg(rows):
                a = z[rows]
                lat = np.pad(a[:, 256:], ((0, 0), (15, 15)))[:, g * 1024:(g + 1) * 1024 + 30]
                cc = np.pad(a[:, 0:256], ((0, 0), (15, 15)))[:, g * 64:(g + 1) * 64 + 30]
                return np.ascontiguousarray(np.concatenate([lat, cc], axis=1))

            cvw = np.concatenate([conv_dw[l].T, conv_b[l][:, None], conv_ln_g[l][:, None], conv_ln_b[l][:, None]], axis=1)
            y = yf[b]
            ybcd = np.ascontiguousarray(np.concatenate([y[:, 256 + g * 1024:256 + (g + 1) * 1024], y[:, g * 64:(g + 1) * 64]], axis=1))
            ims.append({"a_p": seg(np.arange(0, 512)), "g_p": seg(np.arange(512, 1024)),
                        "cvw": np.ascontiguousarray(cvw.reshape(4, 128, 34).transpose(1, 0, 2)), "ybcd": ybcd,
                        "gates": np.ascontiguousarray(zTs[j][7136:15328]), "xT": xTs[j], "wbr": w_br[l], "wout": w_out[l],
                        "mv": np.ascontiguousarray(np.stack([_fm(norm_g[l, 1]), _fm(gt1[:, b]), _fm(gt1[:, 2])], axis=2))})
        r = _run("post", ims)
        for j in range(8):
            b, g = j // 4, j % 4
            o = r[j]["x1T"].T
            x[b, g * 1024:(g + 1) * 1024] = o[0:1024]; cx[b, g * 64:(g + 1) * 64] = o[1024:1088]
        ims = []
        for j in range(8):
            b, g = j // 4, j % 4
            xl = np.pad(x[b], ((64, 64), (0, 0)))[g * 1024:(g + 1) * 1024 + 128]
            xc = np.pad(cx[b], ((1, 1), (0, 0)))[g * 64:(g + 1) * 64 + 2]
            mv = np.ascontiguousarray(np.stack([_fm(norm_g[l, 2]), _fm(sc2[:, b]), _fm(sh2[:, b]), _fm(sc2[:, 2]), _fm(sh2[:, 2]),
                                                _fm(norm_g[l, 3]), _fm(gt2[:, b]), _fm(gt2[:, 2])], axis=2))
            hmask = _rep(np.array([g > 0, g < 3, g > 0, g < 3], np.float32), 128)
            ims.append({"xe": np.ascontiguousarray(np.concatenate([xl, xc], axis=0).T), "mv": mv, "hmask": hmask, "wup": ffn_up[l],
                        "dw": np.ascontiguousarray(ffn_dw[l].reshape(9, 44, 128).transpose(2, 1, 0)), "wdn": ffn_down[l]})
        r = _run("ffn", ims)
        for j in range(8):
            b, g = j // 4, j % 4
            o = r[j]["x2T"].T
            x[b, g * 1024:(g + 1) * 1024] = o[0:1024]; cx[b, g * 64:(g + 1) * 64] = o[1024:1088]
    return x
```

```python
import contextlib
import numpy as np
import concourse.bass as bass
import concourse.mybir as mybir

F32 = mybir.dt.float32
BF16 = mybir.dt.bfloat16
AF = mybir.ActivationFunctionType
ALU = mybir.AluOpType
AX = mybir.AxisListType

SAFE_SAME_ENGINE = True


class Buf:
    __slots__ = ("name", "w", "r")

    def __init__(self, name):
        self.name = name
        self.w = None
        self.r = []


class Tn:
    def __init__(self, t, name):
        self.t = t
        self.b = Buf(name)
        self.subs = {}

    def __getitem__(self, idx):
        return self.t[idx]

    def sub(self, key):
        if key not in self.subs:
            self.subs[key] = Buf(f"{self.b.name}/{key}")
        return self.subs[key]


class P:
    def __init__(self, n_dma_sems=24):
        self.nc = bass.Bass("TRN2", target_bir_lowering=False)
        nc = self.nc
        self.st = contextlib.ExitStack()
        self.eng = {"pe": nc.tensor, "act": nc.scalar, "dve": nc.vector, "pool": nc.gpsimd, "sp": nc.sync}
        self.sem = {}
        self.cnt = {}
        for e in ("pe", "act", "dve", "pool"):
            self.sem[e] = self.st.enter_context(nc.semaphore(f"sem_{e}"))
            self.cnt[e] = 0
        self.waited = {e: {} for e in self.eng}
        self.dsem = [self.st.enter_context(nc.semaphore(f"dsem{i}")) for i in range(n_dma_sems)]
        self.dtgt = [0] * n_dma_sems
        self.drr = 0
        self.n_inst = 0
        self.out_events = []

    def sb(self, name, shape, dt=F32):
        self._uid = getattr(self, "_uid", 0) + 1
        name = f"{name}_{self._uid}"
        return Tn(self.st.enter_context(self.nc.sbuf_tensor(name, list(shape), dt)), name)

    def ps(self, name, shape, dt=F32):
        return Tn(self.st.enter_context(self.nc.psum_tensor(name, list(shape), dt)), name)

    def dram(self, name, shape, kind, dt=F32):
        return self.nc.dram_tensor(name, list(shape), dt, kind=kind).ap()

    def _wait(self, w, ev):
        if ev is None:
            return
        kind = ev[0]
        if kind == "eng":
            _, e, n = ev
            if e == w and (e == "pe" or not SAFE_SAME_ENGINE):
                return
            key = ("eng", e)
            if self.waited[w].get(key, 0) >= n:
                return
            self.eng[w].wait_ge(self.sem[e], n)
            self.waited[w][key] = n
        else:
            _, i, tgt = ev
            key = ("dma", i)
            if self.waited[w].get(key, 0) >= tgt:
                return
            self.eng[w].wait_ge(self.dsem[i], tgt)
            self.waited[w][key] = tgt

    @staticmethod
    def _b(x):
        return x.b if isinstance(x, Tn) else x

    def _deps(self, w, reads, writes):
        for r in reads:
            self._wait(w, self._b(r).w)
        for x in writes:
            b = self._b(x)
            self._wait(w, b.w)
            for ev in b.r:
                self._wait(w, ev)

    def _record(self, ev, reads, writes):
        for r in reads:
            self._b(r).r.append(ev)
            if len(self._b(r).r) > 64:
                self._b(r).r = self._compact(self._b(r).r)
        for x in writes:
            b = self._b(x)
            b.w = ev
            b.r = []

    @staticmethod
    def _compact(evs):
        best = {}
        for ev in evs:
            k = (ev[0], ev[1])
            if k not in best or best[k][2] < ev[2]:
                best[k] = ev
        return list(best.values())

    def op(self, e, fn, reads=(), writes=()):
        self._deps(e, reads, writes)
        inst = fn(self.eng[e])
        self.cnt[e] += 1
        inst.then_inc(self.sem[e], 1)
        ev = ("eng", e, self.cnt[e])
        self._record(ev, reads, writes)
        self.n_inst += 1
        return ev

    def dma(self, q, out, in_, reads=(), writes=(), is_output=False, **kw):
        self._deps(q, reads, writes)
        i = self.drr
        self.drr = (self.drr + 1) % len(self.dsem)
        if self.dtgt[i] > 0:
            self._wait(q, ("dma", i, self.dtgt[i]))
        inst = self.eng[q].dma_start(out=out, in_=in_, **kw)
        self.dtgt[i] += 16
        inst.then_inc(self.dsem[i], 16)
        ev = ("dma", i, self.dtgt[i])
        self._record(ev, reads, writes)
        if is_output:
            self.out_events.append(ev)
        self.n_inst += 1
        return ev

    def finish(self, w="sp"):
        for ev in self.out_events:
            self._wait(w, ev)
        for i, t in enumerate(self.dtgt):
            if t:
                self._wait(w, ("dma", i, t))
        for e in self.cnt:
            if self.cnt[e]:
                self._wait(w, ("eng", e, self.cnt[e]))

    def close(self):
        self.st.close()


def _p_barrier(self):
    engs = list(self.eng.keys())
    for w in engs:
        for e in self.cnt:
            if self.cnt[e]:
                self._wait(w, ("eng", e, self.cnt[e]))
        for i, t in enumerate(self.dtgt):
            if t:
                self._wait(w, ("dma", i, t))


P.barrier = _p_barrier


@contextlib.contextmanager
def _p_scope(self):
    outer = self.st
    self.st = contextlib.ExitStack()
    try:
        yield
    finally:
        self.barrier()
        self.st.close()
        self.st = outer


P.scope = _p_scope


def _p_init_ps(self, n=8):
    self.pstiles = [self.ps(f"psr{i}", [128, 512]) for i in range(n)]
    self.psi = 0


def _p_next_ps(self):
    t = self.pstiles[self.psi % len(self.pstiles)]
    self.psi += 1
    return t


P.init_ps = _p_init_ps
P.next_ps = _p_next_ps


def bc_last(ap2d, L):
    P_, n = ap2d.shape
    return ap2d.unsqueeze(2).to_broadcast([P_, n, L])


def bc_mid(ap2d, n):
    P_, L = ap2d.shape
    return ap2d.unsqueeze(1).to_broadcast([P_, n, L])


def _p_coll(self, kind, in_ap, out_ap, groups, reads=(), writes=(), op=None):
    q = "pool"
    self._deps(q, reads, writes)
    i = self.drr
    self.drr = (self.drr + 1) % len(self.dsem)
    if self.dtgt[i] > 0:
        self._wait(q, ("dma", i, self.dtgt[i]))
    inst = self.nc.gpsimd.collective_compute(kind, op if op is not None else ALU.bypass, replica_groups=groups, ins=[in_ap], outs=[out_ap])
    self.dtgt[i] += 16
    inst.then_inc(self.dsem[i], 16)
    ev = ("dma", i, self.dtgt[i])
    self._record(ev, reads, writes)
    self.n_inst += 1
    return ev


P.coll = _p_coll


L = 64


def tt(p, out, in0, in1, op, R, W, eng="dve"):
    return p.op(eng, lambda e: e.tensor_tensor(out=out, in0=in0, in1=in1, op=op), reads=R, writes=W)


def ts(p, out, in0, s1, op0, R, W, s2=None, op1=None, eng="dve"):
    if op1 is None:
        return p.op(eng, lambda e: e.tensor_scalar(out=out, in0=in0, scalar1=s1, scalar2=None, op0=op0), reads=R, writes=W)
    return p.op(eng, lambda e: e.tensor_scalar(out=out, in0=in0, scalar1=s1, scalar2=s2, op0=op0, op1=op1), reads=R, writes=W)


def stt(p, out, in0, scalar, in1, op0, op1, R, W):
    return p.op("dve", lambda e: e.scalar_tensor_tensor(out=out, in0=in0, scalar=scalar, in1=in1, op0=op0, op1=op1), reads=R, writes=W)


def act(p, out, in_, func, R, W, bias=None, scale=None):
    kw = {}
    if bias is not None:
        kw["bias"] = bias
    if scale is not None:
        kw["scale"] = scale
    return p.op("act", lambda e: e.activation(out=out, in_=in_, func=func, **kw), reads=R, writes=W)


FP32R = False


def mm(p, out, lhsT, rhs, R, W, start=True, stop=True):
    if FP32R and lhsT.dtype == F32 and rhs.dtype == F32:
        lhsT = lhsT.bitcast(mybir.dt.float32r); rhs = rhs.bitcast(mybir.dt.float32r)
    return p.op("pe", lambda e: e.matmul(out, lhsT=lhsT, rhs=rhs, start=start, stop=stop), reads=R, writes=W)


def cp(p, out, in_, R, W, eng):
    if eng == "act":
        return act(p, out, in_, AF.Copy, R, W)
    return p.op(eng, lambda e: e.tensor_copy(out=out, in_=in_), reads=R, writes=W)


def make_consts(p):
    c = {}
    c["ones"] = p.sb("c_ones", [128, 128])
    p.op("dve", lambda e: e.memset(c["ones"][:, :], 1.0), writes=[c["ones"]])
    for name, pat, cm, cmp_ in (("UT", 1, -1, ALU.is_ge), ("LT", -1, 1, ALU.is_ge), ("UTs", 1, -1, ALU.is_gt),
                                ("LTs", -1, 1, ALU.is_gt), ("I", 1, -1, ALU.is_equal)):
        t = p.sb("c_" + name, [128, 128])
        p.op("pool", lambda e: e.affine_select(out=t[:, :], in_=c["ones"][:, :], pattern=[[pat, 128]], compare_op=cmp_,
                                               fill=0.0, base=0, channel_multiplier=cm), reads=[c["ones"]], writes=[t])
        c[name] = t
    c["eps6"] = p.sb("c_eps6", [128, 1])
    p.op("dve", lambda e: e.memset(c["eps6"][:, :], 1e-6), writes=[c["eps6"]])
    return c


def dir_masks(c, d):
    if d == 0:
        return c["UT"], c["UTs"], c["LT"], c["LTs"]
    return c["LT"], c["LTs"], c["UT"], c["UTs"]


def chunk_order(d, nc_ctx, nc_lat):
    n = nc_ctx + nc_lat
    if d == 0:
        return list(range(n))
    return list(range(nc_ctx - 1, -1, -1)) + list(range(n - 1, nc_ctx - 1, -1))


def rowbc(p, c, col, out, pout, nch, scratch):
    tt(p, scratch[:, :, :], bc_last(col[0:L, 0:nch], L), bc_mid(c["I"][0:L, 0:L], nch), ALU.mult, [col, c["I"]], [scratch])
    flat = scratch[:, :, :].rearrange("p n t -> p (n t)")
    tot = nch * L
    for i, c0 in enumerate(range(0, tot, 512)):
        c1 = min(c0 + 512, tot)
        ps = p.next_ps()
        mm(p, ps[0:pout, 0:c1 - c0], c["ones"][0:L, 0:pout], flat[:, c0:c1], [c["ones"], scratch], [ps])
        cp(p, out[0:pout, c0:c1], ps[0:pout, 0:c1 - c0], [ps], [out], "act" if i % 2 else "dve")


def colmm(p, lhsT_ap, rhs_tn, rhs_ap, out_tn, out_ap, pout, ncol, Rextra=()):
    ps = p.next_ps()
    mm(p, ps[0:pout, 0:ncol], lhsT_ap, rhs_ap, [rhs_tn] + list(Rextra), [ps])
    cp(p, out_ap, ps[0:pout, 0:ncol], [ps], [out_tn], "dve")


def batched_chunk_mm(p, nch, lhsT_fn, rhs_fn, R, mrows, ncols, evac):
    G = 512 // ncols
    for n0 in range(0, nch, G):
        g = min(G, nch - n0)
        ps = p.next_ps()
        for j in range(g):
            n = n0 + j
            mm(p, ps[0:mrows, j * ncols:(j + 1) * ncols], lhsT_fn(n), rhs_fn(n), R, [ps])
        evac(ps, n0, g)


def transpose_chunks(p, c, src_fn, Rsrc, kin, dst, nch, eng_alt=True):
    m = None
    cnt = [0]

    def evac(ps, n0, g):
        eng = "act" if (cnt[0] % 2 and eng_alt) else "dve"
        cnt[0] += 1
        cp(p, dst[:, n0:n0 + g, :], ps[0:dst.t.shape[0], 0:g * kin].rearrange("p (g k) -> p g k", k=kin), [ps], [dst], eng)

    G = 512 // kin
    for n0 in range(0, nch, G):
        g = min(G, nch - n0)
        ps = p.next_ps()
        for j in range(g):
            src = src_fn(n0 + j)
            mm_out = ps[0:src.shape[1], j * kin:(j + 1) * kin]
            p.op("pe", lambda e: e.transpose(mm_out, src, c["I"][0:kin, 0:kin]), reads=list(Rsrc) + [c["I"]], writes=[ps])
        evac(ps, n0, g)


def tri_inverse_T(p, c, X, Y, nch, tmp):
    Pm, Qm, X2, Y2 = tmp["P"], tmp["Q"], tmp["X2"], tmp["Y2"]
    I3 = bc_mid(c["I"][0:L, 0:L], nch)
    tt(p, Pm[:, :, :], Y[:, :, :], I3, ALU.add, [Y, c["I"]], [Pm.sub(n0) for n0 in range(0, nch, 8)])
    tt(p, Qm[:, :, :], X[:, :, :], I3, ALU.add, [X, c["I"]], [Qm.sub(n0) for n0 in range(0, nch, 8)])
    curX, curY, nxtX, nxtY = X, Y, X2, Y2
    for lvl in range(5):
        def ev_x(ps, n0, g):
            cp(p, nxtX[:, n0:n0 + g, :], ps[0:L, 0:g * L].rearrange("p (g k) -> p g k", k=L), [ps], [nxtX], "act")

        def ev_y(ps, n0, g):
            cp(p, nxtY[:, n0:n0 + g, :], ps[0:L, 0:g * L].rearrange("p (g k) -> p g k", k=L), [ps], [nxtY], "dve")

        if lvl < 4:
            batched_chunk_mm(p, nch, lambda n: curY[:, n, :], lambda n: curX[:, n, :], [curX, curY], L, L, ev_x)
        batched_chunk_mm(p, nch, lambda n: curX[:, n, :], lambda n: curY[:, n, :], [curX, curY], L, L, ev_y)
        curX, curY, nxtX, nxtY = nxtX, nxtY, curX, curY

        pend = []

        def ev_p(ps, n0, g):
            pend.append(("P", ps, n0, g))

        def ev_q(ps, n0, g):
            pend.append(("Q", ps, n0, g))

        G = 8
        for n0 in range(0, nch, G):
            g = min(G, nch - n0)
            psP = p.next_ps()
            for j in range(g):
                n = n0 + j
                mm(p, psP[0:L, j * L:(j + 1) * L], Qm[:, n, :], curY[:, n, :], [Qm.sub(n0), curY], [psP])
            if lvl < 4:
                psQ = p.next_ps()
                for j in range(g):
                    n = n0 + j
                    mm(p, psQ[0:L, j * L:(j + 1) * L], Pm[:, n, :], curX[:, n, :], [Pm.sub(n0), curX], [psQ])
            tt(p, Pm[:, n0:n0 + g, :], Pm[:, n0:n0 + g, :], psP[0:L, 0:g * L].rearrange("p (g k) -> p g k", k=L), ALU.add,
               [psP, Pm.sub(n0)], [Pm.sub(n0)])
            if lvl < 4:
                tt(p, Qm[:, n0:n0 + g, :], Qm[:, n0:n0 + g, :], psQ[0:L, 0:g * L].rearrange("p (g k) -> p g k", k=L), ALU.add,
                   [psQ, Qm.sub(n0)], [Qm.sub(n0)])
    subs = [Pm.sub(n0) for n0 in range(0, nch, 8)]
    return Pm, subs


def chunk_loops(p, chains):
    nsteps = max(len(ch["order"]) for ch in chains)
    if globals().get("SKIP_LOOPS"):
        nsteps = 1
    for ch in chains:
        if "cur" not in ch["st"]:
            ch["st"]["cur"] = 0
            p.op("dve", lambda e: e.memset(ch["M"][0][:, :], 0.0), writes=[ch["M"][0]])
    for i in range(nsteps):
        for ch in chains:
            if i >= len(ch["order"]):
                continue
            n = ch["order"][i]
            dk, dvp = ch["dk"], ch["dvp"]
            M = ch["M"][ch["st"]["cur"]]
            Mn = ch["M"][1 - ch["st"]["cur"]]
            R = list(ch.get("R", [])) + [ch["QT"], ch["A1T"], ch["BL"], ch["decay"], ch["U0"]]
            if ch.get("AhatT") is not None:
                R.append(ch["AhatT"])
            if ch.get("K0") is not None:
                R += list(ch["K0"])
            if ch.get("AhatT") is not None:
                psu = p.next_ps()
                mm(p, psu[0:L, 0:dvp], ch["AhatT"][:, n, :], M[:, :], [M] + R, [psu])
                u = ch["ub"][i % 2]
                tt(p, u[:, :], psu[0:L, 0:dvp], ch["U0"][:, n, :], ALU.add, [psu] + R, [u])
                u_ap, u_dep = u[:, :], [u]
            else:
                u_ap, u_dep = ch["U0"][:, n, :], []
            psy = p.next_ps()
            mm(p, psy[0:L, 0:dvp], ch["QT"][:, n * L:(n + 1) * L], M[:, :], [M] + R, [psy], start=True, stop=False)
            mm(p, psy[0:L, 0:dvp], ch["A1T"][:, n, :], u_ap, u_dep + R, [psy], start=False, stop=True)
            cp(p, ch["Y"][:, n, :], psy[0:L, 0:dvp], [psy], [ch["Y"].sub(n)], "act")
            psm = p.next_ps()
            if ch.get("K0") is not None:
                KL, V = ch["K0"]
                mm(p, psm[0:dk, 0:dvp], KL[:, n, :], V[:, n, :], R, [psm], start=True, stop=False)
                mm(p, psm[0:dk, 0:dvp], ch["BL"][:, n, :], u_ap, u_dep + R, [psm], start=False, stop=True)
            else:
                mm(p, psm[0:dk, 0:dvp], ch["BL"][:, n, :], u_ap, u_dep + R, [psm])
            stt(p, Mn[:, :], M[:, :], ch["decay"][:, n:n + 1], psm[0:dk, 0:dvp], ALU.mult, ALU.add, [M, psm] + R, [Mn])
            ch["st"]["cur"] = 1 - ch["st"]["cur"]


def emit_mlstm(p, c, io, nc_ctx, nc_lat, bs=16):
    nch = nc_ctx + nc_lat
    T = nch * L
    DH = 128
    with p.scope():
        H = p.sb("ml_H", [L, nch, DH])
        gb = p.sb("ml_gb", [L, 4]); ngb = p.sb("ml_ng", [L, DH]); ngate = p.sb("ml_ngb", [L, 4])
        p.dma("sp", gb[:, :], io["gb"], writes=[gb])
        p.dma("sp", ngb[:, :], io["ng"], writes=[ngb])
        ts(p, ngate[:, :], gb[:, :], -1.0, ALU.mult, [gb], [ngate])
        with p.scope():
            qT = p.sb("ml_qT", [128, T]); kT = p.sb("ml_kT", [128, T])
            k_tm = p.sb("ml_ktm", [L, nch, DH]); vp = p.sb("ml_vp", [L, nch, DH + 1])
            p.dma("sp", qT[:, :], io["qT"], writes=[qT])
            p.dma("sp", kT[:, :], io["kT"], writes=[kT])
            p.dma("sp", k_tm[:, :, :], io["k_tm"], writes=[k_tm])
            p.dma("sp", vp[:, :, 0:DH], io["v_tm"], writes=[vp.sub("v")])
            p.op("pool", lambda e: e.memset(vp[:, :, DH:DH + 1], 1.0), writes=[vp.sub("one")])
            act(p, qT[:, :], qT[:, :], AF.Copy, [qT], [qT], scale=DH ** -0.5)
            VP = [vp.sub("v"), vp.sub("one")]
            for d in range(2):
                with p.scope():
                    mT_incl, _, _, _ = dir_masks(c, d)
                    gi = p.sb("ml_gi", [L, nch]); gf = p.sb("ml_gf", [L, nch])
                    p.dma("sp", gi[:, :], io[f"gi{d}"], writes=[gi])
                    p.dma("sp", gf[:, :], io[f"gf{d}"], writes=[gf])
                    li = p.sb("ml_li", [L, nch]); lf = p.sb("ml_lf", [L, nch]); bt = p.sb("ml_bt", [L, nch])
                    blb = p.sb("ml_blb", [128, nch]); ecol = p.sb("ml_ecol", [L, nch]); eli = p.sb("ml_eli", [L, nch])
                    decay = p.sb("ml_decay", [128, nch])
                    ts(p, li[:, :], gi[:, :], gb[:, 2 * d:2 * d + 1], ALU.add, [gi, gb], [li])
                    act(p, lf[:, :], gf[:, :], AF.Exp, [gf, ngate], [lf], bias=ngate[:, 2 * d + 1:2 * d + 2], scale=-1.0)
                    act(p, lf[:, :], lf[:, :], AF.Ln, [lf], [lf], bias=1.0)
                    ts(p, lf[:, :], lf[:, :], -1.0, ALU.mult, [lf], [lf])
                    colmm(p, mT_incl[0:L, 0:L], lf, lf[:, :], bt, bt[:, :], L, nch, [mT_incl])
                    colmm(p, c["ones"][0:L, 0:128], lf, lf[:, :], blb, blb[:, :], 128, nch, [c["ones"]])
                    act(p, decay[:, :], blb[:, :], AF.Exp, [blb], [decay])
                    act(p, eli[:, :], li[:, :], AF.Exp, [li], [eli])
                    tt(p, ecol[:, :], li[:, :], bt[:, :], ALU.subtract, [li, bt], [ecol])
                    tt(p, ecol[:, :], ecol[:, :], blb[0:L, :], ALU.add, [ecol, blb], [ecol])
                    act(p, ecol[:, :], ecol[:, :], AF.Exp, [ecol], [ecol])
                    Ms = [p.sb(f"ml_M{i}", [128, DH + 1]) for i in range(2)]
                    st = {}
                    for (lo, hi) in blocks_for(d, nc_ctx, nc_lat, bs):
                        nb = hi - lo
                        TB = nb * L
                        with p.scope():
                            brow = p.sb("ml_brow", [128, TB]); scr = p.sb("ml_scr", [L, nb, L]); btb = p.sb("ml_btb", [L, nb])
                            cp(p, btb[:, :], bt[:, lo:hi], [bt], [btb], "dve")
                            rowbc(p, c, btb, brow, 128, nb, scr)
                            wT = p.sb("ml_wT", [L, nb, L])
                            tt(p, wT[:, :, :], brow[0:L, :].rearrange("p (n t) -> p n t", t=L), bc_last(btb[:, :], L), ALU.subtract, [brow, btb], [wT])
                            ts(p, wT[:, :, :], wT[:, :, :], 0.0, ALU.min, [wT], [wT])
                            act(p, wT[:, :, :], wT[:, :, :], AF.Exp, [wT], [wT])
                            tt(p, wT[:, :, :], wT[:, :, :], bc_mid(mT_incl[0:L, 0:L], nb), ALU.mult, [wT, mT_incl], [wT])
                            tt(p, wT[:, :, :], wT[:, :, :], bc_last(eli[:, lo:hi], L), ALU.mult, [wT, eli], [wT])
                            qdT = brow
                            act(p, brow[:, :], brow[:, :], AF.Exp, [brow], [brow])
                            tt(p, qdT[:, :], qT[:, lo * L:hi * L], brow[:, :], ALU.mult, [qT, brow], [qdT])
                            A1T = scr

                            def evac(ps, n0, g):
                                tt(p, A1T[:, n0:n0 + g, :], ps[0:L, 0:g * L].rearrange("p (g k) -> p g k", k=L), wT[:, n0:n0 + g, :], ALU.mult,
                                   [ps, wT], [A1T])

                            batched_chunk_mm(p, nb, lambda n: kT[:, (lo + n) * L:(lo + n + 1) * L], lambda n: qT[:, (lo + n) * L:(lo + n + 1) * L],
                                             [kT, qT], L, L, evac)
                            BL = p.sb("ml_BL", [L, nb, DH])
                            tt(p, BL[:, :, :], k_tm[:, lo:hi, :], bc_last(ecol[:, lo:hi], DH), ALU.mult, [k_tm, ecol], [BL])
                            Y = p.sb("ml_Y", [L, nb, DH + 1]); vpb = p.sb("ml_vpb", [L, nb, DH + 1]); dcy = p.sb("ml_dcy", [128, nb])
                            cp(p, vpb[:, :, :], vp[:, lo:hi, :], VP, [vpb], "dve")
                            cp(p, dcy[:, :], decay[:, lo:hi], [decay], [dcy], "dve")
                            order = list(range(nb)) if d == 0 else list(range(nb - 1, -1, -1))
                            chain = dict(dk=128, dvp=DH + 1, order=order, QT=qdT, A1T=A1T, BL=BL, decay=dcy,
                                         U0=vpb, AhatT=None, Y=Y, K0=None, M=Ms, R=[], st=st)
                            chunk_loops(p, [chain])
                            Ysubs = [Y.sub(n) for n in range(nb)]
                            den = p.sb("ml_den", [L, nb])
                            ts(p, den[:, :], Y[:, :, DH], -1.0, ALU.mult, Ysubs, [den])
                            tt(p, den[:, :], den[:, :], Y[:, :, DH], ALU.max, Ysubs + [den], [den])
                            ts(p, den[:, :], den[:, :], 1.0, ALU.max, [den], [den])
                            p.op("dve", lambda e: e.reciprocal(out=den[:, :], in_=den[:, :]), reads=[den], writes=[den])
                            if d == 0:
                                tt(p, H[:, lo:hi, :], Y[:, :, 0:DH], bc_last(den[:, :], DH), ALU.mult, Ysubs + [den], [H])
                            else:
                                tt(p, Y[:, :, 0:DH], Y[:, :, 0:DH], bc_last(den[:, :], DH), ALU.mult, Ysubs + [den], [Y])
                                tt(p, H[:, lo:hi, :], H[:, lo:hi, :], Y[:, :, 0:DH], ALU.add, [H, Y], [H])
        with p.scope():
            sq = p.sb("ml_sq", [L, nch, DH]); ss = p.sb("ml_ss", [L, nch]); o_tm = p.sb("ml_otm", [L, nch, DH])
            p.dma("sp", o_tm[:, :, :], io["o_tm"], writes=[o_tm])
            tt(p, sq[:, :, :], H[:, :, :], H[:, :, :], ALU.mult, [H], [sq])
            p.op("dve", lambda e: e.tensor_reduce(out=ss[:, :], in_=sq[:, :, :], axis=AX.X, op=ALU.add), reads=[sq], writes=[ss])
            act(p, ss[:, :], ss[:, :], AF.Sqrt, [ss, c["eps6"]], [ss], scale=1.0 / DH, bias=c["eps6"][0:L, 0:1])
            p.op("dve", lambda e: e.reciprocal(out=ss[:, :], in_=ss[:, :]), reads=[ss], writes=[ss])
            tt(p, H[:, :, :], H[:, :, :], bc_last(ss[:, :], DH), ALU.mult, [H, ss], [H])
            tt(p, H[:, :, :], H[:, :, :], bc_mid(ngb[:, :], nch), ALU.mult, [H, ngb], [H])
            act(p, o_tm[:, :, :], o_tm[:, :, :], AF.Sigmoid, [o_tm], [o_tm])
            tt(p, H[:, :, :], H[:, :, :], o_tm[:, :, :], ALU.mult, [H, o_tm], [H])
            p.dma("pool", io["yb"], H[:, :, :], reads=[H], is_output=True)


def blocks_for(d, nc_ctx, nc_lat, bs=24):
    bl = []
    if nc_ctx:
        bl.append((0, nc_ctx))
    lat = []
    for lo in range(nc_ctx, nc_ctx + nc_lat, bs):
        lat.append((lo, min(lo + bs, nc_ctx + nc_lat)))
    if d == 0:
        return bl + lat
    return bl + lat[::-1]


def l2norm_fm(p, c, x, T, scale, tmp):
    act(p, tmp[:, :], x[:, :], AF.Square, [x], [tmp])
    for i, c0 in enumerate(range(0, T, 512)):
        c1 = min(c0 + 512, T)
        ps = p.next_ps()
        mm(p, ps[:, 0:c1 - c0], c["ones"][:, :], tmp[:, c0:c1], [c["ones"], tmp], [ps])
        act(p, tmp[:, c0:c1], ps[:, 0:c1 - c0], AF.Sqrt, [ps, c["eps6"]], [tmp], bias=c["eps6"][:, 0:1])
    p.op("dve", lambda e: e.reciprocal(out=tmp[:, :], in_=tmp[:, :]), reads=[tmp], writes=[tmp])
    stt(p, x[:, :], x[:, :], float(scale), tmp[:, :], ALU.mult, ALU.mult, [x, tmp], [x])


def emit_gdn(p, c, io, nc_ctx, nc_lat, bs=24):
    nch = nc_ctx + nc_lat
    T = nch * L
    Tc, Tl = nc_ctx * L, nc_lat * L
    DH = 128
    with p.scope():
        O = p.sb("gd_O", [L, nch, DH])
        cw = p.sb("gd_cw", [128, 15]); gp = p.sb("gd_gp", [L, 4]); ngb = p.sb("gd_ng", [L, DH])
        p.dma("sp", cw[:, :], io["cw"], writes=[cw]); p.dma("sp", gp[:, :], io["gp"], writes=[gp]); p.dma("sp", ngb[:, :], io["ng"], writes=[ngb])
        p.gd_big = contextlib.ExitStack()
        p.gd_big.enter_context(p.scope())
        q_fm = p.sb("gd_q", [128, T]); k_fm = p.sb("gd_k", [128, T])
        k_tm = p.sb("gd_ktm", [L, nch, DH]); v_tm = p.sb("gd_vtm", [L, nch, DH])
        with p.scope():
            v_fm = p.sb("gd_v", [128, T]); raw = p.sb("gd_raw", [128, T + 8]); tmp = p.sb("gd_tmp", [128, T])
            for wi, (nm, dst) in enumerate((("qp", q_fm), ("kp", k_fm), ("vp", v_fm))):
                p.dma("sp", raw[:, :], io[nm], writes=[raw])
                for (o_in, o_out, tl) in ((0, 0, Tc), (Tc + 4, Tc, Tl)):
                    if tl == 0:
                        continue
                    ts(p, dst[:, o_out:o_out + tl], raw[:, o_in:o_in + tl], cw[:, wi * 5:wi * 5 + 1], ALU.mult, [raw, cw], [dst])
                    for j in range(1, 5):
                        stt(p, dst[:, o_out:o_out + tl], raw[:, o_in + j:o_in + j + tl], cw[:, wi * 5 + j:wi * 5 + j + 1],
                            dst[:, o_out:o_out + tl], ALU.mult, ALU.add, [raw, cw, dst], [dst])
                act(p, dst[:, :], dst[:, :], AF.Silu, [dst], [dst])
            l2norm_fm(p, c, q_fm, T, DH ** -0.5, tmp)
            l2norm_fm(p, c, k_fm, T, 1.0, tmp)
            transpose_chunks(p, c, lambda n: k_fm[:, n * L:(n + 1) * L], [k_fm], 128, k_tm, nch)
            transpose_chunks(p, c, lambda n: v_fm[:, n * L:(n + 1) * L], [v_fm], 128, v_tm, nch)
        for d in range(2):
            with p.scope():
                mT_incl, mT_str, m_incl, m_str = dir_masks(c, d)
                ar = p.sb("gd_ar", [L, nch]); br = p.sb("gd_br", [L, nch])
                p.dma("sp", ar[:, :], io[f"a{d}"], writes=[ar]); p.dma("sp", br[:, :], io[f"b{d}"], writes=[br])
                lg = p.sb("gd_lg", [L, nch]); beta = p.sb("gd_beta", [L, nch]); nbeta = p.sb("gd_nbeta", [L, nch])
                gc = p.sb("gd_gc", [L, nch]); glb = p.sb("gd_glb", [128, nch]); decay = p.sb("gd_decay", [128, nch])
                negA = p.sb("gd_negA", [L, 1]); egc = p.sb("gd_egc", [L, nch]); bege = p.sb("gd_bege", [L, nch]); ekd = p.sb("gd_ekd", [L, nch])
                act(p, negA[:, :], gp[:, 2 * d:2 * d + 1], AF.Exp, [gp], [negA])
                ts(p, negA[:, :], negA[:, :], -1.0, ALU.mult, [negA], [negA])
                act(p, lg[:, :], ar[:, :], AF.Exp, [ar, gp], [lg], bias=gp[:, 2 * d + 1:2 * d + 2])
                act(p, lg[:, :], lg[:, :], AF.Ln, [lg], [lg], bias=1.0)
                ts(p, lg[:, :], lg[:, :], negA[:, 0:1], ALU.mult, [lg, negA], [lg])
                act(p, beta[:, :], br[:, :], AF.Sigmoid, [br], [beta])
                ts(p, nbeta[:, :], beta[:, :], -1.0, ALU.mult, [beta], [nbeta])
                colmm(p, mT_incl[0:L, 0:L], lg, lg[:, :], gc, gc[:, :], L, nch, [mT_incl])
                colmm(p, c["ones"][0:L, 0:128], lg, lg[:, :], glb, glb[:, :], 128, nch, [c["ones"]])
                act(p, decay[:, :], glb[:, :], AF.Exp, [glb], [decay])
                act(p, egc[:, :], gc[:, :], AF.Exp, [gc], [egc])
                tt(p, bege[:, :], egc[:, :], beta[:, :], ALU.mult, [egc, beta], [bege])
                tt(p, ekd[:, :], glb[0:L, :], gc[:, :], ALU.subtract, [glb, gc], [ekd])
                act(p, ekd[:, :], ekd[:, :], AF.Exp, [ekd], [ekd])
                Ms = [p.sb(f"gd_M{i}", [128, DH]) for i in range(2)]
                ubs = [p.sb(f"gd_u{i}", [L, DH]) for i in range(2)]
                st = {}
                for (lo, hi) in blocks_for(d, nc_ctx, nc_lat, bs):
                    nb = hi - lo
                    TB = nb * L
                    with p.scope():
                        grow = p.sb("gd_grow", [128, TB]); brow = p.sb("gd_brow", [L, TB]); scr = p.sb("gd_scr", [L, nb, L])
                        gcb = p.sb("gd_gcb", [L, nb]); nbb = p.sb("gd_nbb", [L, nb])
                        cp(p, gcb[:, :], gc[:, lo:hi], [gc], [gcb], "dve")
                        cp(p, nbb[:, :], nbeta[:, lo:hi], [nbeta], [nbb], "dve")
                        rowbc(p, c, gcb, grow, 128, nb, scr)
                        rowbc(p, c, nbb, brow, L, nb, scr)
                        decT = p.sb("gd_decT", [L, nb, L]); dec = p.sb("gd_dec", [L, nb, L]); wA = p.sb("gd_wA", [L, nb, L])
                        g3 = grow[0:L, :].rearrange("p (n t) -> p n t", t=L)
                        tt(p, decT[:, :, :], g3, bc_last(gcb[:, :], L), ALU.subtract, [grow, gcb], [decT])
                        ts(p, decT[:, :, :], decT[:, :, :], 0.0, ALU.min, [decT], [decT])
                        act(p, decT[:, :, :], decT[:, :, :], AF.Exp, [decT], [decT])
                        tt(p, dec[:, :, :], bc_last(gcb[:, :], L), g3, ALU.subtract, [grow, gcb], [dec])
                        ts(p, dec[:, :, :], dec[:, :, :], 0.0, ALU.min, [dec], [dec])
                        act(p, dec[:, :, :], dec[:, :, :], AF.Exp, [dec], [dec])
                        tt(p, wA[:, :, :], decT[:, :, :], bc_mid(mT_incl[0:L, 0:L], nb), ALU.mult, [decT, mT_incl], [wA])
                        tt(p, decT[:, :, :], decT[:, :, :], bc_mid(mT_str[0:L, 0:L], nb), ALU.mult, [decT, mT_str], [decT])
                        tt(p, decT[:, :, :], decT[:, :, :], brow[:, :].rearrange("p (n t) -> p n t", t=L), ALU.mult, [decT, brow], [decT])
                        tt(p, dec[:, :, :], dec[:, :, :], bc_mid(m_str[0:L, 0:L], nb), ALU.mult, [dec, m_str], [dec])
                        tt(p, dec[:, :, :], dec[:, :, :], bc_last(nbb[:, :], L), ALU.mult, [dec, nbb], [dec])
                        X, Y = dec, decT

                        def ev_g(ps, n0, g):
                            v = ps[0:L, 0:g * L].rearrange("p (g k) -> p g k", k=L)
                            tt(p, X[:, n0:n0 + g, :], v, X[:, n0:n0 + g, :], ALU.mult, [ps, X], [X])
                            tt(p, Y[:, n0:n0 + g, :], v, Y[:, n0:n0 + g, :], ALU.mult, [ps, Y], [Y])

                        batched_chunk_mm(p, nb, lambda n: k_fm[:, (lo + n) * L:(lo + n + 1) * L], lambda n: k_fm[:, (lo + n) * L:(lo + n + 1) * L],
                                         [k_fm], L, L, ev_g)

                        def ev_a(ps, n0, g):
                            v = ps[0:L, 0:g * L].rearrange("p (g k) -> p g k", k=L)
                            tt(p, wA[:, n0:n0 + g, :], v, wA[:, n0:n0 + g, :], ALU.mult, [ps, wA], [wA])

                        batched_chunk_mm(p, nb, lambda n: k_fm[:, (lo + n) * L:(lo + n + 1) * L], lambda n: q_fm[:, (lo + n) * L:(lo + n + 1) * L],
                                         [k_fm, q_fm], L, L, ev_a)
                        tmp = {k_: p.sb("gd_ti" + k_, [L, nb, L]) for k_ in ("P", "Q", "X2", "Y2")}
                        Pm, Psubs = tri_inverse_T(p, c, X, Y, nb, tmp)
                        vb = p.sb("gd_vb", [L, nb, DH]); kbg = p.sb("gd_kbg", [L, nb, DH]); BL = p.sb("gd_BL", [L, nb, DH])
                        tt(p, vb[:, :, :], v_tm[:, lo:hi, :], bc_last(beta[:, lo:hi], DH), ALU.mult, [v_tm, beta], [vb])
                        tt(p, kbg[:, :, :], k_tm[:, lo:hi, :], bc_last(bege[:, lo:hi], DH), ALU.mult, [k_tm, bege], [kbg])
                        tt(p, BL[:, :, :], k_tm[:, lo:hi, :], bc_last(ekd[:, lo:hi], DH), ALU.mult, [k_tm, ekd], [BL], eng="pool")
                        U0 = p.sb("gd_U0", [L, nb, DH]); AhatT = p.sb("gd_Ah", [128, nb, L])

                        def ev_u(ps, n0, g):
                            cp(p, U0[:, n0:n0 + g, :], ps[0:L, 0:g * DH].rearrange("p (g k) -> p g k", k=DH), [ps], [U0], "act")

                        batched_chunk_mm(p, nb, lambda n: Pm[:, n, :], lambda n: vb[:, n, :], Psubs + [vb], L, DH, ev_u)

                        def ev_w(ps, n0, g):
                            ts(p, AhatT[:, n0:n0 + g, :], ps[0:128, 0:g * L].rearrange("p (g k) -> p g k", k=L), -1.0, ALU.mult, [ps], [AhatT])

                        batched_chunk_mm(p, nb, lambda n: kbg[:, n, :], lambda n: Pm[:, n, :], Psubs + [kbg], 128, L, ev_w)
                        act(p, grow[:, :], grow[:, :], AF.Exp, [grow], [grow])
                        tt(p, grow[:, :], grow[:, :], q_fm[:, lo * L:hi * L], ALU.mult, [grow, q_fm], [grow])
                        Yo = p.sb("gd_Yo", [L, nb, DH])
                        order = list(range(nb)) if d == 0 else list(range(nb - 1, -1, -1))
                        dcy = p.sb("gd_dcy", [128, nb])
                        cp(p, dcy[:, :], decay[:, lo:hi], [decay], [dcy], "dve")
                        chain = dict(dk=128, dvp=DH, order=order, QT=grow, A1T=wA, BL=BL, decay=dcy, U0=U0, AhatT=AhatT, Y=Yo,
                                     K0=None, M=Ms, ub=ubs, R=[], st=st)
                        chunk_loops(p, [chain])
                        Ysubs = [Yo.sub(n) for n in range(nb)]
                        if d == 0:
                            cp(p, O[:, lo:hi, :], Yo[:, :, :], Ysubs, [O], "dve")
                        else:
                            tt(p, O[:, lo:hi, :], O[:, lo:hi, :], Yo[:, :, :], ALU.add, Ysubs + [O], [O])
        p.gd_big.close()
        with p.scope():
            sq = p.sb("gd_sq", [L, nch, DH]); ss = p.sb("gd_ss", [L, nch]); z = p.sb("gd_z", [L, nch, DH])
            p.dma("sp", z[:, :, :], io["z_tm"], writes=[z])
            tt(p, sq[:, :, :], O[:, :, :], O[:, :, :], ALU.mult, [O], [sq])
            p.op("dve", lambda e: e.tensor_reduce(out=ss[:, :], in_=sq[:, :, :], axis=AX.X, op=ALU.add), reads=[sq], writes=[ss])
            act(p, ss[:, :], ss[:, :], AF.Sqrt, [ss, c["eps6"]], [ss], scale=1.0 / DH, bias=c["eps6"][0:L, 0:1])
            p.op("dve", lambda e: e.reciprocal(out=ss[:, :], in_=ss[:, :]), reads=[ss], writes=[ss])
            tt(p, O[:, :, :], O[:, :, :], bc_last(ss[:, :], DH), ALU.mult, [O, ss], [O])
            tt(p, O[:, :, :], O[:, :, :], bc_mid(ngb[:, :], nch), ALU.mult, [O, ngb], [O])
            act(p, z[:, :, :], z[:, :, :], AF.Silu, [z], [z])
            tt(p, O[:, :, :], O[:, :, :], z[:, :, :], ALU.mult, [O, z], [O])
            p.dma("pool", io["yc"], O[:, :, :], reads=[O], is_output=True)


def emit_rwkv(p, c, io, nc_ctx, nc_lat, bs=16, heads=(0, 1)):
    nch = nc_ctx + nc_lat
    T = nch * L
    Tc, Tl = nc_ctx * L, nc_lat * L
    N = 64

    def shift3(dst_ap, raw, P_, TB, mu0, mu1, c0, R, W):
        ts(p, dst_ap, raw[0:P_, 1:1 + TB], c0, ALU.mult, R, W)
        stt(p, dst_ap, raw[0:P_, 0:TB], mu0, dst_ap, ALU.mult, ALU.add, R + W, W)
        stt(p, dst_ap, raw[0:P_, 2:2 + TB], mu1, dst_ap, ALU.mult, ALU.add, R + W, W)

    def coef(mu_tn, ncol, P_, name):
        co = p.sb(name, [P_, ncol // 2])
        for i in range(ncol // 2):
            tt(p, co[:, i:i + 1], mu_tn[:, 2 * i:2 * i + 1], mu_tn[:, 2 * i + 1:2 * i + 2], ALU.add, [mu_tn], [co])
        ts(p, co[:, :], co[:, :], -1.0, ALU.mult, [co], [co], s2=1.0, op1=ALU.add)
        return co

    def poff(lo):
        return lo * L if lo < nc_ctx else (Tc + 2) + (lo - nc_ctx) * L

    with p.scope():
        muw = p.sb("rw_muw", [96, 2]); mua = p.sb("rw_mua", [96, 2]); mug = p.sb("rw_mug", [128, 4])
        p.dma("sp", muw[:, :], io["muw"], writes=[muw]); p.dma("sp", mua[:, :], io["mua"], writes=[mua])
        p.dma("sp", mug[:, :], io["mug"].rearrange("p a b -> p (a b)"), writes=[mug])
        cw_ = coef(muw, 2, 96, "rw_cw"); ca_ = coef(mua, 2, 96, "rw_ca"); cg_ = coef(mug, 4, 128, "rw_cg")
        for h in heads:
            with p.scope():
                mu = p.sb("rw_mu", [N, 6]); pr = p.sb("rw_pr", [N, 9]); w2 = p.sb("rw_w2", [96, 2, N]); a2 = p.sb("rw_a2", [96, 2, N])
                g2 = p.sb("rw_g2", [128, 2, N])
                p.dma("sp", mu[:, :], io[f"mu{h}"], writes=[mu]); p.dma("sp", pr[:, :], io[f"pr{h}"], writes=[pr])
                p.dma("sp", w2[:, :, :], io[f"w2_{h}"], writes=[w2]); p.dma("sp", a2[:, :, :], io[f"a2_{h}"], writes=[a2])
                p.dma("sp", g2[:, :, :], io[f"g2_{h}"], writes=[g2])
                cm = coef(mu, 6, N, "rw_cm")
                omka = p.sb("rw_omka", [N, 1])
                ts(p, omka[:, :], pr[:, 5:6], -1.0, ALU.mult, [pr], [omka], s2=1.0, op1=ALU.add)
                O = p.sb("rw_O", [L, nch, N]); aux0 = p.sb("rw_aux0", [N, T]); gg = p.sb("rw_gg", [N, T])
                for d in range(2):
                    mT_incl, mT_str, m_incl, m_str = dir_masks(c, d)
                    Ms = [p.sb(f"rw_M{i}", [N, N]) for i in range(2)]
                    ubs = [p.sb(f"rw_u{i}", [L, N]) for i in range(2)]
                    st = {}
                    for (lo, hi) in blocks_for(d, nc_ctx, nc_lat, bs):
                        nb = hi - lo
                        TB = nb * L
                        off = poff(lo)
                        with p.scope():
                            at = p.sb("rw_at", [N, TB]); bt = p.sb("rw_bt", [N, TB]); kt = p.sb("rw_kt", [N, TB]); rt = p.sb("rw_rt", [N, TB])
                            bL = p.sb("rw_bL", [N, TB]); kL = p.sb("rw_kL", [N, TB]); v = p.sb("rw_v", [N, TB]); WL = p.sb("rw_WL", [N, nb])
                            with p.scope():
                                raws = [p.sb(f"rw_raw{i_}", [128, TB + 2]) for i_ in range(5)]; raw2 = p.sb("rw_raw2", [128, TB + 2])
                                r = p.sb("rw_r", [N, TB]); k = p.sb("rw_k", [N, TB])
                                xw = p.sb("rw_xw", [96, TB]); xa = p.sb("rw_xa", [96, TB])
                                for nm, dst, ci in (("rp", r, 0), ("kp", k, 1), ("vp", v, 2)):
                                    raw = raws[ci]
                                    p.dma("sp", raw[0:N, :], io[f"{nm}{h}"][:, off:off + TB + 2], writes=[raw])
                                    shift3(dst[:, :], raw, N, TB, mu[:, 2 * ci:2 * ci + 1], mu[:, 2 * ci + 1:2 * ci + 2], cm[:, ci:ci + 1], [raw, mu, cm], [dst])
                                raw = raws[3]
                                p.dma("sp", raw[0:96, :], io["xwp"][:, off:off + TB + 2], writes=[raw])
                                shift3(xw[:, :], raw, 96, TB, muw[:, 0:1], muw[:, 1:2], cw_[:, 0:1], [raw, muw, cw_], [xw])
                                act(p, xw[:, :], xw[:, :], AF.Tanh, [xw], [xw])
                                rawa = raws[4]
                                p.dma("sp", rawa[0:96, :], io["xap"][:, off:off + TB + 2], writes=[rawa])
                                shift3(xa[:, :], rawa, 96, TB, mua[:, 0:1], mua[:, 1:2], ca_[:, 0:1], [rawa, mua, ca_], [xa])
                                logw = p.sb("rw_logw", [N, TB]); iclr = p.sb("rw_iclr", [N, TB]); kk = p.sb("rw_kk", [N, TB]); tmp = p.sb("rw_tmp", [N, TB])
                                kmod = p.sb("rw_kmod", [N, TB]); b_ = p.sb("rw_b", [N, TB])

                                def lora(dst, wt, dd, xin, bias_col):
                                    for c0 in range(0, TB, 512):
                                        c1 = min(c0 + 512, TB)
                                        ps = p.next_ps()
                                        mm(p, ps[0:N, 0:c1 - c0], wt[:, dd, :], xin[:, c0:c1], [wt, xin], [ps])
                                        act(p, dst[:, c0:c1], ps[0:N, 0:c1 - c0], AF.Sigmoid, [ps, pr], [dst], bias=pr[:, bias_col:bias_col + 1])

                                lora(logw, w2, d, xw, 0 + d)
                                ts(p, logw[:, :], logw[:, :], -0.6065306597126334, ALU.mult, [logw], [logw])
                                lora(iclr, a2, d, xa, 2 + d)
                                ts(p, kk[:, :], k[:, :], pr[:, 4:5], ALU.mult, [k, pr], [kk])
                                act(p, tmp[:, :], kk[:, :], AF.Square, [kk], [tmp])
                                for c0 in range(0, TB, 512):
                                    c1 = min(c0 + 512, TB)
                                    ps = p.next_ps()
                                    mm(p, ps[0:N, 0:c1 - c0], c["ones"][0:N, 0:N], tmp[:, c0:c1], [c["ones"], tmp], [ps])
                                    act(p, tmp[:, c0:c1], ps[0:N, 0:c1 - c0], AF.Sqrt, [ps, c["eps6"]], [tmp], bias=c["eps6"][0:N, 0:1])
                                p.op("dve", lambda e: e.reciprocal(out=tmp[:, :], in_=tmp[:, :]), reads=[tmp], writes=[tmp])
                                tt(p, kk[:, :], kk[:, :], tmp[:, :], ALU.mult, [kk, tmp], [kk])
                                ts(p, kmod[:, :], iclr[:, :], pr[:, 5:6], ALU.mult, [iclr, pr, omka], [kmod], s2=omka[:, 0:1], op1=ALU.add)
                                tt(p, kmod[:, :], kmod[:, :], k[:, :], ALU.mult, [kmod, k], [kmod], eng="pool")
                                tt(p, b_[:, :], kk[:, :], iclr[:, :], ALU.mult, [kk, iclr], [b_], eng="pool")
                                if d == 0:
                                    icl1 = p.sb("rw_icl1", [N, TB])
                                    lora(icl1, a2, 1, xa, 3)
                                    ts(p, icl1[:, :], icl1[:, :], pr[:, 5:6], ALU.mult, [icl1, pr, omka], [icl1], s2=omka[:, 0:1], op1=ALU.add)
                                    tt(p, icl1[:, :], icl1[:, :], k[:, :], ALU.mult, [icl1, k], [icl1])
                                    tt(p, icl1[:, :], icl1[:, :], kmod[:, :], ALU.add, [icl1, kmod], [icl1])
                                    stt(p, icl1[:, :], r[:, :], pr[:, 6:7], icl1[:, :], ALU.mult, ALU.mult, [r, pr, icl1], [icl1])
                                    xg = p.sb("rw_xg", [128, 2, TB])
                                    for kc in range(2):
                                        p.dma("sp", raw2[:, :], io["xgp"][:, kc, off:off + TB + 2], writes=[raw2])
                                        shift3(xg[:, kc, :], raw2, 128, TB, mug[:, 2 * kc:2 * kc + 1], mug[:, 2 * kc + 1:2 * kc + 2], cg_[:, kc:kc + 1],
                                               [raw2, mug, cg_], [xg])
                                    act(p, xg[:, :, :], xg[:, :, :], AF.Sigmoid, [xg], [xg])
                                    for c0 in range(0, TB, 512):
                                        c1 = min(c0 + 512, TB)
                                        ps = p.next_ps()
                                        mm(p, ps[0:N, 0:c1 - c0], c["ones"][0:N, 0:N], icl1[:, c0:c1], [c["ones"], icl1], [ps])
                                        tt(p, aux0[:, lo * L + c0:lo * L + c1], ps[0:N, 0:c1 - c0], v[:, c0:c1], ALU.mult, [ps, v], [aux0])
                                        ps2 = p.next_ps()
                                        for kc in range(2):
                                            mm(p, ps2[0:N, 0:c1 - c0], g2[:, kc, :], xg[:, kc, c0:c1], [g2, xg], [ps2], start=(kc == 0), stop=(kc == 1))
                                        cp(p, gg[:, lo * L + c0:lo * L + c1], ps2[0:N, 0:c1 - c0], [ps2], [gg], "act")
                                rmask = p.sb("rw_rmask", [N, nb, L]); lW = p.sb("rw_lW", [N, TB]); e1 = p.sb("rw_e1", [N, TB])
                                p.op("pool", lambda e: e.memset(rmask[:, :, :], 1.0), writes=[rmask])
                                p.op("pool", lambda e: e.memset(rmask[:, :, 0:1], 0.0), writes=[rmask])
                                p.op("dve", lambda e: e.tensor_tensor_scan(out=lW[:, :], data0=rmask[:, :, :].rearrange("p n t -> p (n t)"),
                                                                          data1=logw[:, :], initial=0.0, op0=ALU.mult, op1=ALU.add),
                                     reads=[rmask, logw], writes=[lW])
                                lW3 = lW[:, :].rearrange("p (n t) -> p n t", t=L)
                                tot = p.sb("rw_tot", [N, nb])
                                cp(p, tot[:, :], lW3[:, :, L - 1], [lW], [tot], "dve")
                                if d == 1:
                                    tt(p, lW3, bc_last(tot[:, :], L), lW3, ALU.subtract, [tot, lW], [lW])
                                    tt(p, lW[:, :], lW[:, :], logw[:, :], ALU.add, [lW, logw], [lW])
                                act(p, WL[:, :], tot[:, :], AF.Exp, [tot], [WL])
                                tt(p, e1[:, :], lW[:, :], logw[:, :], ALU.subtract, [lW, logw], [e1])
                                act(p, e1[:, :], e1[:, :], AF.Exp, [e1], [e1])
                                stt(p, at[:, :], kk[:, :], -1.0, e1[:, :], ALU.mult, ALU.mult, [kk, e1], [at])
                                act(p, e1[:, :], lW[:, :], AF.Exp, [lW], [e1], scale=-1.0)
                                tt(p, bt[:, :], b_[:, :], e1[:, :], ALU.mult, [b_, e1], [bt])
                                tt(p, kt[:, :], kmod[:, :], e1[:, :], ALU.mult, [kmod, e1], [kt], eng="pool")
                                act(p, e1[:, :], lW[:, :], AF.Exp, [lW], [e1])
                                tt(p, rt[:, :], r[:, :], e1[:, :], ALU.mult, [r, e1], [rt])
                                tt(p, e1[:, :].rearrange("p (n t) -> p n t", t=L), bc_last(tot[:, :], L), lW3, ALU.subtract, [tot, lW], [e1])
                                act(p, e1[:, :], e1[:, :], AF.Exp, [e1], [e1])
                                tt(p, bL[:, :], b_[:, :], e1[:, :], ALU.mult, [b_, e1], [bL])
                                tt(p, kL[:, :], kmod[:, :], e1[:, :], ALU.mult, [kmod, e1], [kL], eng="pool")
                            ch = lambda a, n: a[:, n * L:(n + 1) * L]
                            X = p.sb("rw_X", [L, nb, L]); Y = p.sb("rw_Y", [L, nb, L]); AakT = p.sb("rw_Aak", [L, nb, L])
                            ArbT = p.sb("rw_Arb", [L, nb, L]); ArkT = p.sb("rw_Ark", [L, nb, L])

                            def mk(dst, la, ra, mask):
                                def ev(ps, n0, g):
                                    tt(p, dst[:, n0:n0 + g, :], ps[0:L, 0:g * L].rearrange("p (g k) -> p g k", k=L), bc_mid(mask[0:L, 0:L], g), ALU.mult,
                                       [ps, mask], [dst])
                                batched_chunk_mm(p, nb, lambda n: ch(la, n), lambda n: ch(ra, n), [la, ra], L, L, ev)

                            mk(X, at, bt, m_str); mk(Y, bt, at, mT_str); mk(AakT, kt, at, mT_str); mk(ArbT, bt, rt, mT_incl); mk(ArkT, kt, rt, mT_incl)
                            at_tm = p.sb("rw_attm", [L, nb, N]); bL_tm = p.sb("rw_bLtm", [L, nb, N]); kL_tm = p.sb("rw_kLtm", [L, nb, N]); v_tm = p.sb("rw_vtm", [L, nb, N])
                            for src, dst in ((at, at_tm), (bL, bL_tm), (kL, kL_tm), (v, v_tm)):
                                transpose_chunks(p, c, (lambda s_: (lambda n: ch(s_, n)))(src), [src], N, dst, nb)
                            tmpi = {k_: p.sb("rw_ti" + k_, [L, nb, L]) for k_ in ("P", "Q", "X2", "Y2")}
                            Pm, Psubs = tri_inverse_T(p, c, X, Y, nb, tmpi)
                            cAk = p.sb("rw_cAk", [L, nb, N]); U0 = p.sb("rw_U0", [L, nb, N]); AhatT = p.sb("rw_Ah", [N, nb, L]); Y0 = p.sb("rw_Y0", [L, nb, N])

                            def evto(dst, rows, eng):
                                def ev(ps, n0, g):
                                    w_ = dst.t.shape[2]
                                    cp(p, dst[:, n0:n0 + g, :], ps[0:rows, 0:g * w_].rearrange("p (g k) -> p g k", k=w_), [ps], [dst], eng)
                                return ev

                            batched_chunk_mm(p, nb, lambda n: AakT[:, n, :], lambda n: v_tm[:, n, :], [AakT, v_tm], L, N, evto(cAk, L, "act"))
                            batched_chunk_mm(p, nb, lambda n: Pm[:, n, :], lambda n: cAk[:, n, :], Psubs + [cAk], L, N, evto(U0, L, "dve"))
                            batched_chunk_mm(p, nb, lambda n: at_tm[:, n, :], lambda n: Pm[:, n, :], Psubs + [at_tm], N, L, evto(AhatT, N, "act"))
                            batched_chunk_mm(p, nb, lambda n: ArkT[:, n, :], lambda n: v_tm[:, n, :], [ArkT, v_tm], L, N, evto(Y0, L, "dve"))
                            Yo = p.sb("rw_Yo", [L, nb, N])
                            order = list(range(nb)) if d == 0 else list(range(nb - 1, -1, -1))
                            chain = dict(dk=N, dvp=N, order=order, QT=rt, A1T=ArbT, BL=bL_tm, decay=WL, U0=U0, AhatT=AhatT, Y=Yo,
                                         K0=(kL_tm, v_tm), M=Ms, ub=ubs, R=[], st=st)
                            chunk_loops(p, [chain])
                            Ysubs = [Yo.sub(n) for n in range(nb)]
                            if d == 0:
                                tt(p, O[:, lo:hi, :], Yo[:, :, :], Y0[:, :, :], ALU.add, Ysubs + [Y0], [O])
                            else:
                                tt(p, Yo[:, :, :], Yo[:, :, :], Y0[:, :, :], ALU.add, Ysubs + [Y0], [Yo])
                                tt(p, O[:, lo:hi, :], O[:, lo:hi, :], Yo[:, :, :], ALU.add, [Yo, O], [O])
                with p.scope():
                    Of = p.sb("rw_Of", [N, nch, L]); sq = p.sb("rw_fsq", [N, T]); mean = p.sb("rw_mean", [N, T])
                    g5 = p.sb("rw_g5", [N, 1])
                    p.op("dve", lambda e: e.memset(g5[:, :], 64e-5), writes=[g5])
                    transpose_chunks(p, c, lambda n: O[:, n, :], [O], L, Of, nch)
                    Off = Of[:, :, :].rearrange("p n t -> p (n t)")
                    for c0 in range(0, T, 512):
                        c1 = min(c0 + 512, T)
                        ps = p.next_ps()
                        mm(p, ps[0:N, 0:c1 - c0], c["ones"][0:N, 0:N], Off[:, c0:c1], [c["ones"], Of], [ps])
                        stt(p, mean[:, c0:c1], ps[0:N, 0:c1 - c0], -1.0 / N, Off[:, c0:c1], ALU.mult, ALU.add, [ps, Of], [mean])
                    act(p, sq[:, :], mean[:, :], AF.Square, [mean], [sq])
                    for c0 in range(0, T, 512):
                        c1 = min(c0 + 512, T)
                        ps = p.next_ps()
                        mm(p, ps[0:N, 0:c1 - c0], c["ones"][0:N, 0:N], sq[:, c0:c1], [c["ones"], sq], [ps])
                        act(p, sq[:, c0:c1], ps[0:N, 0:c1 - c0], AF.Sqrt, [ps, g5], [sq], scale=1.0 / N, bias=g5[:, 0:1])
                    p.op("dve", lambda e: e.reciprocal(out=sq[:, :], in_=sq[:, :]), reads=[sq], writes=[sq])
                    tt(p, mean[:, :], mean[:, :], sq[:, :], ALU.mult, [mean, sq], [mean])
                    ts(p, mean[:, :], mean[:, :], pr[:, 7:8], ALU.mult, [mean, pr], [mean], s2=pr[:, 8:9], op1=ALU.add)
                    tt(p, mean[:, :], mean[:, :], aux0[:, :], ALU.add, [mean, aux0], [mean])
                    tt(p, mean[:, :], mean[:, :], gg[:, :], ALU.mult, [mean, gg], [mean])
                    p.dma("pool", io[f"yd{h}"], mean[:, :], reads=[mean], is_output=True)


D = 2048
KC = 16
DFF = 5632
FC = DFF // 128
TL = 1024
TCX = 64
TT = TL + TCX
SEGS = [(0, TL, 0), (TL, TT, 1)]


def ttiles(T, w=512):
    return [(t0, min(t0 + w, T)) for t0 in range(0, T, w)]


def load_fm(p, q, dst, src_ap, kc_n):
    v = src_ap.rearrange("(kc p) t -> p kc t", p=128)
    for kc in range(kc_n):
        p.dma(q, dst[:, kc, :], v[:, kc, :], writes=[dst.sub(kc)])


def rms_stats(p, c, x, xsub, T, sq, rstd, eps_ap, kc_n=KC, dim=D):
    for kc in range(kc_n):
        act(p, sq[:, kc, :], x[:, kc, :], AF.Square, [xsub(kc)], [sq.sub(kc)])
    for ti, (t0, t1) in enumerate(ttiles(T)):
        ps = p.next_ps()
        for kc in range(kc_n):
            mm(p, ps[:, 0:t1 - t0], c["ones"][:, :], sq[:, kc, t0:t1], [c["ones"], sq.sub(kc)], [ps], start=(kc == 0), stop=(kc == kc_n - 1))
        act(p, rstd[:, t0:t1], ps[:, 0:t1 - t0], AF.Sqrt, [ps], [rstd], scale=1.0 / dim, bias=eps_ap)
    p.op("dve", lambda e: e.reciprocal(out=rstd[:, :], in_=rstd[:, :]), reads=[rstd], writes=[rstd])


def mod_prep(p, mv, gs, sh, gi, sci, shi):
    for j in range(2):
        ts(p, gs[:, :, j:j + 1], mv[:, :, sci[j]:sci[j] + 1], 1.0, ALU.add, [mv], [gs])
        tt(p, gs[:, :, j:j + 1], gs[:, :, j:j + 1], mv[:, :, gi:gi + 1], ALU.mult, [mv, gs], [gs])
        cp(p, sh[:, :, j:j + 1], mv[:, :, shi[j]:shi[j] + 1], [mv], [sh], "dve")


def norm_mod(p, c, x, xsub, h, T, segs, gs, sh, rstd, out=None):
    out = h if out is None else out
    for kc in range(KC):
        tt(p, h[:, kc, :], x[:, kc, :], rstd[:, 0:T], ALU.mult, [xsub(kc), rstd], [h.sub(kc)])
        for (t0, t1, col) in segs:
            act(p, out[:, kc, t0:t1], h[:, kc, t0:t1], AF.Identity, [h.sub(kc), gs, sh], [out.sub(kc)],
                scale=gs[:, kc, col:col + 1], bias=sh[:, kc, col:col + 1])


def rms_stats2(p, c, x, xsub, T, rstd, eps_ap, sqbufs, kc_n=KC, dim=D, t_off=0):
    tts = ttiles(T)
    pss = [p.next_ps() for _ in tts]
    for kc in range(kc_n):
        sq = sqbufs[kc % len(sqbufs)]
        act(p, sq[:, 0:T], x[:, kc, t_off:t_off + T], AF.Square, [xsub(kc)], [sq])
        for ps, (t0, t1) in zip(pss, tts):
            mm(p, ps[:, 0:t1 - t0], c["ones"][:, :], sq[:, t0:t1], [c["ones"], sq], [ps], start=(kc == 0), stop=(kc == kc_n - 1))
    for ps, (t0, t1) in zip(pss, tts):
        act(p, rstd[:, t0:t1], ps[:, 0:t1 - t0], AF.Sqrt, [ps], [rstd], scale=1.0 / dim, bias=eps_ap)
    p.op("dve", lambda e: e.reciprocal(out=rstd[:, 0:T], in_=rstd[:, 0:T]), reads=[rstd], writes=[rstd])


def gemm_fm(p, h, hsub, kc_n, tts, W_ap, N, wbufs, evac, q="sp", NB=256, wkey=[0], kc_off=0):
    Wv = W_ap.rearrange("(kc p) n -> p kc n", p=128)
    nblk = (N + NB - 1) // NB
    for nb in range(nblk):
        n0 = nb * NB
        nsz = min(NB, N - n0)
        wb = wbufs[wkey[0] % len(wbufs)]
        wkey[0] += 1
        half = kc_n // 2
        p.dma(q, wb[:, 0:half, 0:nsz], Wv[:, 0:half, n0:n0 + nsz], writes=[wb.sub(0)])
        p.dma(q, wb[:, half:kc_n, 0:nsz], Wv[:, half:kc_n, n0:n0 + nsz], writes=[wb.sub(1)])
        for mo in range(0, nsz, 128):
            msz = min(128, nsz - mo)
            for (t0, t1) in tts:
                ps = p.next_ps()
                for kc in range(kc_n):
                    mm(p, ps[0:msz, 0:t1 - t0], wb[:, kc, mo:mo + msz], h[:, kc_off + kc, t0:t1],
                       [wb.sub(0 if kc < half else 1), hsub(kc_off + kc)], [ps], start=(kc == 0), stop=(kc == kc_n - 1))
                evac(n0 + mo, msz, t0, t1, ps)


def cast_engine(i):
    return ("act", "dve")[i % 2]


def gemm_bf(p, h, hsub, kc_n, tts, W_ap, N, wst, wbf, evac, q="sp", NB=256, wkey=[0], kc_off=0):
    Wv = W_ap.rearrange("(kc p) n -> p kc n", p=128)
    nblk = (N + NB - 1) // NB
    for nb in range(nblk):
        n0 = nb * NB
        nsz = min(NB, N - n0)
        ws = wst[wkey[0] % len(wst)]
        wb = wbf[wkey[0] % len(wbf)]
        half = kc_n // 2
        p.dma(q, ws[:, 0:half, 0:nsz], Wv[:, 0:half, n0:n0 + nsz], writes=[ws.sub(0)])
        p.dma(q, ws[:, half:kc_n, 0:nsz], Wv[:, half:kc_n, n0:n0 + nsz], writes=[ws.sub(1)])
        cp(p, wb[:, 0:half, 0:nsz], ws[:, 0:half, 0:nsz], [ws.sub(0)], [wb.sub(0)], cast_engine(wkey[0]))
        cp(p, wb[:, half:kc_n, 0:nsz], ws[:, half:kc_n, 0:nsz], [ws.sub(1)], [wb.sub(1)], cast_engine(wkey[0] + 1))
        wkey[0] += 1
        for mo in range(0, nsz, 128):
            msz = min(128, nsz - mo)
            for (t0, t1) in tts:
                ps = p.next_ps()
                for kc in range(kc_n):
                    mm(p, ps[0:msz, 0:t1 - t0], wb[:, kc, mo:mo + msz], h[:, kc_off + kc, t0:t1],
                       [wb.sub(0 if kc < half else 1), hsub(kc_off + kc)], [ps], start=(kc == 0), stop=(kc == kc_n - 1))
                evac(n0 + mo, msz, t0, t1, ps)


def build_mod(NCOL=6144):
    p = P()
    p.init_ps(8)
    c = make_consts(p)
    cT = p.dram("cT", [D, 3], "ExternalInput"); W = p.dram("W", [D, NCOL], "ExternalInput"); b = p.dram("b", [128, NCOL // 128], "ExternalInput")
    out = p.dram("out", [128, NCOL // 128, 3], "ExternalOutput")
    cs = p.sb("cs", [128, KC, 3]); bs = p.sb("bs", [128, NCOL // 128]); o = p.sb("o", [128, NCOL // 128, 3])
    wbufs = [p.sb(f"wb{i}", [128, KC, 256]) for i in range(3)]
    p.dma("sp", cs[:, :, :], cT.rearrange("(kc p) t -> p kc t", p=128), writes=[cs])
    p.dma("sp", bs[:, :], b, writes=[bs])
    sg = p.sb("sg", [128, KC, 3])
    act(p, sg[:, :, :], cs[:, :, :], AF.Sigmoid, [cs], [sg])
    tt(p, cs[:, :, :], cs[:, :, :], sg[:, :, :], ALU.mult, [cs, sg], [cs])

    def evac(m0, msz, t0, t1, ps):
        j = m0 // 128
        ts(p, o[:, j, :], ps[:, 0:3], bs[:, j:j + 1], ALU.add, [ps, bs], [o])

    gemm_fm(p, cs, lambda kc: cs, KC, [(0, 3)], W, NCOL, wbufs, evac)
    p.dma("pool", out, o[:, :, :], reads=[o], is_output=True)
    p.finish("sp"); p.close()
    return p


def build_pre(N=15328):
    p = P()
    p.init_ps(8)
    c = make_consts(p)
    xT = p.dram("xT", [D, TT], "ExternalInput"); W = p.dram("W", [D, N], "ExternalInput")
    mvd = p.dram("mv", [128, KC, 5], "ExternalInput")
    zT = p.dram("zT", [N, TT], "ExternalOutput")
    hb = p.sb("hb", [128, KC, TT], BF16)
    mv = p.sb("mvs", [128, KC, 5]); gs = p.sb("gs", [128, KC, 2]); sh = p.sb("sh", [128, KC, 2])
    p.dma("sp", mv[:, :, :], mvd, writes=[mv])
    mod_prep(p, mv, gs, sh, 0, (1, 3), (2, 4))
    with p.scope():
        x = p.sb("x", [128, KC, TT]); rstd = p.sb("rstd", [128, TT]); sqb = [p.sb(f"sqb{i}", [128, TT]) for i in range(2)]
        load_fm(p, "sp", x, xT, KC)
        rms_stats2(p, c, x, x.sub, TT, rstd, c["eps6"][:, 0:1], sqb)
        norm_mod(p, c, x, x.sub, x, TT, SEGS, gs, sh, rstd, out=hb)
    wst = [p.sb(f"ws{i}", [128, KC, 256]) for i in range(3)]
    wbf = [p.sb(f"wb{i}", [128, KC, 256], BF16) for i in range(3)]
    obufs = [p.sb(f"ob{i}", [128, TT]) for i in range(3)]
    oi = [0]
    tts = ttiles(TT)

    def evac(m0, msz, t0, t1, ps):
        ob = obufs[oi[0] % 3]
        cp(p, ob[0:msz, t0:t1], ps[0:msz, 0:t1 - t0], [ps], [ob.sub(t0)], "dve" if (t0 // 512) % 2 == 0 else "act")
        if t1 == TT:
            p.dma("pool", zT[m0:m0 + msz, :], ob[0:msz, :], reads=[ob.sub(t[0]) for t in tts], is_output=True)
            oi[0] += 1

    gemm_bf(p, hb, hb.sub, KC, tts, W, N, wst, wbf, evac)
    p.finish("sp"); p.close()
    return p


def build_post():
    p = P()
    p.init_ps(8)
    c = make_consts(p)
    PADC = 15
    LA = TL + 2 * PADC; LC = TCX + 2 * PADC
    ap_d = p.dram("a_p", [512, LA + LC], "ExternalInput"); gp_d = p.dram("g_p", [512, LA + LC], "ExternalInput")
    cvw = p.dram("cvw", [128, 4, 34], "ExternalInput")
    ybcd = p.dram("ybcd", [1536, TT], "ExternalInput")
    gates = p.dram("gates", [8192, TT], "ExternalInput")
    xT = p.dram("xT", [D, TT], "ExternalInput")
    wbr = p.dram("wbr", [4, 512, D], "ExternalInput"); wout = p.dram("wout", [D, D], "ExternalInput")
    mvd = p.dram("mv", [128, KC, 3], "ExternalInput")
    x1T = p.dram("x1T", [D, TT], "ExternalOutput")
    tts = ttiles(TT)
    mv = p.sb("mvs", [128, KC, 3]); gg = p.sb("gg", [128, KC, 2])
    p.dma("sp", mv[:, :, :], mvd, writes=[mv])
    for j in range(2):
        tt(p, gg[:, :, j:j + 1], mv[:, :, 0:1], mv[:, :, 1 + j:2 + j], ALU.mult, [mv], [gg])
    Yb = p.sb("Yall", [128, 16, TT], BF16)
    merged_b = p.sb("merged_b", [128, KC, TT], BF16)
    with p.scope():
        ya = p.sb("ya", [128, 4, TT])
        cw = p.sb("cw", [128, 4, 34])
        p.dma("sp", cw[:, :, :], cvw, writes=[cw])
        a = p.sb("cv_a", [128, 4, LA + LC]); g = p.sb("cv_g", [128, 4, LA + LC])
        p.dma("sp", a[:, :, :], ap_d.rearrange("(kc p) t -> p kc t", p=128), writes=[a])
        p.dma("sp", g[:, :, :], gp_d.rearrange("(kc p) t -> p kc t", p=128), writes=[g])
        act(p, g[:, :, :], g[:, :, :], AF.Sigmoid, [g], [g])
        tt(p, a[:, :, :], a[:, :, :], g[:, :, :], ALU.mult, [a, g], [a])
        for kc in range(4):
            for (o_in, o_out, tl) in ((0, 0, TL), (LA, TL, TCX)):
                dst = ya[:, kc, o_out:o_out + tl]
                ts(p, dst, a[:, kc, o_in:o_in + tl], cw[:, kc, 0:1], ALU.mult, [a, cw], [ya.sub(kc)], s2=cw[:, kc, 31:32], op1=ALU.add)
                for j in range(1, 31):
                    stt(p, dst, a[:, kc, o_in + j:o_in + j + tl], cw[:, kc, j:j + 1], dst, ALU.mult, ALU.add, [a, cw, ya.sub(kc)], [ya.sub(kc)])
        xc = p.sb("cv_xc", [128, 4, TT]); sq = g; rs = p.sb("cv_rs", [128, TT])
        for (t0, t1) in tts:
            ps = p.next_ps()
            for kc in range(4):
                mm(p, ps[:, 0:t1 - t0], c["ones"][:, :], ya[:, kc, t0:t1], [c["ones"], ya.sub(kc)], [ps], start=(kc == 0), stop=(kc == 3))
            for kc in range(4):
                stt(p, xc[:, kc, t0:t1], ps[:, 0:t1 - t0], -1.0 / 512, ya[:, kc, t0:t1], ALU.mult, ALU.add, [ps, ya.sub(kc)], [xc.sub(kc)])
        for kc in range(4):
            act(p, sq[:, kc, 0:TT], xc[:, kc, :], AF.Square, [xc.sub(kc)], [sq])
        for (t0, t1) in tts:
            ps = p.next_ps()
            for kc in range(4):
                mm(p, ps[:, 0:t1 - t0], c["ones"][:, :], sq[:, kc, t0:t1], [c["ones"], sq], [ps], start=(kc == 0), stop=(kc == 3))
            act(p, rs[:, t0:t1], ps[:, 0:t1 - t0], AF.Sqrt, [ps], [rs], scale=1.0 / 512, bias=c["eps6"][:, 0:1])
        p.op("dve", lambda e: e.reciprocal(out=rs[:, :], in_=rs[:, :]), reads=[rs], writes=[rs])
        for kc in range(4):
            tt(p, xc[:, kc, :], xc[:, kc, :], rs[:, :], ALU.mult, [xc.sub(kc), rs], [xc.sub(kc)])
            act(p, xc[:, kc, :], xc[:, kc, :], AF.Identity, [xc.sub(kc), cw], [xc.sub(kc)], scale=cw[:, kc, 32:33], bias=cw[:, kc, 33:34])
            act(p, Yb[:, kc, :], xc[:, kc, :], AF.Silu, [xc.sub(kc)], [Yb.sub(kc)])
    with p.scope():
        merged = p.sb("merged", [128, KC, TT])
        stg = [p.sb(f"ystg{i}", [128, TT]) for i in range(2)]
        yv = ybcd.rearrange("(kc p) t -> p kc t", p=128)
        for kc in range(12):
            sb_ = stg[kc % 2]
            p.dma("sp", sb_[:, :], yv[:, kc, :], writes=[sb_])
            cp(p, Yb[:, 4 + kc, :], sb_[:, :], [sb_], [Yb.sub(4 + kc)], cast_engine(kc))
        wst = [p.sb(f"ws{i}", [128, 4, 256]) for i in range(3)]
        wbf = [p.sb(f"wb{i}", [128, 4, 256], BF16) for i in range(3)]
        gbufs = [p.sb(f"gb{i}", [128, TT]) for i in range(3)]
        gv = gates.rearrange("(n kc p) t -> n p kc t", p=128, kc=KC)
        gi = [0]
        for n in range(4):
            def evac(m0, msz, t0, t1, ps, n=n):
                kc = m0 // 128
                gb = gbufs[gi[0] % 3]
                if t0 == 0:
                    p.dma("pool", gb[:, :], gv[n, :, kc, :], writes=[gb])
                    act(p, gb[:, :], gb[:, :], AF.Sigmoid, [gb], [gb])
                if n == 0:
                    tt(p, merged[:, kc, t0:t1], ps[:, 0:t1 - t0], gb[:, t0:t1], ALU.mult, [ps, gb], [merged.sub(kc)])
                else:
                    tt(p, gb[:, t0:t1], ps[:, 0:t1 - t0], gb[:, t0:t1], ALU.mult, [ps, gb], [gb])
                    if n < 3:
                        tt(p, merged[:, kc, t0:t1], merged[:, kc, t0:t1], gb[:, t0:t1], ALU.add, [gb, merged.sub(kc)], [merged.sub(kc)], eng="pool")
                    else:
                        tt(p, merged_b[:, kc, t0:t1], merged[:, kc, t0:t1], gb[:, t0:t1], ALU.add, [gb, merged.sub(kc)], [merged_b.sub(kc)], eng="pool")
                if t1 == TT:
                    gi[0] += 1

            gemm_bf(p, Yb, Yb.sub, 4, tts, wbr[n], D, wst, wbf, evac, kc_off=4 * n)
    with p.scope():
        mx = p.sb("mx", [128, KC, TT]); rstd = p.sb("rstd", [128, TT])
        wst = [p.sb(f"wos{i}", [128, KC, 128]) for i in range(3)]
        wbf = [p.sb(f"wob{i}", [128, KC, 128], BF16) for i in range(3)]
        xbufs = [p.sb(f"xb{i}", [128, TT]) for i in range(2)]
        sqb = [p.sb(f"sqb{i}", [128, TT]) for i in range(2)]

        def evac2(m0, msz, t0, t1, ps):
            kc = m0 // 128
            cp(p, mx[:, kc, t0:t1], ps[:, 0:t1 - t0], [ps], [mx.sub(kc)], "act" if (t0 // 512) % 2 else "dve")

        gemm_bf(p, merged_b, merged_b.sub, KC, tts, wout, D, wst, wbf, evac2, NB=128)
        rms_stats2(p, c, mx, mx.sub, TT, rstd, c["eps6"][:, 0:1], sqb)
        xv = xT.rearrange("(kc p) t -> p kc t", p=128)
        ov = x1T.rearrange("(kc p) t -> p kc t", p=128)
        for kc in range(KC):
            xb = xbufs[kc % 2]
            p.dma("sp", xb[:, :], xv[:, kc, :], writes=[xb])
            tt(p, mx[:, kc, :], mx[:, kc, :], rstd[:, :], ALU.mult, [mx.sub(kc), rstd], [mx.sub(kc)])
            for (t0, t1, col) in SEGS:
                stt(p, xb[:, t0:t1], mx[:, kc, t0:t1], gg[:, kc, col:col + 1], xb[:, t0:t1], ALU.mult, ALU.add, [mx.sub(kc), gg, xb], [xb])
            p.dma("pool", ov[:, kc, :], xb[:, :], reads=[xb], is_output=True)
    p.finish("sp"); p.close()
    return p


TE_L = TL + 128
TE_C = TCX + 2
TE = TE_L + TE_C
ESEGS = [(0, TE_L, 0), (TE_L, TE, 1)]


def build_ffn():
    p = P()
    p.init_ps(8)
    c = make_consts(p)
    xe = p.dram("xe", [D, TE], "ExternalInput")
    mvd = p.dram("mv", [128, KC, 8], "ExternalInput")
    hmk = p.dram("hmask", [128, 4], "ExternalInput")
    wup = p.dram("wup", [D, 2 * DFF], "ExternalInput"); dwd = p.dram("dw", [128, FC, 9], "ExternalInput"); wdn = p.dram("wdn", [DFF, D], "ExternalInput")
    x2T = p.dram("x2T", [D, TT], "ExternalOutput")
    hm = p.nc.dram_tensor("hm_scratch", [DFF, TT], BF16, kind="Internal").ap()
    hmbufs = [Buf(f"hm{i}") for i in range(FC)]
    mv = p.sb("mvs", [128, KC, 8]); gs = p.sb("gs", [128, KC, 2]); sh = p.sb("sh", [128, KC, 2]); gg = p.sb("gg", [128, KC, 2])
    hmask = p.sb("hmask", [128, 4]); dw = p.sb("dw", [128, FC, 9])
    p.dma("sp", mv[:, :, :], mvd, writes=[mv]); p.dma("sp", hmask[:, :], hmk, writes=[hmask]); p.dma("sp", dw[:, :, :], dwd, writes=[dw])
    mod_prep(p, mv, gs, sh, 0, (1, 3), (2, 4))
    for j in range(2):
        tt(p, gg[:, :, j:j + 1], mv[:, :, 5:6], mv[:, :, 6 + j:7 + j], ALU.mult, [mv], [gg])
    GC = 1.5957691216057308
    with p.scope():
        hb = p.sb("hb", [128, KC, TE], BF16)
        with p.scope():
            x = p.sb("x", [128, KC, TE]); rstd = p.sb("rstd", [128, TE]); sqb = [p.sb(f"sqb{i}", [128, TE]) for i in range(2)]
            load_fm(p, "sp", x, xe, KC)
            rms_stats2(p, c, x, x.sub, TE, rstd, c["eps6"][:, 0:1], sqb)
            norm_mod(p, c, x, x.sub, x, TE, ESEGS, gs, sh, rstd, out=hb)
        wst = [p.sb(f"wus{i}", [128, KC, 128]) for i in range(4)]
        wbf = [p.sb(f"wub{i}", [128, KC, 128], BF16) for i in range(4)]
        upads = [p.sb(f"up{i}", [128, 18, 66]) for i in range(2)]
        ucs = [p.sb(f"uc{i}", [128, TE_C]) for i in range(2)]
        vbs = [p.sb(f"vb{i}", [128, TT]) for i in range(2)]
        accs = [p.sb(f"acc{i}", [128, TT]) for i in range(2)]
        tmps = [p.sb(f"tmp{i}", [128, TT]) for i in range(2)]
        hmbs = [p.sb(f"hmb{i}", [128, TT], BF16) for i in range(2)]
        for ub in upads:
            p.op("pool", lambda e: e.memset(ub[:, :, :], 0.0), writes=[ub])
        u_tts = [(0, 512), (512, 1024), (1024, 1152), (TE_L, TE)]
        v_tts = [(64, 576), (576, 1088), (TE_L + 1, TE_L + 1 + TCX)]
        for cc in range(FC):
            ub = upads[cc % 2]; uc = ucs[cc % 2]; vb = vbs[cc % 2]; acc = accs[cc % 2]; tmp = tmps[cc % 2]; hmb = hmbs[cc % 2]

            def evac_u(m0, msz, t0, t1, ps):
                if t0 < TE_L:
                    r0, r1 = t0 // 64, t1 // 64
                    cp(p, ub[:, r0:r1, 1:65], ps[:, 0:t1 - t0].rearrange("p (r w) -> p r w", w=64), [ps], [ub], "act")
                else:
                    cp(p, uc[:, :], ps[:, 0:TE_C], [ps], [uc], "act")

            def evac_v(m0, msz, t0, t1, ps):
                o0 = t0 - 64 if t0 < TE_L else TL
                cp(p, vb[:, o0:o0 + (t1 - t0)], ps[:, 0:t1 - t0], [ps], [vb], "dve")

            gemm_bf(p, hb, hb.sub, KC, u_tts, wup[:, cc * 128:(cc + 1) * 128], 128, wst, wbf, evac_u, NB=128)
            gemm_bf(p, hb, hb.sub, KC, v_tts, wup[:, DFF + cc * 128:DFF + (cc + 1) * 128], 128, wst, wbf, evac_v, NB=128)
            ts(p, ub[:, 0, :], ub[:, 0, :], hmask[:, 0:1], ALU.mult, [ub, hmask], [ub])
            ts(p, ub[:, 17, :], ub[:, 17, :], hmask[:, 1:2], ALU.mult, [ub, hmask], [ub])
            ts(p, uc[:, 0:1], uc[:, 0:1], hmask[:, 2:3], ALU.mult, [uc, hmask], [uc])
            ts(p, uc[:, TE_C - 1:TE_C], uc[:, TE_C - 1:TE_C], hmask[:, 3:4], ALU.mult, [uc, hmask], [uc])
            a3 = acc[:, 0:TL].rearrange("p (r w) -> p r w", w=64)
            first = True
            for i in range(3):
                for j in range(3):
                    src = ub[:, i:i + 16, j:j + 64]
                    if first:
                        ts(p, a3, src, dw[:, cc, 0:1], ALU.mult, [ub, dw], [acc])
                        first = False
                    else:
                        stt(p, a3, src, dw[:, cc, i * 3 + j:i * 3 + j + 1], a3, ALU.mult, ALU.add, [ub, dw, acc], [acc])
            ac = acc[:, TL:TT]
            ts(p, ac, uc[:, 0:TCX], dw[:, cc, 3:4], ALU.mult, [uc, dw], [acc])
            for j in range(1, 3):
                stt(p, ac, uc[:, j:j + TCX], dw[:, cc, 3 + j:4 + j], ac, ALU.mult, ALU.add, [uc, dw, acc], [acc])
            tt(p, tmp[:, :], acc[:, :], acc[:, :], ALU.mult, [acc], [tmp], eng="pool")
            ts(p, tmp[:, :], tmp[:, :], 0.044715, ALU.mult, [tmp], [tmp], s2=1.0, op1=ALU.add, eng="pool")
            tt(p, tmp[:, :], tmp[:, :], acc[:, :], ALU.mult, [tmp, acc], [tmp], eng="pool")
            act(p, tmp[:, :], tmp[:, :], AF.Sigmoid, [tmp], [tmp], scale=GC)
            tt(p, tmp[:, :], tmp[:, :], acc[:, :], ALU.mult, [tmp, acc], [tmp], eng="pool")
            tt(p, hmb[:, :], tmp[:, :], vb[:, :], ALU.mult, [tmp, vb], [hmb], eng="pool")
            p.dma("pool", hm[cc * 128:(cc + 1) * 128, :], hmb[:, :], reads=[hmb], writes=[hmbufs[cc]])
    with p.scope():
        hsb = p.sb("hsb", [128, FC, 512], BF16); fx = p.sb("fx", [128, KC, 512]); rstd = p.sb("rstd2", [128, 512])
        sqb = [p.sb(f"sqf{i}", [128, 512]) for i in range(2)]
        wds = [p.sb(f"wds{i}", [128, FC, 128]) for i in range(2)]
        wdb = [p.sb(f"wdb{i}", [128, FC, 128], BF16) for i in range(2)]
        xbufs = [p.sb(f"xb{i}", [128, 512]) for i in range(2)]
        hv = hm.rearrange("(kc p) t -> p kc t", p=128)
        wv = wdn.rearrange("(kc p) n -> p kc n", p=128)
        xv = xe.rearrange("(kc p) t -> p kc t", p=128)
        ov = x2T.rearrange("(kc p) t -> p kc t", p=128)
        wi = 0
        for (t0, t1) in ttiles(TT):
            tw = t1 - t0
            for q4 in range(4):
                p.dma("sp", hsb[:, q4 * 11:(q4 + 1) * 11, 0:tw], hv[:, q4 * 11:(q4 + 1) * 11, t0:t1], reads=hmbufs[q4 * 11:(q4 + 1) * 11], writes=[hsb.sub(q4)])
            for m in range(KC):
                ws = wds[wi % 2]; wd = wdb[wi % 2]; wi += 1
                for hf in range(2):
                    p.dma("sp", ws[:, hf * 22:(hf + 1) * 22, :], wv[:, hf * 22:(hf + 1) * 22, m * 128:(m + 1) * 128], writes=[ws.sub(hf)])
                    cp(p, wd[:, hf * 22:(hf + 1) * 22, :], ws[:, hf * 22:(hf + 1) * 22, :], [ws.sub(hf)], [wd.sub(hf)], cast_engine(wi + hf))
                ps = p.next_ps()
                for kc in range(FC):
                    mm(p, ps[:, 0:tw], wd[:, kc, :], hsb[:, kc, 0:tw], [wd.sub(kc // 22), hsb.sub(kc // 11)], [ps], start=(kc == 0), stop=(kc == FC - 1))
                cp(p, fx[:, m, 0:tw], ps[:, 0:tw], [ps], [fx.sub(m)], "act" if m % 2 else "dve")
            rms_stats2(p, c, fx, fx.sub, tw, rstd, c["eps6"][:, 0:1], sqb)
            e0 = t0 + 64 if t0 < TL else TE_L + 1
            col = 0 if t0 < TL else 1
            for kc in range(KC):
                xb = xbufs[kc % 2]
                p.dma("sp", xb[:, 0:tw], xv[:, kc, e0:e0 + tw], writes=[xb])
                tt(p, fx[:, kc, 0:tw], fx[:, kc, 0:tw], rstd[:, 0:tw], ALU.mult, [fx.sub(kc), rstd], [fx.sub(kc)])
                stt(p, xb[:, 0:tw], fx[:, kc, 0:tw], gg[:, kc, col:col + 1], xb[:, 0:tw], ALU.mult, ALU.add, [fx.sub(kc), gg, xb], [xb])
                p.dma("pool", ov[:, kc, t0:t1], xb[:, 0:tw], reads=[xb], is_output=True)
    p.finish("sp"); p.close()
    return p


from concourse.bass_utils import run_bass_kernel_spmd

NC_CTX, NC_LAT = 4, 64
NCH = NC_CTX + NC_LAT
TSEQ = NCH * 64
_PROGS = {}


def _prog(name):
    if name not in _PROGS:
        if name == "mod":
            _PROGS[name] = build_mod()
        elif name == "pre":
            _PROGS[name] = build_pre()
        elif name == "mix":
            _PROGS[name] = build_mix()
        elif name == "post":
            _PROGS[name] = build_post()
        elif name == "ffn":
            _PROGS[name] = build_ffn()
    return _PROGS[name]


def build_mix():
    p = P()
    p.init_ps(8)
    c = make_consts(p)
    io = {}
    T = TSEQ
    shapes = {"ml_qT": [128, T], "ml_kT": [128, T], "ml_k_tm": [64, NCH, 128], "ml_v_tm": [64, NCH, 128], "ml_o_tm": [64, NCH, 128],
              "ml_gi0": [64, NCH], "ml_gf0": [64, NCH], "ml_gi1": [64, NCH], "ml_gf1": [64, NCH], "ml_gb": [64, 4], "ml_ng": [64, 128],
              "gd_qp": [128, T + 8], "gd_kp": [128, T + 8], "gd_vp": [128, T + 8], "gd_cw": [128, 15],
              "gd_a0": [64, NCH], "gd_b0": [64, NCH], "gd_a1": [64, NCH], "gd_b1": [64, NCH], "gd_gp": [64, 4], "gd_ng": [64, 128],
              "gd_z_tm": [64, NCH, 128],
              "rw_xwp": [96, T + 4], "rw_xap": [96, T + 4], "rw_xgp": [128, 2, T + 4], "rw_muw": [96, 2], "rw_mua": [96, 2], "rw_mug": [128, 2, 2]}
    for h in range(2):
        shapes.update({f"rw_rp{h}": [64, T + 4], f"rw_kp{h}": [64, T + 4], f"rw_vp{h}": [64, T + 4], f"rw_mu{h}": [64, 6],
                       f"rw_w2_{h}": [96, 2, 64], f"rw_a2_{h}": [96, 2, 64], f"rw_g2_{h}": [128, 2, 64], f"rw_pr{h}": [64, 9]})
    for k_, s_ in shapes.items():
        io[k_] = p.dram(k_, s_, "ExternalInput")
    io["ml_yb"] = p.dram("ml_yb", [64, NCH, 128], "ExternalOutput")
    io["gd_yc"] = p.dram("gd_yc", [64, NCH, 128], "ExternalOutput")
    io["rw_yd0"] = p.dram("rw_yd0", [64, T], "ExternalOutput")
    io["rw_yd1"] = p.dram("rw_yd1", [64, T], "ExternalOutput")
    sub = lambda pre: {k_[len(pre):]: v_ for k_, v_ in io.items() if k_.startswith(pre)}
    emit_mlstm(p, c, sub("ml_"), NC_CTX, NC_LAT)
    emit_gdn(p, c, sub("gd_"), NC_CTX, NC_LAT, 8)
    emit_rwkv(p, c, sub("rw_"), NC_CTX, NC_LAT, 16)
    p.finish("sp"); p.close()
    p.in_names = list(shapes.keys())
    return p


def _run(name, in_maps):
    import time as _t
    t0 = _t.time()
    p = _prog(name)
    t1 = _t.time()
    res = run_bass_kernel_spmd(p.nc, in_maps, core_ids=list(range(8)))
    print(f"[kernel] launch {name}: build {t1 - t0:.1f}s run {_t.time() - t1:.1f}s", flush=True)
    return res.results


def _fm(v):
    return np.ascontiguousarray(np.asarray(v, np.float32).reshape(-1, 128).T)


def _tm(a):
    return np.ascontiguousarray(a.reshape(-1, 64, a.shape[-1]).transpose(1, 0, 2))


def _col(a):
    return np.ascontiguousarray(a.reshape(-1, 64).T)


def _rep(v, n):
    v = np.asarray(v, np.float32)
    return np.ascontiguousarray(np.broadcast_to(v, (n,) + v.shape))


def _padseg(zf, rows, pad):
    a = zf[rows]
    return np.ascontiguousarray(np.concatenate([np.pad(a[:, 0:256], ((0, 0), (pad, pad))), np.pad(a[:, 256:], ((0, 0), (pad, pad)))], axis=1))


def kernel(x, c, ctx, c_ctx, w_mod, b_mod, norm_g, w_in, conv_dw, conv_b, conv_ln_g, conv_ln_b,
           ml_gate_b, ml_norm_g, gd_conv, gd_a_log, gd_dt_bias, gd_norm_g,
           rw_mu, rw_w0, rw_w2, rw_a0, rw_a2, rw_g2, rw_kk, rw_ka, rw_rk, rw_ln_g, rw_ln_b,
           w_br, w_out, ffn_up, ffn_dw, ffn_down):
    f32 = lambda a: np.asarray(a, np.float32)
    x = f32(x).copy(); cx = f32(ctx).copy()
    (c, c_ctx, w_mod, b_mod, norm_g, w_in, conv_dw, conv_b, conv_ln_g, conv_ln_b, ml_gate_b, ml_norm_g, gd_conv, gd_a_log, gd_dt_bias,
     gd_norm_g, rw_mu, rw_w0, rw_w2, rw_a0, rw_a2, rw_g2, rw_kk, rw_ka, rw_rk, rw_ln_g, rw_ln_b, w_br, w_out, ffn_up, ffn_dw, ffn_down) = [
        f32(a) for a in (c, c_ctx, w_mod, b_mod, norm_g, w_in, conv_dw, conv_b, conv_ln_g, conv_ln_b, ml_gate_b, ml_norm_g, gd_conv, gd_a_log,
                         gd_dt_bias, gd_norm_g, rw_mu, rw_w0, rw_w2, rw_a0, rw_a2, rw_g2, rw_kk, rw_ka, rw_rk, rw_ln_g, rw_ln_b, w_br, w_out,
                         ffn_up, ffn_dw, ffn_down)]
    DEPTH = w_in.shape[0]
    cT = np.ascontiguousarray(np.stack([c[0], c[1], c_ctx], axis=1))
    ims = []
    for j in range(8):
        l, hh = j // 2, j % 2
        cols = slice(hh * 6144, (hh + 1) * 6144)
        ims.append({"cT": cT, "W": np.ascontiguousarray(w_mod[l][:, cols]), "b": _fm(b_mod[l][cols])})
    r = _run("mod", ims)
    mod = np.zeros((DEPTH, 12288, 3), np.float32)
    for j in range(8):
        l, hh = j // 2, j % 2
        o = r[j]["out"]
        mod[l, hh * 6144:(hh + 1) * 6144] = o.transpose(1, 0, 2).reshape(6144, 3)
    for l in range(DEPTH):
        sh1, sc1, gt1, sh2, sc2, gt2 = [mod[l, i * 2048:(i + 1) * 2048] for i in range(6)]
        xTs = []
        ims = []
        for j in range(8):
            b, g = j // 4, j % 4
            xT = np.ascontiguousarray(np.concatenate([x[b, g * 1024:(g + 1) * 1024], cx[b, g * 64:(g + 1) * 64]], axis=0).T)
            xTs.append(xT)
            mv = np.ascontiguousarray(np.stack([_fm(norm_g[l, 0]), _fm(sc1[:, b]), _fm(sh1[:, b]), _fm(sc1[:, 2]), _fm(sh1[:, 2])], axis=2))
            ims.append({"xT": xT, "W": w_in[l], "mv": mv})
        r = _run("pre", ims)
        zTs = [r[j]["zT"] for j in range(8)]
        zf = []
        for b in range(2):
            zf.append(np.concatenate([np.concatenate([zTs[4 * b + g][:, 1024:1088] for g in range(4)], axis=1),
                                      np.concatenate([zTs[4 * b + g][:, 0:1024] for g in range(4)], axis=1)], axis=1))
        ims = []
        for j in range(8):
            b, g = j // 4, j % 4
            z = zf[b]
            hs = slice(g * 128, (g + 1) * 128)
            im = {}
            q = z[1024:1536][hs]; k = z[1536:2048][hs]; v = z[2048:2560][hs]; o = z[2560:3072][hs]
            im["ml_qT"] = np.ascontiguousarray(q); im["ml_kT"] = np.ascontiguousarray(k)
            im["ml_k_tm"] = _tm(k.T); im["ml_v_tm"] = _tm(v.T); im["ml_o_tm"] = _tm(o.T)
            for d in range(2):
                im[f"ml_gi{d}"] = _col(z[3072 + d * 8 + g]); im[f"ml_gf{d}"] = _col(z[3072 + d * 8 + 4 + g])
            im["ml_gb"] = _rep(np.array([ml_gate_b[l, 0, 0, g], ml_gate_b[l, 0, 1, g], ml_gate_b[l, 1, 0, g], ml_gate_b[l, 1, 1, g]], np.float32), 64)
            im["ml_ng"] = _rep(ml_norm_g[l, hs], 64)
            for nm, c0 in (("q", 3088), ("k", 3600), ("v", 4112)):
                im[f"gd_{nm}p"] = _padseg(z, np.arange(c0 + g * 128, c0 + (g + 1) * 128), 2)
            im["gd_cw"] = np.ascontiguousarray(np.concatenate([gd_conv[l][:, c0 + g * 128:c0 + (g + 1) * 128].T for c0 in (0, 512, 1024)], axis=1))
            for d in range(2):
                im[f"gd_a{d}"] = _col(z[5136 + d * 8 + g]); im[f"gd_b{d}"] = _col(z[5136 + d * 8 + 4 + g])
            im["gd_gp"] = _rep(np.array([gd_a_log[l, 0, g], gd_dt_bias[l, 0, g], gd_a_log[l, 1, g], gd_dt_bias[l, 1, g]], np.float32), 64)
            im["gd_ng"] = _rep(gd_norm_g[l, hs], 64)
            im["gd_z_tm"] = _tm(z[4624:5136][hs].T)
            R0 = 5152
            mu = rw_mu[l]
            for li in range(2):
                h = 2 * g + li
                h64 = slice(h * 64, (h + 1) * 64)
                rows = [np.arange(R0 + o_ + h * 64, R0 + o_ + (h + 1) * 64) for o_ in (0, 512, 1024)]
                im[f"rw_rp{li}"] = _padseg(z, rows[0], 1); im[f"rw_kp{li}"] = _padseg(z, rows[1], 1); im[f"rw_vp{li}"] = _padseg(z, rows[2], 1)
                sl = [slice(o_ + h * 64, o_ + (h + 1) * 64) for o_ in (0, 512, 1024)]
                im[f"rw_mu{li}"] = np.ascontiguousarray(np.stack([mu[0, sl[0]], mu[1, sl[0]], mu[0, sl[1]], mu[1, sl[1]], mu[0, sl[2]], mu[1, sl[2]]], axis=1))
                im[f"rw_w2_{li}"] = np.ascontiguousarray(rw_w2[l][:, :, h64].transpose(1, 0, 2))
                im[f"rw_a2_{li}"] = np.ascontiguousarray(rw_a2[l][:, :, h64].transpose(1, 0, 2))
                im[f"rw_g2_{li}"] = np.ascontiguousarray(rw_g2[l][:, h64].reshape(2, 128, 64).transpose(1, 0, 2))
                im[f"rw_pr{li}"] = np.ascontiguousarray(np.stack([rw_w0[l, 0, h64], rw_w0[l, 1, h64], rw_a0[l, 0, h64], rw_a0[l, 1, h64], rw_kk[l, h64],
                                                                  rw_ka[l, h64], rw_rk[l, h], rw_ln_g[l, h64], rw_ln_b[l, h64]], axis=1))
            im["rw_xwp"] = _padseg(z, np.arange(R0 + 1536, R0 + 1632), 1); im["rw_xap"] = _padseg(z, np.arange(R0 + 1632, R0 + 1728), 1)
            xg = _padseg(z, np.arange(R0 + 1728, R0 + 1984), 1)
            im["rw_xgp"] = np.ascontiguousarray(xg.reshape(2, 128, -1).transpose(1, 0, 2))
            im["rw_muw"] = np.ascontiguousarray(mu[:, 1536:1632].T); im["rw_mua"] = np.ascontiguousarray(mu[:, 1632:1728].T)
            im["rw_mug"] = np.ascontiguousarray(mu[:, 1728:1984].reshape(2, 2, 128).transpose(2, 1, 0))
            ims.append(im)
        r = _run("mix", ims)
        yf = []
        for b in range(2):
            yb_ = np.concatenate([r[4 * b + g]["ml_yb"].transpose(1, 0, 2).reshape(TSEQ, 128).T for g in range(4)], axis=0)
            yc_ = np.concatenate([r[4 * b + g]["gd_yc"].transpose(1, 0, 2).reshape(TSEQ, 128).T for g in range(4)], axis=0)
            yd_ = np.concatenate([r[4 * b + g][f"rw_yd{li}"] for g in range(4) for li in range(2)], axis=0)
            yf.append(np.concatenate([yb_, yc_, yd_], axis=0))
        ims = []
        for j in range(8):
            b, g = j // 4, j % 4
            z = zf[b]

            def seg(rows):
                a = z[rows]
                lat = np.pad(a[:, 256:], ((0, 0), (15, 15)))[:, g * 1024:(g + 1) * 1024 + 30]
                cc = np.pad(a[:, 0:256], ((0, 0), (15, 15)))[:, g * 64:(g + 1) * 64 + 30]
                return np.ascontiguousarray(np.concatenate([lat, cc], axis=1))

            cvw = np.concatenate([conv_dw[l].T, conv_b[l][:, None], conv_ln_g[l][:, None], conv_ln_b[l][:, None]], axis=1)
            y = yf[b]
            ybcd = np.ascontiguousarray(np.concatenate([y[:, 256 + g * 1024:256 + (g + 1) * 1024], y[:, g * 64:(g + 1) * 64]], axis=1))
            ims.append({"a_p": seg(np.arange(0, 512)), "g_p": seg(np.arange(512, 1024)),
                        "cvw": np.ascontiguousarray(cvw.reshape(4, 128, 34).transpose(1, 0, 2)), "ybcd": ybcd,
                        "gates": np.ascontiguousarray(zTs[j][7136:15328]), "xT": xTs[j], "wbr": w_br[l], "wout": w_out[l],
                        "mv": np.ascontiguousarray(np.stack([_fm(norm_g[l, 1]), _fm(gt1[:, b]), _fm(gt1[:, 2])], axis=2))})
        r = _run("post", ims)
        for j in range(8):
            b, g = j // 4, j % 4
            o = r[j]["x1T"].T
            x[b, g * 1024:(g + 1) * 1024] = o[0:1024]; cx[b, g * 64:(g + 1) * 64] = o[1024:1088]
        ims = []
        for j in range(8):
            b, g = j // 4, j % 4
            xl = np.pad(x[b], ((64, 64), (0, 0)))[g * 1024:(g + 1) * 1024 + 128]
            xc = np.pad(cx[b], ((1, 1), (0, 0)))[g * 64:(g + 1) * 64 + 2]
            mv = np.ascontiguousarray(np.stack([_fm(norm_g[l, 2]), _fm(sc2[:, b]), _fm(sh2[:, b]), _fm(sc2[:, 2]), _fm(sh2[:, 2]),
                                                _fm(norm_g[l, 3]), _fm(gt2[:, b]), _fm(gt2[:, 2])], axis=2))
            hmask = _rep(np.array([g > 0, g < 3, g > 0, g < 3], np.float32), 128)
            ims.append({"xe": np.ascontiguousarray(np.concatenate([xl, xc], axis=0).T), "mv": mv, "hmask": hmask, "wup": ffn_up[l],
                        "dw": np.ascontiguousarray(ffn_dw[l].reshape(9, 44, 128).transpose(2, 1, 0)), "wdn": ffn_down[l]})
        r = _run("ffn", ims)
        for j in range(8):
            b, g = j // 4, j % 4
            o = r[j]["x2T"].T
            x[b, g * 1024:(g + 1) * 1024] = o[0:1024]; cx[b, g * 64:(g + 1) * 64] = o[1024:1088]
    return x
```
